# Optimizing a Trainium2 kernel written in Bass

```python
import math
import jax, jax.numpy as jnp
from jax import lax
import numpy as np

D_MODEL = 2048
BATCH = 32
SEQ = 256
DEPTH = 4
DEC_BATCH = 8
DEC_SEQ = 2048
PAST_LEN = 512

GRID_W = 64
WIN_R = 8
WIN_C = 16
Q_BLOCK = 128
EPS = 1e-6
ROPE_BASE = 10000.0
N_EVEN = (DEPTH + 1) // 2
N_ODD = DEPTH // 2
NA_DIM = 128
NA_WIDTH = D_MODEL // 2
NA_HEADS = NA_WIDTH // NA_DIM
MLA_NOPE = 128
MLA_ROPE = 64
MLA_V = 128
MLA_WIDTH = D_MODEL // 2
MLA_HEADS = MLA_WIDTH // MLA_V
MLA_Q_LORA = D_MODEL // 4
MLA_KV_LORA = D_MODEL // 8
EVEN_SPLITS = (NA_WIDTH, NA_WIDTH, NA_WIDTH, NA_WIDTH, MLA_Q_LORA, MLA_KV_LORA, MLA_ROPE, MLA_WIDTH)
EVEN_IN = 4 * NA_WIDTH + MLA_Q_LORA + MLA_KV_LORA + MLA_ROPE + MLA_WIDTH
DIFF_HEADS = 8
DIFF_D = D_MODEL // (2 * DIFF_HEADS)
DIFF_WIDTH = 2 * DIFF_HEADS * DIFF_D
ODD_IN = 4 * DIFF_WIDTH
NA_SCALE = NA_DIM ** -0.5
MLA_SCALE = (MLA_NOPE + MLA_ROPE) ** -0.5
DIFF_SCALE = DIFF_D ** -0.5

kernel_name = 'hybrid_diffusion_na_mla_diffattn_step'


def _split(x, sizes):
    out, start = [], 0
    for s in sizes:
        out.append(x[..., start:start + s])
        start += s
    return out


def rmsnorm(x, g):
    x32 = x.astype(jnp.float32)
    y = x32 * lax.rsqrt(jnp.mean(x32 * x32, axis=-1, keepdims=True) + EPS)
    return (y * g.astype(jnp.float32)).astype(x.dtype)


def modulation(cond, w_ada, b_ada):
    ada = (jax.nn.silu(cond) @ w_ada + b_ada)[:, None, :]
    return jnp.split(ada, 3, axis=-1)


def rope_1d(x, pos):
    half = x.shape[-1] // 2
    inv = ROPE_BASE ** (-jnp.arange(half, dtype=jnp.float32) / half)
    ang = pos.astype(jnp.float32)[:, None] * inv[None, :]
    cos = jnp.cos(ang)[:, None, :]
    sin = jnp.sin(ang)[:, None, :]
    x1 = x[..., :half].astype(jnp.float32)
    x2 = x[..., half:].astype(jnp.float32)
    return jnp.concatenate([x1 * cos - x2 * sin, x2 * cos + x1 * sin], axis=-1).astype(x.dtype)


def axial_rope(x, rows, cols):
    a = x.shape[-1] // 2
    return jnp.concatenate([rope_1d(x[..., :a], rows), rope_1d(x[..., a:], cols)], axis=-1)


def dense_attention(q, k, v, scale):
    B, Sq, H, d = q.shape
    nb = Sq // Q_BLOCK
    qb = q.reshape(B, nb, Q_BLOCK, H, d).transpose(1, 0, 2, 3, 4)

    def block(q_blk):
        s = jnp.einsum('bqhd,bkhd->bhqk', q_blk, k).astype(jnp.float32) * scale
        p = jax.nn.softmax(s, axis=-1).astype(v.dtype)
        return jnp.einsum('bhqk,bkhd->bqhd', p, v)

    o = lax.map(block, qb)
    return o.transpose(1, 0, 2, 3, 4).reshape(B, Sq, H, v.shape[-1])


def na_tables(rows):
    wr = min(WIN_R, rows)
    r = np.arange(rows)
    c = np.arange(GRID_W)
    rs = np.clip(r - wr // 2, 0, rows - wr)
    cs = np.clip(c - WIN_C // 2, 0, GRID_W - WIN_C)
    kr = rs[:, None] + np.arange(wr)[None, :]
    kc = cs[:, None] + np.arange(WIN_C)[None, :]
    key_idx = kr[:, None, :, None] * GRID_W + kc[None, :, None, :]
    dr = kr - r[:, None]
    dc = kc - c[:, None]
    bias_idx = (dr[:, None, :, None] + WIN_R - 1) * (2 * WIN_C - 1) + (dc[None, :, None, :] + WIN_C - 1)
    K = wr * WIN_C
    return (jnp.asarray(key_idx.reshape(rows, GRID_W, K), dtype=jnp.int32),
            jnp.asarray(bias_idx.reshape(rows, GRID_W, K), dtype=jnp.int32))


def na_latent(q, k, v, k_ctx, v_ctx, rpb):
    B, S, H, d = q.shape
    rows = S // GRID_W
    key_idx, bias_idx = na_tables(rows)
    rpb_flat = rpb.reshape(H, -1)
    qr = q.reshape(B, rows, GRID_W, H, d).transpose(1, 0, 2, 3, 4)

    def row_block(args):
        q_blk, kidx, bidx = args
        k_win = k[:, kidx]
        v_win = v[:, kidx]
        bias = rpb_flat[:, bidx].astype(jnp.float32)
        s_win = jnp.einsum('bqhd,bqkhd->bhqk', q_blk, k_win).astype(jnp.float32) * NA_SCALE + bias[None]
        s_ctx = jnp.einsum('bqhd,bkhd->bhqk', q_blk, k_ctx).astype(jnp.float32) * NA_SCALE
        p = jax.nn.softmax(jnp.concatenate([s_win, s_ctx], axis=-1), axis=-1).astype(v.dtype)
        K = kidx.shape[-1]
        return (jnp.einsum('bhqk,bqkhd->bqhd', p[..., :K], v_win)
                + jnp.einsum('bhqk,bkhd->bqhd', p[..., K:], v_ctx))

    o = lax.map(row_block, (qr, key_idx, bias_idx))
    return o.transpose(1, 0, 2, 3, 4).reshape(B, S, H, d)


def even_project(h, w_in, g_q, w_uq, g_kv):
    B, S, _ = h.shape
    qa, ka, va, ga, cq, ckv_raw, kpe, gb = _split(h @ w_in, EVEN_SPLITS)
    heads = lambda t: t.reshape(B, S, NA_HEADS, NA_DIM)
    q_mla = (rmsnorm(cq, g_q) @ w_uq).reshape(B, S, MLA_HEADS, MLA_NOPE + MLA_ROPE)
    ckv = rmsnorm(ckv_raw, g_kv)
    return heads(qa), heads(ka), heads(va), ga, q_mla, ckv, kpe, gb


def mla_kv(ckv, kpe, w_ukv):
    B, S, _ = ckv.shape
    kv = (ckv @ w_ukv).reshape(B, S, MLA_HEADS, MLA_NOPE + MLA_V)
    k = jnp.concatenate([kv[..., :MLA_NOPE],
                         jnp.broadcast_to(kpe[:, :, None, :], (B, S, MLA_HEADS, MLA_ROPE))], axis=-1)
    return k, kv[..., MLA_NOPE:]


def even_merge(oa, ga, ob, gb, w_out):
    B, S = ga.shape[:2]
    y = jnp.concatenate([oa.reshape(B, S, -1) * jax.nn.silu(ga),
                         ob.reshape(B, S, -1) * jax.nn.silu(gb)], axis=-1)
    return y @ w_out


def even_context(h, w_in, w_out, g_q, w_uq, g_kv, w_ukv):
    qa, ka, va, ga, q_mla, ckv, kpe, gb = even_project(h, w_in, g_q, w_uq, g_kv)
    oa = dense_attention(qa, ka, va, NA_SCALE)
    kb, vb = mla_kv(ckv, kpe, w_ukv)
    ob = dense_attention(q_mla, kb, vb, MLA_SCALE)
    return even_merge(oa, ga, ob, gb, w_out), (ka, va, ckv, kpe)


def even_latent(h, ctx_k, ctx_v, ctx_ckv, ctx_kpe, rows, cols, w_in, w_out, rpb, g_q, w_uq, g_kv, w_ukv):
    qa, ka, va, ga, q_mla, ckv, kpe, gb = even_project(h, w_in, g_q, w_uq, g_kv)
    oa = na_latent(qa, ka, va, ctx_k, ctx_v, rpb)
    q_mla = jnp.concatenate([q_mla[..., :MLA_NOPE], axial_rope(q_mla[..., MLA_NOPE:], rows, cols)], axis=-1)
    kpe = axial_rope(kpe[:, :, None, :], rows, cols)[:, :, 0, :]
    kb_lat, vb_lat = mla_kv(ckv, kpe, w_ukv)
    kb_ctx, vb_ctx = mla_kv(ctx_ckv, ctx_kpe, w_ukv)
    ob = dense_attention(q_mla, jnp.concatenate([kb_ctx, kb_lat], axis=1),
                         jnp.concatenate([vb_ctx, vb_lat], axis=1), MLA_SCALE)
    return even_merge(oa, ga, ob, gb, w_out)


def odd_project(h, w_in):
    B, S, _ = h.shape
    q, k, v, g = jnp.split(h @ w_in, 4, axis=-1)
    q = q.reshape(B, S, DIFF_HEADS, 2, DIFF_D)
    k = k.reshape(B, S, DIFF_HEADS, 2, DIFF_D)
    v = v.reshape(B, S, DIFF_HEADS, 2 * DIFF_D)
    return q, k, v, g


def diff_rope(t, rows, cols):
    B, S = t.shape[:2]
    return axial_rope(t.reshape(B, S, 2 * DIFF_HEADS, DIFF_D), rows, cols).reshape(B, S, DIFF_HEADS, 2, DIFF_D)


def diff_attend(q, k, v, g, lam_params, g_sub, lam_init, w_out):
    B, S = q.shape[:2]
    o1 = dense_attention(q[..., 0, :], k[..., 0, :], v, DIFF_SCALE)
    o2 = dense_attention(q[..., 1, :], k[..., 1, :], v, DIFF_SCALE)
    lp = lam_params.astype(jnp.float32)
    lam = jnp.exp(jnp.sum(lp[0] * lp[1])) - jnp.exp(jnp.sum(lp[2] * lp[3])) + lam_init
    o = rmsnorm(o1 - lam.astype(o1.dtype) * o2, g_sub) * (1.0 - lam_init)
    return (o.reshape(B, S, -1) * jax.nn.silu(g)) @ w_out


def odd_context(h, w_in, w_out, lam_params, g_sub, lam_init):
    B, S, _ = h.shape
    q, k, v, g = odd_project(h, w_in)
    y = diff_attend(q, k, v, g, lam_params, g_sub, lam_init, w_out)
    return y, (k.reshape(B, S, DIFF_HEADS, 2 * DIFF_D), v)


def odd_latent(h, ctx_k, ctx_v, rows, cols, w_in, w_out, lam_params, g_sub, lam_init):
    B, L = ctx_k.shape[:2]
    q, k, v, g = odd_project(h, w_in)
    q = diff_rope(q, rows, cols)
    k = diff_rope(k, rows, cols)
    k_all = jnp.concatenate([ctx_k.reshape(B, L, DIFF_HEADS, 2, DIFF_D), k], axis=1)
    v_all = jnp.concatenate([ctx_v, v], axis=1)
    return diff_attend(q, k_all, v_all, g, lam_params, g_sub, lam_init, w_out)


def setup_inputs(seed: int = 0) -> dict:
    key = jax.random.key(seed)
    ks = jax.random.split(key, 32)
    nrm = lambda k, shape, s: jax.random.normal(k, shape, jnp.float32) * s
    D = D_MODEL
    return {
        'x_prompt': nrm(ks[0], (BATCH, SEQ, D), 1.0),
        'x_sample': nrm(ks[1], (DEC_BATCH, DEC_SEQ, D), 1.0),
        'cache_na_k': nrm(ks[2], (DEC_BATCH, N_EVEN, PAST_LEN, NA_HEADS, NA_DIM), 1.0),
        'cache_na_v': nrm(ks[3], (DEC_BATCH, N_EVEN, PAST_LEN, NA_HEADS, NA_DIM), 1.0),
        'cache_mla_ckv': nrm(ks[4], (DEC_BATCH, N_EVEN, PAST_LEN, MLA_KV_LORA), 1.0),
        'cache_mla_kpe': nrm(ks[5], (DEC_BATCH, N_EVEN, PAST_LEN, MLA_ROPE), 1.0),
        'cache_diff_k': nrm(ks[6], (DEC_BATCH, N_ODD, PAST_LEN, DIFF_HEADS, 2 * DIFF_D), 1.0),
        'cache_diff_v': nrm(ks[7], (DEC_BATCH, N_ODD, PAST_LEN, DIFF_HEADS, 2 * DIFF_D), 1.0),
        'c': nrm(ks[8], (DEC_BATCH, D), 1.0),
        'c_ctx': nrm(ks[9], (D,), 1.0),
        'w_ada': nrm(ks[10], (DEPTH, D, 3 * D), 0.5 * D ** -0.5),
        'b_ada': nrm(ks[11], (DEPTH, 3 * D), 0.01),
        'g_pre': 1.0 + nrm(ks[12], (DEPTH, D), 0.01),
        'g_post': 1.0 + nrm(ks[13], (DEPTH, D), 0.01),
        'w_in_even': nrm(ks[14], (N_EVEN, D, EVEN_IN), D ** -0.5),
        'w_out_even': nrm(ks[15], (N_EVEN, NA_WIDTH + MLA_WIDTH, D), (NA_WIDTH + MLA_WIDTH) ** -0.5),
        'na_rpb': nrm(ks[16], (N_EVEN, NA_HEADS, 2 * WIN_R - 1, 2 * WIN_C - 1), 0.1),
        'mla_g_q': 1.0 + nrm(ks[17], (N_EVEN, MLA_Q_LORA), 0.01),
        'mla_w_uq': nrm(ks[18], (N_EVEN, MLA_Q_LORA, MLA_HEADS * (MLA_NOPE + MLA_ROPE)), MLA_Q_LORA ** -0.5),
        'mla_g_kv': 1.0 + nrm(ks[19], (N_EVEN, MLA_KV_LORA), 0.01),
        'mla_w_ukv': nrm(ks[20], (N_EVEN, MLA_KV_LORA, MLA_HEADS * (MLA_NOPE + MLA_V)), MLA_KV_LORA ** -0.5),
        'w_in_odd': nrm(ks[21], (N_ODD, D, ODD_IN), D ** -0.5),
        'w_out_odd': nrm(ks[22], (N_ODD, DIFF_WIDTH, D), DIFF_WIDTH ** -0.5),
        'diff_lambda': nrm(ks[23], (N_ODD, 4, DIFF_D), 0.1),
        'diff_g': 1.0 + nrm(ks[24], (N_ODD, 2 * DIFF_D), 0.01),
    }


def reference(x_prompt, x_sample, cache_na_k, cache_na_v, cache_mla_ckv, cache_mla_kpe, cache_diff_k,
              cache_diff_v, c, c_ctx, w_ada, b_ada, g_pre, g_post, w_in_even, w_out_even, na_rpb,
              mla_g_q, mla_w_uq, mla_g_kv, mla_w_ukv, w_in_odd, w_out_odd, diff_lambda, diff_g):
    S = x_sample.shape[1]
    t = jnp.arange(S)
    rows = t // GRID_W
    cols = t % GRID_W
    xp, xs = x_prompt, x_sample
    na_k, na_v, mla_ckv, mla_kpe, diff_k, diff_v = [], [], [], [], [], []
    for l in range(DEPTH):
        i = l // 2
        sh_p, sc_p, gt_p = modulation(c_ctx[None, :], w_ada[l], b_ada[l])
        sh_s, sc_s, gt_s = modulation(c, w_ada[l], b_ada[l])
        hp = rmsnorm(xp, g_pre[l]) * (1.0 + sc_p) + sh_p
        hs = rmsnorm(xs, g_pre[l]) * (1.0 + sc_s) + sh_s
        if l % 2 == 0:
            yp, (ka, va, ckv, kpe) = even_context(hp, w_in_even[i], w_out_even[i], mla_g_q[i], mla_w_uq[i],
                                                  mla_g_kv[i], mla_w_ukv[i])
            na_k.append(ka)
            na_v.append(va)
            mla_ckv.append(ckv)
            mla_kpe.append(kpe)
            ys = even_latent(hs, cache_na_k[:, i], cache_na_v[:, i], cache_mla_ckv[:, i], cache_mla_kpe[:, i],
                             rows, cols, w_in_even[i], w_out_even[i], na_rpb[i], mla_g_q[i], mla_w_uq[i],
                             mla_g_kv[i], mla_w_ukv[i])
        else:
            lam_init = 0.8 - 0.6 * math.exp(-0.3 * l)
            yp, (kd, vd) = odd_context(hp, w_in_odd[i], w_out_odd[i], diff_lambda[i], diff_g[i], lam_init)
            diff_k.append(kd)
            diff_v.append(vd)
            ys = odd_latent(hs, cache_diff_k[:, i], cache_diff_v[:, i], rows, cols, w_in_odd[i], w_out_odd[i],
                            diff_lambda[i], diff_g[i], lam_init)
        xp = xp + gt_p * rmsnorm(yp, g_post[l])
        xs = xs + gt_s * rmsnorm(ys, g_post[l])
    return (xp, xs, jnp.stack(na_k, axis=1), jnp.stack(na_v, axis=1), jnp.stack(mla_ckv, axis=1),
            jnp.stack(mla_kpe, axis=1), jnp.stack(diff_k, axis=1), jnp.stack(diff_v, axis=1))
```

```python
import math
from contextlib import ExitStack

import numpy as np
import concourse.bass as bass
import concourse.mybir as mybir
from concourse.bass_utils import run_bass_kernel_spmd

F32 = mybir.dt.float32
BF16 = mybir.dt.bfloat16
AF = mybir.ActivationFunctionType
ALU = mybir.AluOpType

D = 2048
KC = 16
DEPTH = 4
EPS = 1e-6
NA_SCALE = 128 ** -0.5
MLA_SCALE = 192 ** -0.5
DIFF_SCALE = 128 ** -0.5
MASKV = -30000.0
NCORES = 8


class Buf:
    __slots__ = ("name", "w", "r", "init", "excl")

    def __init__(self, name, init):
        self.name = name
        self.excl = False
        self.w = None
        self.r = {}
        self.init = dict(init)


class Tile:
    __slots__ = ("t", "b")

    def __init__(self, t, b):
        self.t = t
        self.b = b


class Scope:
    def __init__(self, pr):
        self.pr = pr
        self.stack = ExitStack()
        self.bufs = []
        self.sems = []

    def __enter__(self):
        self.stack.__enter__()
        return self

    def tile(self, shape, dt, name=None):
        pr = self.pr
        pr.uid += 1
        nm = f"{name or 't'}_{pr.uid}"
        t = self.stack.enter_context(pr.nc.sbuf_tensor(nm, list(shape), dt))
        b = Buf(nm, pr.frontier)
        self.bufs.append(b)
        return Tile(t, b)

    def buf(self, name="b"):
        b = Buf(name, self.pr.frontier)
        self.bufs.append(b)
        return b

    def small(self):
        if not hasattr(self, "_sm"):
            self._sm = [self.tile([128, 2], F32, "sm") for _ in range(32)]
            self._smi = 0
        self._smi += 1
        return self._sm[self._smi % 32]

    def __exit__(self, *a):
        pr = self.pr
        for b in self.bufs:
            if b.w is not None:
                pr.front_merge(b.w[0], b.w[1])
            for k, v in b.r.items():
                pr.front_merge(k, v)
            pr.release_dma(b)
        return self.stack.__exit__(*a)


class Prog:
    def __init__(self, nc, stack, n_dma_sems=56):
        self.nc = nc
        self.uid = 0
        self.eng = {"pe": nc.tensor, "act": nc.scalar, "dve": nc.vector, "pool": nc.gpsimd, "sp": nc.sync}
        self.sems = {}
        for e in ("pe", "act", "dve", "pool"):
            self.sems[e] = stack.enter_context(nc.semaphore(f"sem_{e}"))
        self.cnt = {e: 0 for e in ("pe", "act", "dve", "pool")}
        self.known = {e: {} for e in self.eng}
        self.frontier = {}
        self.dma_cnt = {}
        self.free_dma = {"sp": [], "pool": []}
        for i in range(n_dma_sems):
            k = f"dma{i}"
            self.sems[k] = stack.enter_context(nc.semaphore(f"sem_{k}"))
            self.dma_cnt[k] = 0
            self.free_dma["pool" if i < 16 else "sp"].append(k)
        self.buf_dma = {}
        self.n_ins = 0

    def front_merge(self, k, v):
        if self.frontier.get(k, 0) < v:
            self.frontier[k] = v

    def scope(self):
        return Scope(self)

    def gbuf(self, name):
        return Buf(name, {})

    def dma_key(self, b, q):
        k = self.buf_dma.get((id(b), q))
        if k is None:
            k = self.free_dma[q].pop()
            self.buf_dma[(id(b), q)] = k
        return k

    def release_dma(self, b):
        for q in ("sp", "pool"):
            k = self.buf_dma.pop((id(b), q), None)
            if k is not None:
                self.free_dma[q].insert(0, k)

    def _waits(self, eng, own, reads, writes):
        w = {}

        def add(k, v, raw):
            if k == own and eng == "pe":
                return
            if w.get(k, 0) < v:
                w[k] = v

        for b in reads:
            for k, v in b.init.items():
                add(k, v, True)
            if b.w is not None:
                add(b.w[0], b.w[1], True)
            if b.excl:
                for k, v in b.r.items():
                    if k != own:
                        add(k, v, False)
        for b in writes:
            for k, v in b.init.items():
                add(k, v, True)
            if b.w is not None:
                add(b.w[0], b.w[1], False)
            for k, v in b.r.items():
                add(k, v, False)
        return w

    def _emit_waits(self, eng, w):
        e = self.eng[eng]
        kn = self.known[eng]
        for k, v in w.items():
            if kn.get(k, 0) < v:
                e.wait_ge(self.sems[k], v)
                kn[k] = v
                self.n_ins += 1

    def _record(self, ev, reads, writes):
        k, v = ev
        for b in reads:
            if b.r.get(k, 0) < v:
                b.r[k] = v
        for b in writes:
            b.w = ev
            b.r = {}

    def op(self, eng, fn, reads=(), writes=()):
        w = self._waits(eng, eng, reads, writes)
        self._emit_waits(eng, w)
        ins = fn(self.eng[eng])
        self.cnt[eng] += 1
        ins.then_inc(self.sems[eng], 1)
        self.n_ins += 1
        self._record((eng, self.cnt[eng]), reads, writes)

    def dma(self, q, pairs, slot, reads=(), writes=()):
        k = self.dma_key(slot, q)
        w = self._waits(q, None, reads, writes)
        if self.dma_cnt[k] > 0:
            w[k] = max(w.get(k, 0), self.dma_cnt[k])
        self._emit_waits(q, w)
        e = self.eng[q]
        for (o, i) in pairs:
            e.dma_start(out=o, in_=i).then_inc(self.sems[k], 16)
            self.dma_cnt[k] += 16
            self.n_ins += 1
        self._record((k, self.dma_cnt[k]), reads, writes)

    def wait_all_dma(self, eng):
        w = {k: v for k, v in self.dma_cnt.items() if v > 0}
        self._emit_waits(eng, w)

    def mm(self, out, pairs, reads, writes):
        n = len(pairs)

        def fn(e):
            ins = None
            for j, (l, r) in enumerate(pairs):
                ins = e.matmul(out, lhsT=l, rhs=r, start=(j == 0), stop=(j == n - 1))
            return ins
        self.n_ins += n - 1
        self.op("pe", fn, reads, writes)

    def transposes(self, items, ident, reads, writes):
        def fn(e):
            ins = None
            for (o, i) in items:
                ins = e.transpose(o, i, ident)
            return ins
        self.n_ins += len(items) - 1
        self.op("pe", fn, reads, writes)

    def act(self, out, in_, func, reads, writes, **kw):
        self.op("act", lambda e: e.activation(out=out, in_=in_, func=func, **kw), reads, writes)

    def copy(self, eng, out, in_, reads, writes):
        if eng == "act":
            self.op("act", lambda e: e.activation(out=out, in_=in_, func=AF.Copy), reads, writes)
        else:
            self.op(eng, lambda e: e.tensor_copy(out=out, in_=in_), reads, writes)

    def tt(self, eng, out, in0, in1, op, reads, writes):
        self.op(eng, lambda e: e.tensor_tensor(out=out, in0=in0, in1=in1, op=op), reads, writes)

    def ts(self, eng, out, in0, s1, s2, op0, op1, reads, writes):
        if s2 is None:
            self.op(eng, lambda e: e.tensor_single_scalar(out=out, in_=in0, scalar=s1, op=op0), reads, writes)
        else:
            self.op(eng, lambda e: e.tensor_scalar(out=out, in0=in0, scalar1=s1, scalar2=s2, op0=op0, op1=op1),
                    reads, writes)

    def stt(self, eng, out, in0, scalar, in1, op0, op1, reads, writes, accum_out=None):
        if accum_out is None:
            self.op(eng, lambda e: e.scalar_tensor_tensor(out=out, in0=in0, scalar=scalar, in1=in1, op0=op0, op1=op1),
                    reads, writes)
        else:
            self.op(eng, lambda e: e.scalar_tensor_tensor(out=out, in0=in0, scalar=scalar, in1=in1, op0=op0, op1=op1,
                                                          accum_out=accum_out), reads, writes)

    def memset(self, eng, ap, val, writes):
        self.op(eng, lambda e: e.memset(ap, val), (), writes)


def rope_tables(hw):
    inv = (10000.0 ** (-np.arange(hw, dtype=np.float32) / np.float32(hw))).astype(np.float32)
    t = np.arange(2048)
    rows = (t // 64).astype(np.float32)
    cols = (t % 64).astype(np.float32)
    out = np.zeros((128, 16, 2, 2, 2, hw), np.float32)
    for ax, pos in enumerate((rows, cols)):
        ang = (pos[:, None] * inv[None, :]).astype(np.float32)
        c = np.cos(ang).astype(np.float32).reshape(16, 128, hw).transpose(1, 0, 2)
        s = np.sin(ang).astype(np.float32).reshape(16, 128, hw).transpose(1, 0, 2)
        out[:, :, 0, ax, 0, :] = c
        out[:, :, 0, ax, 1, :] = c
        out[:, :, 1, ax, 0, :] = -s
        out[:, :, 1, ax, 1, :] = s
    return out


def na_bias_tables(rpb):
    L, H = rpb.shape[0], rpb.shape[1]
    kc = np.arange(64)[:, None]
    c = np.arange(64)[None, :]
    cs = np.clip(c - 8, 0, 48)
    colok = (kc >= cs) & (kc < cs + 16)
    dcidx = np.clip(kc - c + 15, 0, 30)
    out = np.full((L, H, 2, 128, 22, 64), MASKV, np.float32)
    for a in range(2):
        for t in range(22):
            dr = 10 + a - t
            if abs(dr) > 7:
                continue
            vals = rpb[:, :, dr + 7, :][:, :, dcidx]
            blk = np.where(colok[None, None], vals, np.float32(MASKV)).astype(np.float32)
            out[:, :, 1, a * 64:(a + 1) * 64, t, :] = blk
            if -4 <= dr <= 3:
                out[:, :, 0, a * 64:(a + 1) * 64, t, :] = blk
    return out.reshape(L, H, 2, 128, 22 * 64)


def na_window_tiles(b):
    rng = {0: range(0, 6), 1: range(2, 10), 2: range(6, 14), 3: range(10, 16)}[b]
    res = []
    for j in rng:
        full = (b == 0 and j in (2, 3)) or (b == 3 and j in (12, 13))
        res.append((j, 1 if full else 0, 10 - (2 * j - 8 * b)))
    return res


class Group:
    def __init__(self, name):
        self.name = name
        if name == "S":
            self.nt = 16
            self.tile0 = 0
            self.klen = 2560
            self.rope = True
            self.ctx = True
            self.qblocks = [(qb * 512, 512, list(range(16)) + [16, 17, 18, 19]) for qb in range(4)]
            self.mod_row = 0
        else:
            self.nt = 8
            self.tile0 = 16
            self.klen = 1024
            self.rope = False
            self.ctx = False
            self.qblocks = [(j * 256, 256, [2 * j, 2 * j + 1]) for j in range(4)]
            self.mod_row = 1
        self.ntok = self.nt * 128
        self.nkt = self.klen // 128


class Builder:
    def __init__(self, cfg):
        self.cfg = cfg
        self.nc = bass.Bass("TRN2", target_bir_lowering=False)
        self.stack = ExitStack()

    def declare(self):
        nc = self.nc
        I = lambda n, s: nc.dram_tensor(n, list(s), F32, kind="ExternalInput").ap()
        O = lambda n, s: nc.dram_tensor(n, list(s), F32, kind="ExternalOutput").ap()
        d = {}
        d["xs"] = I("xs", (2048, D)); d["xp"] = I("xp", (1024, D))
        d["cnak"] = I("cnak", (2, 512, 1024)); d["cnav"] = I("cnav", (2, 512, 1024))
        d["cckv"] = I("cckv", (2, 512, 256)); d["ckpe"] = I("ckpe", (2, 512, 64))
        d["cdk"] = I("cdk", (2, 512, 2048)); d["cdv"] = I("cdv", (2, 512, 2048))
        d["condT"] = I("condT", (128, 32))
        d["w_ada"] = I("w_ada", (4, D, 3 * D)); d["b_ada"] = I("b_ada", (4, 3 * D))
        d["g_pre"] = I("g_pre", (4, D)); d["g_post"] = I("g_post", (4, D))
        d["w_in_even"] = I("w_in_even", (2, D, 5952)); d["w_out_even"] = I("w_out_even", (2, D, D))
        d["mla_g_q"] = I("mla_g_q", (2, 512)); d["mla_w_uq"] = I("mla_w_uq", (2, 512, 1536))
        d["mla_g_kv"] = I("mla_g_kv", (2, 256)); d["mla_w_ukv"] = I("mla_w_ukv", (2, 256, 2048))
        d["w_in_odd"] = I("w_in_odd", (2, D, 8192)); d["w_out_odd"] = I("w_out_odd", (2, D, D))
        d["diff_lambda"] = I("diff_lambda", (2, 512)); d["diff_g"] = I("diff_g", (2, 256))
        d["ident"] = I("ident", (128, 128))
        d["ropeD"] = I("ropeD", (128, 16 * 2 * 2 * 2 * 32)); d["ropeM"] = I("ropeM", (128, 16 * 2 * 2 * 2 * 16))
        d["natab"] = I("natab", (2, 8, 2, 128, 1408))
        d["y_p"] = O("y_p", (1024, D)); d["y_s"] = O("y_s", (2048, D))
        d["o_nak"] = O("o_nak", (4, 2, 256, 1024)); d["o_nav"] = O("o_nav", (4, 2, 256, 1024))
        d["o_ckv"] = O("o_ckv", (4, 2, 256, 256)); d["o_kpe"] = O("o_kpe", (4, 2, 256, 64))
        d["o_dk"] = O("o_dk", (4, 2, 256, 2048)); d["o_dv"] = O("o_dv", (4, 2, 256, 2048))
        d["xst"] = nc.dram_tensor("xst", [3072, D], F32, kind="Internal").ap()
        d["ysc"] = nc.dram_tensor("ysc", [3072, D], BF16, kind="Internal").ap()
        d["modv"] = nc.dram_tensor("modv", [4, 2, 3, D], F32, kind="Internal").ap()
        d["sgsc"] = nc.dram_tensor("sgsc", [3072, 1024], BF16, kind="Internal").ap()
        self.d = d

    def bcast_rows(self, ap_row, nparts=128):
        n = ap_row.shape[-1]
        return bass.AP(tensor=ap_row.tensor, offset=ap_row.offset, ap=[[0, nparts], [1, n]])

    def wview(self, w2d, c0, ncols, kchunks):
        return w2d[:, c0:c0 + ncols].rearrange("(kc p) n -> p kc n", p=128)

    def ps(self, i, n=512, dt=F32):
        t, b = self.PS[i]
        if dt == F32:
            return t[:, 0:n]
        return t[:].bitcast(BF16)[:, 0:n]

    def evac_eng(self):
        self.evi += 1
        return "act" if (self.evi % 2) else "dve"

    def rot(self, name, n):
        v = self.rotc.get(name, 0)
        self.rotc[name] = v + 1
        return v % n

    def psum_gen(self):
        return self.rot("psg", 4)

    def rsqrt(self, sc, out_t, ssq_t, epsn):
        pr = self.pr
        tmp = self.small(sc)
        pr.ts("dve", tmp.t[:, 0:1], ssq_t.t[:, 0:1], float(epsn), None, ALU.add, None, [ssq_t.b], [tmp.b])
        pr.tt("pool", out_t.t[:, 0:1], tmp.t[:, 0:1], self.cst.t[:, 0:1], ALU.pow, [tmp.b, self.cst.b], [out_t.b])

    def small(self, sc):
        return sc.small()

    @staticmethod
    def pipelined(n, first, second):
        first(0)
        for t in range(n):
            if t + 1 < n:
                first(t + 1)
            second(t)

    def w_issue(self, req):
        pr = self.pr
        s = self.wslot_i % len(self.W)
        self.wslot_i += 1
        slot = self.W[s]
        pairs = []
        for (c0, src, kch) in req:
            n = src.shape[-1]
            pairs.append((slot.t[:, 0:kch, c0:c0 + n], src.rearrange("(kc p) n -> p kc n", p=128)))
        pr.dma("pool", pairs, slot.b, reads=(), writes=[slot.b])
        return slot

    def stream(self, reqs, consume):
        nW = len(self.W)
        slots = {}
        for j in range(min(nW - 1, len(reqs))):
            slots[j] = self.w_issue(reqs[j])
        for i in range(len(reqs)):
            j = i + nW - 1
            if j < len(reqs):
                slots[j] = self.w_issue(reqs[j])
            consume(i, slots.pop(i))

    def rope(self, sc, src_ap, src_buf, dst_ap, dst_buf, nb, hw, tab, tile_idx, dup_ap=None):
        pr = self.pr
        n = nb * 4 * hw
        ra = self.ra[self.rot("ra", len(self.ra))]
        rb = self.rb[self.rot("rb", len(self.rb))]
        v5 = lambda ap: ap.rearrange("p (n a h w) -> p n a h w", n=nb, a=2, h=2, w=hw)
        v3 = lambda ap: ap.rearrange("p (n f) -> p n f", n=nb)
        tC = tab.t[:, tile_idx, 0, :, :, :].rearrange("p a h w -> p (a h w)").unsqueeze(1).broadcast_to([128, nb, 4 * hw])
        tS = lambda h: tab.t[:, tile_idx, 1, :, h, :].unsqueeze(1).broadcast_to([128, nb, 2, hw])
        x5 = v5(src_ap)
        pr.tt("dve", v3(ra.t[:, 0:n]), v3(src_ap), tC, ALU.mult, [src_buf, tab.b], [ra.b])
        rb5 = v5(rb.t[:, 0:n])
        pr.tt("dve", rb5[:, :, :, 0, :], x5[:, :, :, 1, :], tS(0), ALU.mult, [src_buf, tab.b], [rb.b])
        pr.tt("dve", rb5[:, :, :, 1, :], x5[:, :, :, 0, :], tS(1), ALU.mult, [src_buf, tab.b], [rb.b])
        pr.tt("pool", dst_ap, ra.t[:, 0:n], rb.t[:, 0:n], ALU.add, [ra.b, rb.b], [dst_buf])
        if dup_ap is not None:
            pr.tt("pool", dup_ap, ra.t[:, 0:n], rb.t[:, 0:n], ALU.add, [ra.b, rb.b], [dst_buf])

    def attention(self, sc, qn, ktiles, st_parts, v_ap, v_reads, dvp1, scale, bias=None, o_base=4):
        return self.attention_multi([dict(qn=qn, ktiles=ktiles, st_parts=st_parts, v_ap=v_ap, v_reads=v_reads, bias=bias)],
                                    dvp1, scale, [o_base])[0]

    def attention_multi(self, jobs, dvp1, scale, o_bases):
        pr = self.pr
        J = []
        for jb, ob0 in zip(jobs, o_bases):
            nsub = jb["qn"] // 128
            J.append(dict(jb, nsub=nsub, nk=len(jb["ktiles"]), obanks=[self.PS[ob0 + s_] for s_ in range(nsub)], bis={}))

        def issue_st(j, idx, extra_reads=()):
            kt = j["ktiles"][idx]
            bi = self.psum_gen()
            pt_, pb = self.PS[bi]
            pairs, rds = j["st_parts"](kt)
            pr.mm(pt_[:, 0:j["qn"]], pairs, list(rds) + list(extra_reads), [pb])
            j["bis"][idx] = bi

        def issue_exp(j, idx):
            kt = j["ktiles"][idx]
            qn = j["qn"]
            pt_, pb = self.PS[j["bis"].pop(idx)]
            ptile = self.PT[self.rot("pt", len(self.PT))]
            bsl = j["bias"](kt) if j.get("bias") is not None else None
            if bsl is None:
                pr.act(ptile.t[:, 0:qn], pt_[:, 0:qn], AF.Exp, [pb], [ptile.b], scale=scale)
            else:
                bap, bbuf = bsl
                tmp = self.TMP[self.rot("tmp", len(self.TMP))]
                pr.stt("dve", tmp.t[:, 0:qn], pt_[:, 0:qn], scale, bap, ALU.mult, ALU.add, [pb, bbuf], [tmp.b])
                pr.act(ptile.t[:, 0:qn], tmp.t[:, 0:qn], AF.Exp, [tmp.b], [ptile.b])
            return ptile

        def issue_pv(j, idx, ptile):
            kt = j["ktiles"][idx]
            va = j["v_ap"](kt)
            obanks, nsub, nk = j["obanks"], j["nsub"], j["nk"]

            def fn(e):
                ins = None
                for s_ in range(nsub):
                    ins = e.matmul(obanks[s_][0][:, 0:dvp1], lhsT=ptile.t[:, s_ * 128:(s_ + 1) * 128], rhs=va,
                                   start=(idx == 0), stop=(idx == nk - 1))
                return ins
            pr.n_ins += nsub - 1
            pr.op("pe", fn, [ptile.b] + list(j["v_reads"]), [ob[1] for ob in obanks])

        nkmax = max(j["nk"] for j in J)
        if len(J) == 1:
            j = J[0]
            nk = j["nk"]
            issue_st(j, 0)
            if nk > 1:
                issue_st(j, 1)
            for idx in range(nk):
                ptile = issue_exp(j, idx)
                if idx + 2 < nk:
                    issue_st(j, idx + 2, extra_reads=[ptile.b])
                issue_pv(j, idx, ptile)
            return [j["obanks"]]
        for j in J:
            issue_st(j, 0)
        for idx in range(nkmax):
            for j in J:
                if idx + 1 < j["nk"]:
                    issue_st(j, idx + 1)
            pts = []
            for j in J:
                pts.append(issue_exp(j, idx) if idx < j["nk"] else None)
            for j, ptile in zip(J, pts):
                if ptile is not None:
                    issue_pv(j, idx, ptile)
        return [j["obanks"] for j in J]

    def mod_prepare(self, g0):
        pr, d = self.pr, self.d
        self.cb = g0.tile([128, 32], BF16, "cTb")
        with pr.scope() as sc:
            cT = sc.tile([128, 32], F32, "cT")
            th = sc.tile([128, 32], F32, "cth")
            pr.dma("sp", [(cT.t[:], d["condT"][:, :])], cT.b, (), [cT.b])
            pr.act(th.t[:], cT.t[:], AF.Tanh, [cT.b], [th.b], scale=0.5)
            pr.stt("dve", th.t[:], th.t[:], 1.0, cT.t[:], ALU.add, ALU.mult, [th.b, cT.b], [th.b])
            pr.ts("dve", self.cb.t[:], th.t[:], 0.5, None, ALU.mult, None, [th.b], [self.cb.b])

    def mod_stream(self, l, sc):
        pr, d = self.pr, self.d
        cb3 = self.cb.t[:].rearrange("p (k r) -> p k r", r=2)
        bt = sc.tile([2, 512], F32, "mbt")
        gt = sc.tile([2, 512], F32, "mgt")
        vt = sc.tile([2, 512], F32, "mvt")
        sq = float(math.sqrt(D))
        reqs = [[(0, d["w_ada"][l][:, c * 512:(c + 1) * 512], KC)] for c in range(12)]
        nW = len(self.W)
        slots = {}
        for j in range(nW - 1):
            slots[j] = self.w_issue(reqs[j])
        state = {"i": 0}

        def step():
            c = state["i"]
            if c >= 12:
                return
            state["i"] += 1
            j = c + nW - 1
            if j < 12:
                slots[j] = self.w_issue(reqs[j])
            slot = slots.pop(c)
            which, jb = c // 4, c % 4
            cs = slice(jb * 512, (jb + 1) * 512)
            pr.dma("sp", [(bt.t[:], self.bcast_rows(d["b_ada"][l, c * 512:(c + 1) * 512], 2))], bt.b, (), [bt.b])
            if which == 1:
                pr.dma("sp", [(gt.t[:], self.bcast_rows(d["g_pre"][l, cs], 2))], gt.b, (), [gt.b])
            elif which == 2:
                pr.dma("sp", [(gt.t[:], self.bcast_rows(d["g_post"][l, cs], 2))], gt.b, (), [gt.b])
            bi = self.psum_gen()
            pt_, pb = self.PS[bi]
            pr.mm(pt_[0:2, 0:512], [(cb3[:, kc, :], slot.t[:, kc, 0:512]) for kc in range(KC)], [self.cb.b, slot.b], [pb])
            pr.tt("dve", vt.t[:], pt_[0:2, 0:512], bt.t[:], ALU.add, [pb, bt.b], [vt.b])
            if which == 0:
                w_ = 1
            elif which == 1:
                pr.stt("dve", vt.t[:], vt.t[:], 1.0, gt.t[:], ALU.add, ALU.mult, [vt.b, gt.b], [vt.b])
                pr.ts("dve", vt.t[:], vt.t[:], sq, None, ALU.mult, None, [vt.b], [vt.b])
                w_ = 0
            else:
                pr.stt("dve", vt.t[:], vt.t[:], sq, gt.t[:], ALU.mult, ALU.mult, [vt.b, gt.b], [vt.b])
                w_ = 2
            pr.dma("sp", [(d["modv"][l, :, w_, cs], vt.t[:])], vt.b, [vt.b], [self.modv_b[l]])
        return step

    def load_mod(self, sc, l, g, which, name):
        pr, d = self.pr, self.d
        t = sc.tile([128, D], F32, name)
        pr.dma("sp", [(t.t[:], self.bcast_rows(d["modv"][l, g.mod_row, which, :]))], t.b, [self.modv_b[l]], [t.b])
        return t

    def phase_A(self, l, g):
        pr, d = self.pr, self.d
        with pr.scope() as sc:
            mA = self.load_mod(sc, l, g, 0, "mA")
            mB = self.load_mod(sc, l, g, 1, "mB")
            xts = [sc.tile([128, D], F32, "xt") for _ in range(3)]
            junk = sc.tile([128, D], BF16, "junk")
            t1s = [sc.tile([128, D], F32, "t1") for _ in range(1)]
            hbs = [sc.tile([128, D], BF16, "hb") for _ in range(2)]
            rst = {}

            def stage1(t):
                T = g.tile0 + t
                xt = xts[t % 3]
                if l == self.layers[0]:
                    src = d["xs"][t * 128:(t + 1) * 128, :] if g.name == "S" else d["xp"][t * 128:(t + 1) * 128, :]
                    pr.dma("sp", [(xt.t[:], src)], xt.b, (), [xt.b])
                else:
                    pr.dma("sp", [(xt.t[:], d["xst"][T * 128:(T + 1) * 128, :])], xt.b, [self.xst_b[T]], [xt.b])
                ssq = self.small(sc)
                rstd = self.small(sc)
                pr.act(junk.t[:], xt.t[:], AF.Square, [xt.b], [junk.b, ssq.b], accum_out=ssq.t[:, 0:1])
                self.rsqrt(sc, rstd, ssq, D * EPS)
                rst[t] = rstd

            def stage2(t):
                xt = xts[t % 3]
                rstd = rst.pop(t)
                t1 = t1s[0]
                hb = hbs[t % 2]
                pr.stt("dve", t1.t[:], xt.t[:], rstd.t[:, 0:1], mA.t[:], ALU.mult, ALU.mult,
                       [xt.b, rstd.b, mA.b], [t1.b])
                pr.tt("dve", hb.t[:], t1.t[:], mB.t[:], ALU.add, [t1.b, mB.b], [hb.b])
                for half in range(2):
                    bi = self.psum_gen()
                    pt_, pb = self.PS[bi]
                    pbf = self.ps(bi, 1024, BF16)
                    pr.transposes([(pbf[:, j * 128:(j + 1) * 128], hb.t[:, (half * 8 + j) * 128:(half * 8 + j + 1) * 128])
                                   for j in range(8)], self.ident.t[:], [hb.b, self.ident.b], [pb])
                    pr.copy("act", self.hT.t[:, half * 8:(half + 1) * 8, t * 128:(t + 1) * 128],
                            pbf.rearrange("p (a b) -> p a b", a=8), [pb], [self.hT_b[t]])

            stage1(0)
            stage1(1)
            for t in range(g.nt):
                if t + 2 < g.nt:
                    stage1(t + 2)
                stage2(t)

    def issue_wout(self, l):
        if self.wout_done.get(l):
            return
        self.wout_done[l] = True
        pr, d = self.pr, self.d
        wout = d["w_out_even"][l // 2] if l % 2 == 0 else d["w_out_odd"][l // 2]
        for c in range(4):
            pr.dma("pool", [(self.hT.t[:, :, c * 512:(c + 1) * 512],
                             wout[:, c * 512:(c + 1) * 512].rearrange("(kc p) n -> p kc n", p=128))],
                   self.hT_b[4 * c], (), self.hT_b[4 * c:4 * c + 4])

    def phase_C(self, l):
        pr, d = self.pr, self.d
        last = (l == self.layers[-1])
        wout = d["w_out_even"][l // 2] if l % 2 == 0 else d["w_out_odd"][l // 2]
        with pr.scope() as sc:
            self.issue_wout(l)
            Gt = sc.tile([128, D], F32, "mG")
            ysl = [sc.tile([128, D], BF16, "ysl") for _ in range(1)]
            xts = [sc.tile([128, D], F32, "xt") for _ in range(2)]
            yTs = [sc.tile([128, KC, 128], BF16, "yTt") for _ in range(2)]
            us = [sc.tile([128, D], F32, "u") for _ in range(2)]
            junk = sc.tile([128, 512], BF16, "junk")
            ssq4s = [sc.tile([128, 4], F32, "ssq4") for _ in range(2)]

            def prep(T):
                g = self.gS if T < 16 else self.gP
                t = T - g.tile0
                yl = ysl[0]
                xt = xts[T % 2]
                yT = yTs[T % 2]
                pr.dma("sp", [(yl.t[:], d["ysc"][T * 128:(T + 1) * 128, :])], yl.b, [self.ysc_b[T]], [yl.b])
                if l == self.layers[0]:
                    src = d["xs"][t * 128:(t + 1) * 128, :] if g.name == "S" else d["xp"][t * 128:(t + 1) * 128, :]
                    pr.dma("sp", [(xt.t[:], src)], xt.b, (), [xt.b])
                else:
                    pr.dma("sp", [(xt.t[:], d["xst"][T * 128:(T + 1) * 128, :])], xt.b, [self.xst_b[T]], [xt.b])
                for half in range(2):
                    bi = self.psum_gen()
                    pt_, pb = self.PS[bi]
                    pbf = self.ps(bi, 1024, BF16)
                    pr.transposes([(pbf[:, j * 128:(j + 1) * 128], yl.t[:, (half * 8 + j) * 128:(half * 8 + j + 1) * 128])
                                   for j in range(8)], self.ident.t[:], [yl.b, self.ident.b], [pb])
                    pr.copy("act", yT.t[:, half * 8:(half + 1) * 8, :], pbf.rearrange("p (a b) -> p a b", a=8),
                            [pb], [yT.b])

            mstep = None
            li = self.layers.index(l)
            if li + 1 < len(self.layers):
                mstep = self.mod_stream(self.layers[li + 1], sc)
            prep(0)
            for T in range(24):
                g = self.gS if T < 16 else self.gP
                t = T - g.tile0
                xt = xts[T % 2]
                yT = yTs[T % 2]
                u = us[T % 2]
                if mstep is not None and T % 2 == 1:
                    mstep()
                if t == 0:
                    pr.dma("sp", [(Gt.t[:], self.bcast_rows(d["modv"][l, g.mod_row, 2, :]))], Gt.b, [self.modv_b[l]], [Gt.b])
                ssq4 = ssq4s[T % 2]
                for c in range(4):
                    ot, ob = self.PS[4 + c]
                    pr.mm(ot[:, 0:512], [(yT.t[:, kc, :], self.hT.t[:, kc, c * 512:(c + 1) * 512]) for kc in range(KC)],
                          [yT.b] + self.hT_b[4 * c:4 * c + 4], [ob])
                    pr.act(junk.t[:, 0:512], ot[:, 0:512], AF.Square, [ob], [junk.b, ssq4.b],
                           accum_out=ssq4.t[:, c:c + 1])
                if T + 1 < 24:
                    prep(T + 1)
                ssq = self.small(sc)
                rstd = self.small(sc)
                pr.op("dve", lambda e, ssq=ssq, ssq4=ssq4: e.reduce_sum(out=ssq.t[:, 0:1], in_=ssq4.t[:, 0:4],
                                                                        axis=mybir.AxisListType.X), [ssq4.b], [ssq.b])
                self.rsqrt(sc, rstd, ssq, D * EPS)
                G = Gt
                for c in range(4):
                    ot, ob = self.PS[4 + c]
                    sl = slice(c * 512, (c + 1) * 512)
                    pr.stt("dve", u.t[:, sl], ot[:, 0:512], rstd.t[:, 0:1], G.t[:, sl], ALU.mult, ALU.mult,
                           [ob, rstd.b, G.b], [u.b])
                pr.tt("pool", u.t[:], u.t[:], xt.t[:], ALU.add, [u.b, xt.b], [u.b])
                if last:
                    dst = d["y_s"][t * 128:(t + 1) * 128, :] if g.name == "S" else d["y_p"][t * 128:(t + 1) * 128, :]
                    pr.dma("sp", [(dst, u.t[:])], u.b, [u.b], ())
                else:
                    pr.dma("sp", [(d["xst"][T * 128:(T + 1) * 128, :], u.t[:])], u.b, [u.b], [self.xst_b[T]])

    def phase_B_odd(self, l, g):
        pr, d = self.pr, self.d
        i = l // 2
        lam_init = 0.8 - 0.6 * math.exp(-0.3 * l)
        w = d["w_in_odd"][i]
        if self.cfg.get("dbg", 99) < 1:
            return
        with pr.scope() as sc:
            s2 = sc.tile([128, 2], F32, "s2")
            with pr.scope() as sl:
                lp = sl.tile([128, 4, 128], F32, "lp")
                pr.dma("sp", [(lp.t[:].rearrange("p a b -> p (a b)"), self.bcast_rows(d["diff_lambda"][i, :]))], lp.b, (), [lp.b])
                prod = sl.tile([128, 2, 128], F32, "prod")
                lp4 = lp.t[:].rearrange("p (a two) b -> p a two b", two=2)
                pr.tt("dve", prod.t[:], lp4[:, :, 0, :], lp4[:, :, 1, :], ALU.mult, [lp.b], [prod.b])
                pr.op("dve", lambda e: e.reduce_sum(out=s2.t[:], in_=prod.t[:], axis=mybir.AxisListType.X), [prod.b], [s2.b])
            e2 = sc.tile([128, 2], F32, "e2")
            pr.act(e2.t[:], s2.t[:], AF.Exp, [s2.b], [e2.b])
            nlam = sc.tile([128, 2], F32, "nlam")
            pr.tt("dve", nlam.t[:, 0:1], e2.t[:, 1:2], e2.t[:, 0:1], ALU.subtract, [e2.b], [nlam.b])
            pr.ts("dve", nlam.t[:, 1:2], nlam.t[:, 0:1], -lam_init, None, ALU.add, None, [nlam.b], [nlam.b])
            gsub = sc.tile([128, 256], F32, "gsub")
            pr.dma("sp", [(gsub.t[:], self.bcast_rows(d["diff_g"][i, :]))], gsub.b, (), [gsub.b])
            pr.ts("dve", gsub.t[:], gsub.t[:], (1.0 - lam_init) * 0.5 * 16.0, None, ALU.mult, None, [gsub.b], [gsub.b])
            QKT = sc.tile([128, 4, g.klen], BF16, "QKT")
            QKT_q = sc.buf("QKT_q"); QKT_k = sc.buf("QKT_k")
            VA = sc.tile([128, g.nkt, 264], BF16, "VA")
            SG = sc.tile([128, g.nt, 256], BF16, "SG")
            pr.memset("dve", VA.t[:, :, 256:257], 1.0, [VA.b])
            qkb = [sc.tile([128, 512], BF16, "qkb") for _ in range(2)]
            thb = [sc.tile([128, 256], F32, "thb") for _ in range(1)]
            oraw = [sc.tile([128, 4, 264], F32, "oraw") for _ in range(2)]
            orb = [[sc.buf(f"orb{s_}{j}") for j in range(4)] for s_ in range(2)]
            yst = [sc.tile([128, 256], BF16, "yst") for _ in range(3)]
            junks = [sc.tile([128, 256], BF16, "junk") for _ in range(2)]
            kvst = [sc.tile([128, 512], F32, "kvst") for _ in range(2)] if not g.ctx else []
            kctx = sc.tile([128, 4, 256], BF16, "kctx") if g.ctx else None

            reqs = []
            for h in range(8):
                reqs.append([(0, w[:, h * 256:(h + 1) * 256], KC), (256, w[:, 2048 + h * 256:2048 + (h + 1) * 256], KC)])
                reqs.append([(0, w[:, 4096 + h * 256:4096 + (h + 1) * 256], KC),
                             (256, w[:, 6144 + h * 256:6144 + (h + 1) * 256], KC)])

            dbg = self.cfg.get("dbg", 99)

            def consume(ri, slot):
                h, kind = ri // 2, ri % 2
                if dbg < 2 or (dbg < 3 and kind == 1) or dbg == 5:
                    return
                if kind == 0:
                    if g.ctx and not self.cfg.get("no_ctx"):
                        pr.dma("pool", [(kctx.t[:], d["cdk"][i][:, h * 256:(h + 1) * 256].rearrange("(c p) n -> p c n", p=128))],
                               kctx.b, (), [kctx.b])
                        pr.dma("pool", [(VA.t[:, 16:20, 0:256],
                                         d["cdv"][i][:, h * 256:(h + 1) * 256].rearrange("(c p) n -> p c n", p=128))],
                               VA.b, (), [VA.b])
                    def first(t):
                        bi = self.psum_gen()
                        pt_, pb = self.PS[bi]
                        pr.mm(pt_[:, 0:512], [(self.hT.t[:, kc, t * 128:(t + 1) * 128], slot.t[:, kc, 0:512]) for kc in range(KC)],
                              [self.hT_b[t], slot.b], [pb])
                        qb_ = qkb[t % 2]
                        if g.rope:
                            self.rope(sc, pt_[:, 0:512], pb, qb_.t[:], qb_.b, 4, 32, self.ropeD, t)
                        else:
                            pr.copy("act", qb_.t[:], pt_[:, 0:512], [pb], [qb_.b])
                            ks = kvst[self.rot("kvst", 2)]
                            pr.copy("act", ks.t[:, 0:256], pt_[:, 256:512], [pb], [ks.b])
                            sq_, tt_ = t // 2, (t % 2) * 128
                            pr.dma("sp", [(d["o_dk"][sq_, i, tt_:tt_ + 128, h * 256:(h + 1) * 256], ks.t[:, 0:256])],
                                   ks.b, [ks.b], ())

                    def second(t):
                        qb_ = qkb[t % 2]
                        bj = self.psum_gen()
                        pj, pjb = self.PS[bj]
                        pbf = self.ps(bj, 512, BF16)
                        pr.transposes([(pbf[:, j * 128:(j + 1) * 128], qb_.t[:, j * 128:(j + 1) * 128]) for j in range(4)],
                                      self.ident.t[:], [qb_.b, self.ident.b], [pjb])
                        pr.copy("act", QKT.t[:, :, t * 128:(t + 1) * 128], pbf.rearrange("p (a b) -> p a b", a=4),
                                [pjb], [QKT_q, QKT_k])
                    self.pipelined(g.nt, first, second)
                    if g.ctx and not self.cfg.get("no_ctx"):
                        bj = self.psum_gen()
                        pj, pjb = self.PS[bj]
                        pbf = self.ps(bj, 1024, BF16)
                        pr.transposes([(pbf[:, (s * 4 + c) * 128:(s * 4 + c + 1) * 128], kctx.t[:, c, s * 128:(s + 1) * 128])
                                       for s in range(2) for c in range(4)], self.ident.t[:], [kctx.b, self.ident.b], [pjb])
                        pr.copy(self.evac_eng(), QKT.t[:, 2:4, 2048:2560], pbf.rearrange("p (a b) -> p a b", a=2),
                                [pjb], [QKT_k])
                else:
                    for t in range(g.nt):
                        bi = self.psum_gen()
                        pt_, pb = self.PS[bi]
                        pr.mm(pt_[:, 0:512], [(self.hT.t[:, kc, t * 128:(t + 1) * 128], slot.t[:, kc, 0:512]) for kc in range(KC)],
                              [self.hT_b[t], slot.b], [pb])
                        pr.copy("act", VA.t[:, t, 0:256], pt_[:, 0:256], [pb], [VA.b])
                        th_ = thb[0]
                        pr.act(th_.t[:], pt_[:, 256:512], AF.Tanh, [pb], [th_.b], scale=0.5)
                        pr.stt("dve", SG.t[:, t, :], th_.t[:], 1.0, pt_[:, 256:512], ALU.add, ALU.mult, [th_.b, pb], [SG.b])
                        if not g.ctx:
                            ks = kvst[self.rot("kvst", 2)]
                            pr.copy("act", ks.t[:, 0:256], pt_[:, 0:256], [pb], [ks.b])
                            sq_, tt_ = t // 2, (t % 2) * 128
                            pr.dma("sp", [(d["o_dv"][sq_, i, tt_:tt_ + 128, h * 256:(h + 1) * 256], ks.t[:, 0:256])],
                                   ks.b, [ks.b], ())
                    if g.name == "P" and h == 7:
                        self.issue_wout(l)
                    for (q0, qn, ktiles) in (g.qblocks if dbg >= 4 else []):
                        nsub = qn // 128
                        def mkjob(s, q0=q0, qn=qn, ktiles=ktiles):
                            def st_parts(kt):
                                return ([(QKT.t[:, 2 + s, kt * 128:(kt + 1) * 128], QKT.t[:, s, q0:q0 + qn])], [QKT_q, QKT_k])
                            return dict(qn=qn, ktiles=ktiles, st_parts=st_parts, v_ap=lambda kt: VA.t[:, kt, 0:257],
                                        v_reads=[VA.b], bias=None)
                        if nsub <= 2:
                            obs = self.attention_multi([mkjob(0), mkjob(1)], 257, DIFF_SCALE, [4, 6])
                        else:
                            obs = [self.attention_multi([mkjob(s)], 257, DIFF_SCALE, [4])[0] for s in range(1)]
                        for s in range(2):
                            if nsub > 2:
                                ob = obs[0] if s == 0 else self.attention_multi([mkjob(1)], 257, DIFF_SCALE, [4])[0]
                            else:
                                ob = obs[s]
                            raw = oraw[s]
                            for sub in range(nsub):
                                ot, obuf = ob[sub]
                                pr.copy("dve", raw.t[:, sub, 0:257], ot[:, 0:257], [obuf], [orb[s][sub]])
                        subs = list(range(nsub))
                        B1 = [orb[0][j] for j in subs]; B2 = [orb[1][j] for j in subs]
                        O1 = [oraw[0].t[:, j, :] for j in subs]; O2 = [oraw[1].t[:, j, :] for j in subs]
                        R1 = [self.small(sc) for _ in subs]; R2 = [self.small(sc) for _ in subs]
                        SS = [self.small(sc) for _ in subs]; TM = [self.small(sc) for _ in subs]; RS = [self.small(sc) for _ in subs]
                        for j in subs:
                            pr.op("dve", lambda e, r=R1[j], o1=O1[j]: e.reciprocal(out=r.t[:, 0:1], in_=o1[:, 256:257]), [B1[j]], [R1[j].b])
                        for j in subs:
                            pr.op("dve", lambda e, r=R2[j], o2=O2[j]: e.reciprocal(out=r.t[:, 0:1], in_=o2[:, 256:257]), [B2[j]], [R2[j].b])
                        for j in subs:
                            pr.ts("dve", O1[j][:, 0:256], O1[j][:, 0:256], R1[j].t[:, 0:1], None, ALU.mult, None, [B1[j], R1[j].b], [B1[j]])
                        for j in subs:
                            pr.tt("dve", R2[j].t[:, 1:2], R2[j].t[:, 0:1], nlam.t[:, 1:2], ALU.mult, [R2[j].b, nlam.b], [R2[j].b])
                        for j in subs:
                            pr.stt("dve", O2[j][:, 0:256], O2[j][:, 0:256], R2[j].t[:, 1:2], O1[j][:, 0:256], ALU.mult, ALU.add,
                                   [B2[j], R2[j].b, B1[j]], [B2[j]])
                        for j in subs:
                            jk = junks[j % 2]
                            pr.stt("dve", jk.t[:], O2[j][:, 0:256], 1.0, O2[j][:, 0:256], ALU.mult, ALU.mult, [B2[j]], [jk.b, SS[j].b],
                                   accum_out=SS[j].t[:, 0:1])
                        for j in subs:
                            pr.ts("dve", TM[j].t[:, 0:1], SS[j].t[:, 0:1], float(256 * EPS), None, ALU.add, None, [SS[j].b], [TM[j].b])
                        for j in subs:
                            pr.tt("pool", RS[j].t[:, 0:1], TM[j].t[:, 0:1], self.cst.t[:, 0:1], ALU.pow, [TM[j].b, self.cst.b], [RS[j].b])
                        for j in subs:
                            pr.stt("dve", O2[j][:, 0:256], O2[j][:, 0:256], RS[j].t[:, 0:1], gsub.t[:], ALU.mult, ALU.mult,
                                   [B2[j], RS[j].b, gsub.b], [B2[j]])
                        for j in subs:
                            tq = q0 // 128 + j
                            ys_ = yst[self.rot("yst", 3)]
                            pr.tt("dve", ys_.t[:], O2[j][:, 0:256], SG.t[:, tq, :], ALU.mult, [B2[j], SG.b], [ys_.b])
                            T = g.tile0 + tq
                            pr.dma("sp", [(d["ysc"][T * 128:(T + 1) * 128, h * 256:(h + 1) * 256], ys_.t[:])],
                                   ys_.b, [ys_.b], [self.ysc_b[T]])
            if dbg != 1:
                self.stream(reqs, consume)

    def phase_B_even(self, l, g):
        self.na_stage(l, g)
        self.mla_stage(l, g)

    def na_stage(self, l, g):
        pr, d = self.pr, self.d
        i = l // 2
        w = d["w_in_even"][i]
        with pr.scope() as sc:
            QaT = sc.tile([128, g.ntok], BF16, "QaT")
            KaT = sc.tile([128, g.klen], BF16, "KaT")
            VA = sc.tile([128, g.nkt, 136], BF16, "VAa")
            SG = sc.tile([128, g.nt, 128], BF16, "SGa")
            pr.memset("dve", VA.t[:, :, 128:129], 2.0, [VA.b])
            thb = [sc.tile([128, 128], F32, "thb") for _ in range(2)]
            yst = [sc.tile([128, 128], BF16, "yst") for _ in range(3)]
            oraw = sc.tile([128, 4, 132], F32, "oraw")
            orb = [sc.buf(f"orb{j}") for j in range(4)]
            kvst = [sc.tile([128, 256], F32, "kvst") for _ in range(2)] if not g.ctx else []
            kctx = sc.tile([128, 4, 128], BF16, "kctx") if g.ctx else None
            tabs = [sc.tile([128, 2, 1408], F32, "natab") for _ in range(2)] if g.ctx else []
            self.TMP = [sc.tile([128, 512], F32, "TMP") for _ in range(2)]
            reqs = [[(j * 128, w[:, j * 1024 + h * 128:j * 1024 + (h + 1) * 128], KC) for j in range(4)] for h in range(8)]

            def consume(h, slot):
                tab = None
                if g.ctx:
                    tab = tabs[h % 2]
                    pr.dma("sp", [(tab.t[:], d["natab"][i, h].rearrange("v p n -> p v n"))], tab.b, (), [tab.b])
                    pr.dma("pool", [(kctx.t[:], d["cnak"][i][:, h * 128:(h + 1) * 128].rearrange("(c p) n -> p c n", p=128))],
                           kctx.b, (), [kctx.b])
                    pr.dma("pool", [(VA.t[:, 16:20, 0:128],
                                     d["cnav"][i][:, h * 128:(h + 1) * 128].rearrange("(c p) n -> p c n", p=128))],
                           VA.b, (), [VA.b])
                for which, dst in ((0, QaT), (1, KaT)):
                    for qb in range(g.ntok // 512):
                        bi = self.psum_gen()
                        pt_, pb = self.PS[bi]
                        pr.mm(pt_[:, 0:512], [(slot.t[:, kc, which * 128:(which + 1) * 128], self.hT.t[:, kc, qb * 512:(qb + 1) * 512])
                                               for kc in range(KC)], [slot.b] + self.hT_b[4 * qb:4 * qb + 4], [pb])
                        pr.copy(self.evac_eng(), dst.t[:, qb * 512:(qb + 1) * 512], pt_[:, 0:512], [pb], [dst.b])
                if g.ctx:
                    bj = self.psum_gen()
                    pj, pjb = self.PS[bj]
                    pbf = self.ps(bj, 512, BF16)
                    pr.transposes([(pbf[:, c * 128:(c + 1) * 128], kctx.t[:, c, :]) for c in range(4)],
                                  self.ident.t[:], [kctx.b, self.ident.b], [pjb])
                    pr.copy(self.evac_eng(), KaT.t[:, 2048:2560], pbf, [pjb], [KaT.b])
                for t in range(g.nt):
                    bi = self.psum_gen()
                    pt_, pb = self.PS[bi]
                    c0 = 256 if g.ctx else 128
                    n = 512 - c0
                    pr.mm(pt_[:, 0:n], [(self.hT.t[:, kc, t * 128:(t + 1) * 128], slot.t[:, kc, c0:512]) for kc in range(KC)],
                          [self.hT_b[t], slot.b], [pb])
                    vo = n - 256
                    pr.copy("act", VA.t[:, t, 0:128], pt_[:, vo:vo + 128], [pb], [VA.b])
                    th_ = thb[t % 2]
                    pr.act(th_.t[:], pt_[:, vo + 128:vo + 256], AF.Tanh, [pb], [th_.b], scale=0.5)
                    pr.stt("dve", SG.t[:, t, :], th_.t[:], 1.0, pt_[:, vo + 128:vo + 256], ALU.add, ALU.mult, [th_.b, pb], [SG.b])
                    if not g.ctx:
                        ks = kvst[self.rot("kvst", 2)]
                        pr.copy("act", ks.t[:, 0:256], pt_[:, 0:256], [pb], [ks.b])
                        sq_, tt_ = t // 2, (t % 2) * 128
                        pr.dma("sp", [(d["o_nak"][sq_, i, tt_:tt_ + 128, h * 128:(h + 1) * 128], ks.t[:, 0:128]),
                                      (d["o_nav"][sq_, i, tt_:tt_ + 128, h * 128:(h + 1) * 128], ks.t[:, 128:256])],
                               ks.b, [ks.b], ())
                groups = [[b] for b in range(4)] if g.ctx else [[0, 1], [2, 3]]
                for grp in groups:
                    jobs = []
                    for bq in grp:
                        q0, qn, ktiles = g.qblocks[bq]
                        bias = None
                        if g.ctx:
                            wt = na_window_tiles(bq)
                            ktiles = [16, 17, 18, 19] + [j for (j, _, _) in wt]
                            info = {j: (v, t0) for (j, v, t0) in wt}

                            def bias(kt, info=info, tab=tab):
                                if kt >= 16:
                                    return None
                                v, t0 = info[kt]
                                return (tab.t[:, v, t0 * 64:t0 * 64 + 512], tab.b)

                        def st_parts(kt, q0=q0, qn=qn):
                            return ([(KaT.t[:, kt * 128:(kt + 1) * 128], QaT.t[:, q0:q0 + qn])], [KaT.b, QaT.b])
                        jobs.append(dict(qn=qn, ktiles=ktiles, st_parts=st_parts, v_ap=lambda kt: VA.t[:, kt, 0:129],
                                         v_reads=[VA.b], bias=bias))
                    obs = self.attention_multi(jobs, 129, NA_SCALE, [4, 6][:len(jobs)])
                    idxs = []
                    for ji, bq in enumerate(grp):
                        q0, qn, _ = g.qblocks[bq]
                        for sub in range(qn // 128):
                            ot, obuf = obs[ji][sub]
                            oi = ji * 2 + sub if len(grp) > 1 else sub
                            pr.copy("dve", oraw.t[:, oi, 0:129], ot[:, 0:129], [obuf], [orb[oi]])
                            idxs.append((oi, q0 // 128 + sub))
                    for (oi, tq) in idxs:
                        r = self.small(sc)
                        o_ = oraw.t[:, oi, :]
                        pr.op("dve", lambda e, r=r, o_=o_: e.reciprocal(out=r.t[:, 0:1], in_=o_[:, 128:129]), [orb[oi]], [r.b])
                        ys_ = yst[self.rot("yst", 3)]
                        pr.stt("dve", ys_.t[:], o_[:, 0:128], r.t[:, 0:1], SG.t[:, tq, :], ALU.mult, ALU.mult,
                               [orb[oi], r.b, SG.b], [ys_.b])
                        T = g.tile0 + tq
                        pr.dma("sp", [(d["ysc"][T * 128:(T + 1) * 128, h * 128:(h + 1) * 128], ys_.t[:])],
                               ys_.b, [ys_.b], [self.ysc_b[T]])
            self.stream(reqs, consume)
            for x in yst + kvst + tabs + ([kctx] if kctx else []) + [VA]:
                pr.release_dma(x.b)

    def mla_stage(self, l, g):
        pr, d = self.pr, self.d
        i = l // 2
        w = d["w_in_even"][i]
        with pr.scope() as sc:
            cqT = sc.tile([128, 4, g.ntok], BF16, "cqT")
            CKT = sc.tile([128, 3, g.klen], BF16, "CKT")
            gqb = sc.tile([128, 512], F32, "gqb")
            gkvb = sc.tile([128, 256], F32, "gkvb")
            pr.dma("sp", [(gqb.t[:], self.bcast_rows(d["mla_g_q"][i, :]))], gqb.b, (), [gqb.b])
            pr.ts("dve", gqb.t[:], gqb.t[:], float(math.sqrt(512.0)), None, ALU.mult, None, [gqb.b], [gqb.b])
            pr.dma("sp", [(gkvb.t[:], self.bcast_rows(d["mla_g_kv"][i, :]))], gkvb.b, (), [gkvb.b])
            pr.ts("dve", gkvb.t[:], gkvb.t[:], 16.0, None, ALU.mult, None, [gkvb.b], [gkvb.b])
            with pr.scope() as s1:
                junk = s1.tile([128, 512], BF16, "junk")
                cqn = [s1.tile([128, 512], BF16, "cqn") for _ in range(2)]
                cat = [s1.tile([128, 384], BF16, "cat") for _ in range(2)]
                ckf = [s1.tile([128, 320], F32, "ckf") for _ in range(2)] if not g.ctx else []
                thb = [s1.tile([128, 512], F32, "thb") for _ in range(2)]
                sgst = [s1.tile([128, 512], BF16, "sgst") for _ in range(2)]
                reqs = [[(0, w[:, 4096:4608], KC)], [(0, w[:, 4608:4928], KC)],
                        [(0, w[:, 4928:5440], KC)], [(0, w[:, 5440:5952], KC)]]

                def consume(ri, slot):
                    n = 320 if ri == 1 else 512

                    def proj(t):
                        bi = self.psum_gen()
                        pt_, pb = self.PS[bi]
                        pr.mm(pt_[:, 0:n], [(self.hT.t[:, kc, t * 128:(t + 1) * 128], slot.t[:, kc, 0:n]) for kc in range(KC)],
                              [self.hT_b[t], slot.b], [pb])
                        return pt_, pb

                    if ri == 0:
                        def first(t):
                            pt_, pb = proj(t)
                            ssq = self.small(s1); rs = self.small(s1)
                            pr.act(junk.t[:], pt_[:, 0:512], AF.Square, [pb], [junk.b, ssq.b], accum_out=ssq.t[:, 0:1])
                            self.rsqrt(s1, rs, ssq, 512 * EPS)
                            cq_ = cqn[t % 2]
                            pr.stt("dve", cq_.t[:], pt_[:, 0:512], rs.t[:, 0:1], gqb.t[:], ALU.mult, ALU.mult,
                                   [pb, rs.b, gqb.b], [cq_.b])

                        def second(t):
                            cq_ = cqn[t % 2]
                            bj = self.psum_gen()
                            pj, pjb = self.PS[bj]
                            pbf = self.ps(bj, 512, BF16)
                            pr.transposes([(pbf[:, j * 128:(j + 1) * 128], cq_.t[:, j * 128:(j + 1) * 128]) for j in range(4)],
                                          self.ident.t[:], [cq_.b, self.ident.b], [pjb])
                            pr.copy("act", cqT.t[:, :, t * 128:(t + 1) * 128], pbf.rearrange("p (a b) -> p a b", a=4),
                                    [pjb], [cqT.b])
                        self.pipelined(g.nt, first, second)
                    elif ri == 1:
                        def first(t):
                            pt_, pb = proj(t)
                            ssq = self.small(s1); rs = self.small(s1)
                            pr.act(junk.t[:, 0:256], pt_[:, 0:256], AF.Square, [pb], [junk.b, ssq.b], accum_out=ssq.t[:, 0:1])
                            self.rsqrt(s1, rs, ssq, 256 * EPS)
                            ct_ = cat[t % 2]
                            if g.ctx:
                                pr.stt("dve", ct_.t[:, 0:256], pt_[:, 0:256], rs.t[:, 0:1], gkvb.t[:], ALU.mult, ALU.mult,
                                       [pb, rs.b, gkvb.b], [ct_.b])
                                self.rope(s1, pt_[:, 256:320], pb, ct_.t[:, 256:320], ct_.b, 1, 16, self.ropeM, t,
                                          dup_ap=ct_.t[:, 320:384])
                            else:
                                cf = ckf[t % 2]
                                pr.stt("dve", cf.t[:, 0:256], pt_[:, 0:256], rs.t[:, 0:1], gkvb.t[:], ALU.mult, ALU.mult,
                                       [pb, rs.b, gkvb.b], [cf.b])
                                pr.copy("dve", cf.t[:, 256:320], pt_[:, 256:320], [pb], [cf.b])
                                pr.copy("act", ct_.t[:, 0:320], cf.t[:, 0:320], [cf.b], [ct_.b])
                                pr.copy("act", ct_.t[:, 320:384], cf.t[:, 256:320], [cf.b], [ct_.b])
                                sq_, tt_ = t // 2, (t % 2) * 128
                                pr.dma("sp", [(d["o_ckv"][sq_, i, tt_:tt_ + 128, :], cf.t[:, 0:256]),
                                              (d["o_kpe"][sq_, i, tt_:tt_ + 128, :], cf.t[:, 256:320])],
                                       cf.b, [cf.b], ())

                        def second(t):
                            ct_ = cat[t % 2]
                            bj = self.psum_gen()
                            pj, pjb = self.PS[bj]
                            pbf = self.ps(bj, 384, BF16)
                            pr.transposes([(pbf[:, j * 128:(j + 1) * 128], ct_.t[:, j * 128:(j + 1) * 128]) for j in range(3)],
                                          self.ident.t[:], [ct_.b, self.ident.b], [pjb])
                            pr.copy("act", CKT.t[:, :, t * 128:(t + 1) * 128], pbf.rearrange("p (a b) -> p a b", a=3),
                                    [pjb], [CKT.b])
                        self.pipelined(g.nt, first, second)
                    else:
                        for t in range(g.nt):
                            pt_, pb = proj(t)
                            th_ = thb[t % 2]
                            c0 = (ri - 2) * 512
                            pr.act(th_.t[:], pt_[:, 0:512], AF.Tanh, [pb], [th_.b], scale=0.5)
                            sg_ = sgst[t % 2]
                            pr.stt("dve", sg_.t[:], th_.t[:], 1.0, pt_[:, 0:512], ALU.add, ALU.mult, [th_.b, pb], [sg_.b])
                            T = g.tile0 + t
                            pr.dma("sp", [(d["sgsc"][T * 128:(T + 1) * 128, c0:c0 + 512], sg_.t[:])], sg_.b, [sg_.b],
                                   [self.sgsc_b[T]])
                self.stream(reqs, consume)
                if g.ctx:
                    cc = s1.tile([128, 4, 384], BF16, "ctxcat")
                    pr.dma("pool", [(cc.t[:, :, 0:256], d["cckv"][i].rearrange("(c p) n -> p c n", p=128)),
                                    (cc.t[:, :, 256:320], d["ckpe"][i].rearrange("(c p) n -> p c n", p=128)),
                                    (cc.t[:, :, 320:384], d["ckpe"][i].rearrange("(c p) n -> p c n", p=128))],
                           cc.b, (), [cc.b])
                    for c in range(4):
                        bj = self.psum_gen()
                        pj, pjb = self.PS[bj]
                        pbf = self.ps(bj, 384, BF16)
                        pr.transposes([(pbf[:, j * 128:(j + 1) * 128], cc.t[:, c, j * 128:(j + 1) * 128]) for j in range(3)],
                                      self.ident.t[:], [cc.b, self.ident.b], [pjb])
                        pr.copy(self.evac_eng(), CKT.t[:, :, 2048 + c * 128:2048 + (c + 1) * 128],
                                pbf.rearrange("p (a b) -> p a b", a=3), [pjb], [CKT.b])
                    pr.release_dma(cc.b)
                for x in ckf:
                    pr.release_dma(x.b)
            if g.name == "P":
                self.issue_wout(l)
            with pr.scope() as s2:
                sl0 = self.W[self.wslot_i % 3]; sl1 = self.W[(self.wslot_i + 1) % 3]
                self.wslot_i += 2
                Wuq = Tile(sl0.t[:, 0:12, :].rearrange("p a b -> p (a b)").rearrange("p (k n) -> p k n", k=4), sl0.b)
                Wukv = Tile(sl1.t[:, 0:8, :].rearrange("p a b -> p (a b)").rearrange("p (k n) -> p k n", k=2), sl1.b)
                pr.dma("pool", [(Wuq.t, d["mla_w_uq"][i].rearrange("(kc p) n -> p kc n", p=128))], Wuq.b, (), [Wuq.b])
                pr.dma("pool", [(Wukv.t, d["mla_w_ukv"][i].rearrange("(kc p) n -> p kc n", p=128))], Wukv.b, (), [Wukv.b])
                QrT = s2.tile([128, g.ntok], BF16, "QrT")
                QnT = s2.tile([128, g.ntok], BF16, "QnT")
                KnT = s2.tile([128, g.klen], BF16, "KnT")
                VM = s2.tile([128, g.nkt, 136], BF16, "VM")
                pr.memset("dve", VM.t[:, :, 128:129], 2.0, [VM.b])
                qrb = [s2.tile([128, 128], BF16, "qrb") for _ in range(2)]
                yst = [s2.tile([128, 128], BF16, "yst") for _ in range(3)]
                SGh = s2.tile([128, g.nt, 128], BF16, "SGh")
                oraw = s2.tile([128, 4, 132], F32, "oraw")
                orb = [s2.buf(f"orb{j}") for j in range(4)]
                Wuq4 = Wuq.t.rearrange("p k (h d) -> p k h d", d=192)
                for h in range(8):
                    hp = h % 2
                    pr.dma("sp", [(SGh.t[:], d["sgsc"][g.tile0 * 128:(g.tile0 + g.nt) * 128, h * 128:(h + 1) * 128]
                                   .rearrange("(t p) n -> p t n", p=128))], SGh.b,
                           self.sgsc_b[g.tile0:g.tile0 + g.nt], [SGh.b])
                    if hp == 0:
                        def first(t, h=h):
                            bi = self.psum_gen()
                            pt_, pb = self.PS[bi]
                            pr.mm(pt_[:, 0:128].rearrange("p (h d) -> p h d", d=64),
                                  [(cqT.t[:, kc, t * 128:(t + 1) * 128], Wuq4[:, kc, h:h + 2, 128:192]) for kc in range(4)],
                                  [cqT.b, Wuq.b], [pb])
                            qr_ = qrb[t % 2]
                            if g.rope:
                                self.rope(s2, pt_[:, 0:128], pb, qr_.t[:], qr_.b, 2, 16, self.ropeM, t)
                            else:
                                pr.copy("act", qr_.t[:], pt_[:, 0:128], [pb], [qr_.b])

                        def second(t):
                            qr_ = qrb[t % 2]
                            bj = self.psum_gen()
                            pj, pjb = self.PS[bj]
                            pbf = self.ps(bj, 128, BF16)
                            pr.transposes([(pbf, qr_.t[:])], self.ident.t[:], [qr_.b, self.ident.b], [pjb])
                            pr.copy("act", QrT.t[:, t * 128:(t + 1) * 128], pbf, [pjb], [QrT.b])
                        self.pipelined(g.nt, first, second)
                    for qb in range(g.ntok // 512):
                        bi = self.psum_gen()
                        pt_, pb = self.PS[bi]
                        pr.mm(pt_[:, 0:512], [(Wuq.t[:, kc, h * 192:h * 192 + 128], cqT.t[:, kc, qb * 512:(qb + 1) * 512])
                                               for kc in range(4)], [Wuq.b, cqT.b], [pb])
                        pr.copy(self.evac_eng(), QnT.t[:, qb * 512:(qb + 1) * 512], pt_[:, 0:512], [pb], [QnT.b])
                    for kb in range(g.klen // 512):
                        bi = self.psum_gen()
                        pt_, pb = self.PS[bi]
                        pr.mm(pt_[:, 0:512], [(Wukv.t[:, kc, h * 256:h * 256 + 128], CKT.t[:, kc, kb * 512:(kb + 1) * 512])
                                               for kc in range(2)], [Wukv.b, CKT.b], [pb])
                        pr.copy(self.evac_eng(), KnT.t[:, kb * 512:(kb + 1) * 512], pt_[:, 0:512], [pb], [KnT.b])
                    for k0 in range(0, g.nkt, 4):
                        bi = self.psum_gen()
                        pt_, pb = self.PS[bi]

                        def fn(e, k0=k0, pt_=pt_, h=h):
                            ins = None
                            for j in range(4):
                                kt = k0 + j
                                for kc in range(2):
                                    ins = e.matmul(pt_[:, j * 128:(j + 1) * 128], lhsT=CKT.t[:, kc, kt * 128:(kt + 1) * 128],
                                                   rhs=Wukv.t[:, kc, h * 256 + 128:h * 256 + 256], start=(kc == 0), stop=(kc == 1))
                            return ins
                        pr.n_ins += 7
                        pr.op("pe", fn, [CKT.b, Wukv.b], [pb])
                        pr.copy(self.evac_eng(), VM.t[:, k0:k0 + 4, 0:128], pt_[:, 0:512].rearrange("p (a b) -> p a b", a=4),
                                [pb], [VM.b])
                    groups = [[b] for b in range(4)] if g.ctx else [[0, 1], [2, 3]]
                    for grp in groups:
                        jobs = []
                        for bq in grp:
                            q0, qn, ktiles = g.qblocks[bq]

                            def st_parts(kt, q0=q0, qn=qn, hp=hp):
                                return ([(KnT.t[:, kt * 128:(kt + 1) * 128], QnT.t[:, q0:q0 + qn]),
                                         (CKT.t[hp * 64:(hp + 1) * 64, 2, kt * 128:(kt + 1) * 128],
                                          QrT.t[hp * 64:(hp + 1) * 64, q0:q0 + qn])], [KnT.b, QnT.b, CKT.b, QrT.b])
                            jobs.append(dict(qn=qn, ktiles=ktiles, st_parts=st_parts, v_ap=lambda kt: VM.t[:, kt, 0:129],
                                             v_reads=[VM.b], bias=None))
                        obs = self.attention_multi(jobs, 129, MLA_SCALE, [4, 6][:len(jobs)])
                        idxs = []
                        for ji, bq in enumerate(grp):
                            q0, qn, _ = g.qblocks[bq]
                            for sub in range(qn // 128):
                                ot, obuf = obs[ji][sub]
                                oi = ji * 2 + sub if len(grp) > 1 else sub
                                pr.copy("dve", oraw.t[:, oi, 0:129], ot[:, 0:129], [obuf], [orb[oi]])
                                idxs.append((oi, q0 // 128 + sub))
                        for (oi, tq) in idxs:
                            r = self.small(s2)
                            o_ = oraw.t[:, oi, :]
                            pr.op("dve", lambda e, r=r, o_=o_: e.reciprocal(out=r.t[:, 0:1], in_=o_[:, 128:129]), [orb[oi]], [r.b])
                            ys_ = yst[self.rot("yst", 3)]
                            pr.stt("dve", ys_.t[:], o_[:, 0:128], r.t[:, 0:1], SGh.t[:, tq, :],
                                   ALU.mult, ALU.mult, [orb[oi], r.b, SGh.b], [ys_.b])
                            T = g.tile0 + tq
                            pr.dma("sp", [(d["ysc"][T * 128:(T + 1) * 128, 1024 + h * 128:1024 + (h + 1) * 128], ys_.t[:])],
                                   ys_.b, [ys_.b], [self.ysc_b[T]])
            for x in (gqb, gkvb):
                pr.release_dma(x.b)

    def build(self):
        nc = self.nc
        cfg = self.cfg
        self.declare()
        d = self.d
        with self.stack as st:
            pr = self.pr = Prog(nc, st)
            self.rotc = {}
            self.wout_done = {}
            self.evi = 0
            self.wslot_i = 0
            self.gS, self.gP = Group("S"), Group("P")
            self.xst_b = [pr.gbuf(f"xst{T}") for T in range(24)]
            self.ysc_b = [pr.gbuf(f"ysc{T}") for T in range(24)]
            self.sgsc_b = [pr.gbuf(f"sgsc{T}") for T in range(24)]
            self.modv_b = [pr.gbuf(f"modv{l}") for l in range(4)]
            self.out_b = pr.gbuf("outs")
            self.PS = []
            for i in range(8):
                t = st.enter_context(nc.psum_tensor(f"ps{i}", [128, 512], F32))
                pb_ = pr.gbuf(f"ps{i}")
                pb_.excl = True
                self.PS.append((t, pb_))
            with pr.scope() as g0:
                self.ident = g0.tile([128, 128], BF16, "ident")
                pr.dma("pool", [(self.ident.t[:], d["ident"][:, :])], self.ident.b, (), [self.ident.b])
                pr.release_dma(self.ident.b)
                self.cst = g0.tile([128, 4], F32, "cst")
                pr.memset("pool", self.cst.t[:, 0:1], -0.5, [self.cst.b])
                pr.memset("pool", self.cst.t[:, 1:2], float(D * EPS), [self.cst.b])
                pr.memset("pool", self.cst.t[:, 2:3], float(256 * EPS), [self.cst.b])
                pr.memset("pool", self.cst.t[:, 3:4], float(512 * EPS), [self.cst.b])
                self.W = [g0.tile([128, KC, 512], BF16, "W") for _ in range(3)]
                self.PT = [g0.tile([128, 512], BF16, "PT") for _ in range(4)]
                self.layers = cfg.get("layers", list(range(cfg.get("nlayers", DEPTH))))
                self.mod_prepare(g0)
                with pr.scope() as scm:
                    st0 = self.mod_stream(self.layers[0], scm)
                    for _ in range(12):
                        st0()
                self.ropeD = g0.tile([128, 16, 2, 2, 2, 32], F32, "ropeD")
                self.ropeM = g0.tile([128, 16, 2, 2, 2, 16], F32, "ropeM")
                pr.dma("sp", [(self.ropeD.t[:].rearrange("p a b c d e -> p (a b c d e)"), d["ropeD"][:, :])], self.ropeD.b, (), [self.ropeD.b])
                pr.dma("sp", [(self.ropeM.t[:].rearrange("p a b c d e -> p (a b c d e)"), d["ropeM"][:, :])], self.ropeM.b, (), [self.ropeM.b])
                pr.release_dma(self.ropeD.b); pr.release_dma(self.ropeM.b)
                self.hT = g0.tile([128, KC, 2048], BF16, "hT")
                self.hT_b = [g0.buf(f"hT{t}") for t in range(16)]
                self.ra = [g0.tile([128, 512], F32, "ra") for _ in range(1)]
                self.rb = [g0.tile([128, 512], F32, "rb") for _ in range(1)]
                self.layers = cfg.get("layers", list(range(cfg.get("nlayers", DEPTH))))
                for l in self.layers:
                    for g in ((self.gP,) if cfg.get("only_P") else (self.gS, self.gP) if not cfg.get("only_S") else (self.gS,)):
                        self.phase_A(l, g)
                        if l % 2 == 0:
                            self.phase_B_even(l, g)
                        else:
                            self.phase_B_odd(l, g)
                    self.phase_C(l)
                pr.wait_all_dma("sp")
        return nc


_CONST_CACHE = {}


def _consts():
    if not _CONST_CACHE:
        _CONST_CACHE["ident"] = np.eye(128, dtype=np.float32)
        _CONST_CACHE["ropeD"] = np.ascontiguousarray(rope_tables(32).reshape(128, -1))
        _CONST_CACHE["ropeM"] = np.ascontiguousarray(rope_tables(16).reshape(128, -1))
    return _CONST_CACHE


def make_in_maps(inputs, cores=range(NCORES)):
    f = lambda a: np.ascontiguousarray(np.asarray(a, dtype=np.float32))
    cst = _consts()
    natab = na_bias_tables(np.asarray(inputs["na_rpb"], np.float32))
    shared = {
        "w_ada": f(inputs["w_ada"]), "b_ada": f(inputs["b_ada"]), "g_pre": f(inputs["g_pre"]), "g_post": f(inputs["g_post"]),
        "w_in_even": f(inputs["w_in_even"]), "w_out_even": f(inputs["w_out_even"]),
        "mla_g_q": f(inputs["mla_g_q"]), "mla_w_uq": f(inputs["mla_w_uq"]), "mla_g_kv": f(inputs["mla_g_kv"]),
        "mla_w_ukv": f(inputs["mla_w_ukv"]), "w_in_odd": f(inputs["w_in_odd"]), "w_out_odd": f(inputs["w_out_odd"]),
        "diff_lambda": f(np.asarray(inputs["diff_lambda"]).reshape(2, 512)), "diff_g": f(inputs["diff_g"]),
        "ident": cst["ident"], "ropeD": cst["ropeD"], "ropeM": cst["ropeM"], "natab": natab,
    }
    xs = np.asarray(inputs["x_sample"], np.float32)
    xp = np.asarray(inputs["x_prompt"], np.float32)
    c = np.asarray(inputs["c"], np.float32)
    cctx = np.asarray(inputs["c_ctx"], np.float32)
    maps = []
    for b in cores:
        m = dict(shared)
        m["xs"] = f(xs[b])
        m["xp"] = f(xp[4 * b:4 * b + 4].reshape(1024, D))
        m["cnak"] = f(np.asarray(inputs["cache_na_k"])[b].reshape(2, 512, 1024))
        m["cnav"] = f(np.asarray(inputs["cache_na_v"])[b].reshape(2, 512, 1024))
        m["cckv"] = f(np.asarray(inputs["cache_mla_ckv"])[b])
        m["ckpe"] = f(np.asarray(inputs["cache_mla_kpe"])[b])
        m["cdk"] = f(np.asarray(inputs["cache_diff_k"])[b].reshape(2, 512, 2048))
        m["cdv"] = f(np.asarray(inputs["cache_diff_v"])[b].reshape(2, 512, 2048))
        cond = np.stack([c[b], cctx], axis=0)
        m["condT"] = f(cond.reshape(2, KC, 128).transpose(2, 1, 0).reshape(128, 32))
        maps.append(m)
    return maps


_NC_CACHE = {}


def get_program(cfg=None):
    key = repr(sorted((cfg or {}).items()))
    if key not in _NC_CACHE:
        _NC_CACHE[key] = Builder(cfg or {}).build()
    return _NC_CACHE[key]


def kernel(**inputs):
    nc = get_program()
    in_maps = make_in_maps(inputs)
    res = run_bass_kernel_spmd(nc, in_maps, core_ids=list(range(NCORES)))
    R = res.results
    cat = lambda k: np.stack([np.asarray(r[k]) for r in R], axis=0)
    y_p = cat("y_p").reshape(32, 256, D)
    y_s = cat("y_s").reshape(8, 2048, D)
    nak = cat("o_nak").reshape(32, 2, 256, 8, 128)
    nav = cat("o_nav").reshape(32, 2, 256, 8, 128)
    ckv = cat("o_ckv").reshape(32, 2, 256, 256)
    kpe = cat("o_kpe").reshape(32, 2, 256, 64)
    dk = cat("o_dk").reshape(32, 2, 256, 8, 256)
    dv = cat("o_dv").reshape(32, 2, 256, 8, 256)
    return tuple(np.ascontiguousarray(a, dtype=np.float32) for a in (y_p, y_s, nak, nav, ckv, kpe, dk, dv))
```

```python
import math
from contextlib import ExitStack

import numpy as np
import concourse.bass as bass
import concourse.mybir as mybir
from concourse.bass_utils import run_bass_kernel_spmd

F32 = mybir.dt.float32
BF16 = mybir.dt.bfloat16
AF = mybir.ActivationFunctionType
ALU = mybir.AluOpType

D = 2048
KC = 16
DEPTH = 4
EPS = 1e-6
NA_SCALE = 128 ** -0.5
MLA_SCALE = 192 ** -0.5
DIFF_SCALE = 128 ** -0.5
MASKV = -30000.0
NCORES = 8


class Buf:
    __slots__ = ("name", "w", "r", "init", "excl")

    def __init__(self, name, init):
        self.name = name
        self.excl = False
        self.w = None
        self.r = {}
        self.init = dict(init)


class Tile:
    __slots__ = ("t", "b")

    def __init__(self, t, b):
        self.t = t
        self.b = b


class Scope:
    def __init__(self, pr):
        self.pr = pr
        self.stack = ExitStack()
        self.bufs = []
        self.sems = []

    def __enter__(self):
        self.stack.__enter__()
        return self

    def tile(self, shape, dt, name=None):
        pr = self.pr
        pr.uid += 1
        nm = f"{name or 't'}_{pr.uid}"
        t = self.stack.enter_context(pr.nc.sbuf_tensor(nm, list(shape), dt))
        b = Buf(nm, pr.frontier)
        self.bufs.append(b)
        return Tile(t, b)

    def buf(self, name="b"):
        b = Buf(name, self.pr.frontier)
        self.bufs.append(b)
        return b

    def small(self):
        if not hasattr(self, "_sm"):
            self._sm = [self.tile([128, 2], F32, "sm") for _ in range(32)]
            self._smi = 0
        self._smi += 1
        return self._sm[self._smi % 32]

    def __exit__(self, *a):
        pr = self.pr
        for b in self.bufs:
            if b.w is not None:
                pr.front_merge(b.w[0], b.w[1])
            for k, v in b.r.items():
                pr.front_merge(k, v)
            pr.release_dma(b)
        return self.stack.__exit__(*a)


class Prog:
    def __init__(self, nc, stack, n_dma_sems=56):
        self.nc = nc
        self.uid = 0
        self.eng = {"pe": nc.tensor, "act": nc.scalar, "dve": nc.vector, "pool": nc.gpsimd, "sp": nc.sync}
        self.sems = {}
        for e in ("pe", "act", "dve", "pool"):
            self.sems[e] = stack.enter_context(nc.semaphore(f"sem_{e}"))
        self.cnt = {e: 0 for e in ("pe", "act", "dve", "pool")}
        self.known = {e: {} for e in self.eng}
        self.frontier = {}
        self.dma_cnt = {}
        self.free_dma = {"sp": [], "pool": []}
        for i in range(n_dma_sems):
            k = f"dma{i}"
            self.sems[k] = stack.enter_context(nc.semaphore(f"sem_{k}"))
            self.dma_cnt[k] = 0
            self.free_dma["pool" if i < 16 else "sp"].append(k)
        self.buf_dma = {}
        self.n_ins = 0

    def front_merge(self, k, v):
        if self.frontier.get(k, 0) < v:
            self.frontier[k] = v

    def scope(self):
        return Scope(self)

    def gbuf(self, name):
        return Buf(name, {})

    def dma_key(self, b, q):
        k = self.buf_dma.get((id(b), q))
        if k is None:
            k = self.free_dma[q].pop()
            self.buf_dma[(id(b), q)] = k
        return k

    def release_dma(self, b):
        for q in ("sp", "pool"):
            k = self.buf_dma.pop((id(b), q), None)
            if k is not None:
                self.free_dma[q].insert(0, k)

    def _waits(self, eng, own, reads, writes):
        w = {}

        def add(k, v, raw):
            if k == own and eng == "pe":
                return
            if w.get(k, 0) < v:
                w[k] = v

        for b in reads:
            for k, v in b.init.items():
                add(k, v, True)
            if b.w is not None:
                add(b.w[0], b.w[1], True)
            if b.excl:
                for k, v in b.r.items():
                    if k != own:
                        add(k, v, False)
        for b in writes:
            for k, v in b.init.items():
                add(k, v, True)
            if b.w is not None:
                add(b.w[0], b.w[1], False)
            for k, v in b.r.items():
                add(k, v, False)
        return w

    def _emit_waits(self, eng, w):
        e = self.eng[eng]
        kn = self.known[eng]
        for k, v in w.items():
            if kn.get(k, 0) < v:
                e.wait_ge(self.sems[k], v)
                kn[k] = v
                self.n_ins += 1

    def _record(self, ev, reads, writes):
        k, v = ev
        for b in reads:
            if b.r.get(k, 0) < v:
                b.r[k] = v
        for b in writes:
            b.w = ev
            b.r = {}

    def op(self, eng, fn, reads=(), writes=()):
        w = self._waits(eng, eng, reads, writes)
        self._emit_waits(eng, w)
        ins = fn(self.eng[eng])
        self.cnt[eng] += 1
        ins.then_inc(self.sems[eng], 1)
        self.n_ins += 1
        self._record((eng, self.cnt[eng]), reads, writes)

    def dma(self, q, pairs, slot, reads=(), writes=()):
        k = self.dma_key(slot, q)
        w = self._waits(q, None, reads, writes)
        if self.dma_cnt[k] > 0:
            w[k] = max(w.get(k, 0), self.dma_cnt[k])
        self._emit_waits(q, w)
        e = self.eng[q]
        for (o, i) in pairs:
            e.dma_start(out=o, in_=i).then_inc(self.sems[k], 16)
            self.dma_cnt[k] += 16
            self.n_ins += 1
        self._record((k, self.dma_cnt[k]), reads, writes)

    def wait_all_dma(self, eng):
        w = {k: v for k, v in self.dma_cnt.items() if v > 0}
        self._emit_waits(eng, w)

    def mm(self, out, pairs, reads, writes):
        n = len(pairs)

        def fn(e):
            ins = None
            for j, (l, r) in enumerate(pairs):
                ins = e.matmul(out, lhsT=l, rhs=r, start=(j == 0), stop=(j == n - 1))
            return ins
        self.n_ins += n - 1
        self.op("pe", fn, reads, writes)

    def transposes(self, items, ident, reads, writes):
        def fn(e):
            ins = None
            for (o, i) in items:
                ins = e.transpose(o, i, ident)
            return ins
        self.n_ins += len(items) - 1
        self.op("pe", fn, reads, writes)

    def act(self, out, in_, func, reads, writes, **kw):
        self.op("act", lambda e: e.activation(out=out, in_=in_, func=func, **kw), reads, writes)

    def copy(self, eng, out, in_, reads, writes):
        if eng == "act":
            self.op("act", lambda e: e.activation(out=out, in_=in_, func=AF.Copy), reads, writes)
        else:
            self.op(eng, lambda e: e.tensor_copy(out=out, in_=in_), reads, writes)

    def tt(self, eng, out, in0, in1, op, reads, writes):
        self.op(eng, lambda e: e.tensor_tensor(out=out, in0=in0, in1=in1, op=op), reads, writes)

    def ts(self, eng, out, in0, s1, s2, op0, op1, reads, writes):
        if s2 is None:
            self.op(eng, lambda e: e.tensor_single_scalar(out=out, in_=in0, scalar=s1, op=op0), reads, writes)
        else:
            self.op(eng, lambda e: e.tensor_scalar(out=out, in0=in0, scalar1=s1, scalar2=s2, op0=op0, op1=op1),
                    reads, writes)

    def stt(self, eng, out, in0, scalar, in1, op0, op1, reads, writes, accum_out=None):
        if accum_out is None:
            self.op(eng, lambda e: e.scalar_tensor_tensor(out=out, in0=in0, scalar=scalar, in1=in1, op0=op0, op1=op1),
                    reads, writes)
        else:
            self.op(eng, lambda e: e.scalar_tensor_tensor(out=out, in0=in0, scalar=scalar, in1=in1, op0=op0, op1=op1,
                                                          accum_out=accum_out), reads, writes)

    def memset(self, eng, ap, val, writes):
        self.op(eng, lambda e: e.memset(ap, val), (), writes)


def rope_tables(hw):
    inv = (10000.0 ** (-np.arange(hw, dtype=np.float32) / np.float32(hw))).astype(np.float32)
    t = np.arange(2048)
    rows = (t // 64).astype(np.float32)
    cols = (t % 64).astype(np.float32)
    out = np.zeros((128, 16, 2, 2, 2, hw), np.float32)
    for ax, pos in enumerate((rows, cols)):
        ang = (pos[:, None] * inv[None, :]).astype(np.float32)
        c = np.cos(ang).astype(np.float32).reshape(16, 128, hw).transpose(1, 0, 2)
        s = np.sin(ang).astype(np.float32).reshape(16, 128, hw).transpose(1, 0, 2)
        out[:, :, 0, ax, 0, :] = c
        out[:, :, 0, ax, 1, :] = c
        out[:, :, 1, ax, 0, :] = -s
        out[:, :, 1, ax, 1, :] = s
    return out


def na_bias_tables(rpb):
    L, H = rpb.shape[0], rpb.shape[1]
    kc = np.arange(64)[:, None]
    c = np.arange(64)[None, :]
    cs = np.clip(c - 8, 0, 48)
    colok = (kc >= cs) & (kc < cs + 16)
    dcidx = np.clip(kc - c + 15, 0, 30)
    out = np.full((L, H, 2, 128, 22, 64), MASKV, np.float32)
    for a in range(2):
        for t in range(22):
            dr = 10 + a - t
            if abs(dr) > 7:
                continue
            vals = rpb[:, :, dr + 7, :][:, :, dcidx]
            blk = np.where(colok[None, None], vals, np.float32(MASKV)).astype(np.float32)
            out[:, :, 1, a * 64:(a + 1) * 64, t, :] = blk
            if -4 <= dr <= 3:
                out[:, :, 0, a * 64:(a + 1) * 64, t, :] = blk
    return out.reshape(L, H, 2, 128, 22 * 64)


def na_window_tiles(b):
    rng = {0: range(0, 6), 1: range(2, 10), 2: range(6, 14), 3: range(10, 16)}[b]
    res = []
    for j in rng:
        full = (b == 0 and j in (2, 3)) or (b == 3 and j in (12, 13))
        res.append((j, 1 if full else 0, 10 - (2 * j - 8 * b)))
    return res


class Group:
    def __init__(self, name):
        self.name = name
        if name == "S":
            self.nt = 16
            self.tile0 = 0
            self.klen = 2560
            self.rope = True
            self.ctx = True
            self.qblocks = [(qb * 512, 512, list(range(16)) + [16, 17, 18, 19]) for qb in range(4)]
            self.mod_row = 0
        else:
            self.nt = 8
            self.tile0 = 16
            self.klen = 1024
            self.rope = False
            self.ctx = False
            self.qblocks = [(j * 256, 256, [2 * j, 2 * j + 1]) for j in range(4)]
            self.mod_row = 1
        self.ntok = self.nt * 128
        self.nkt = self.klen // 128


class Builder:
    def __init__(self, cfg):
        self.cfg = cfg
        self.nc = bass.Bass("TRN2", target_bir_lowering=False)
        self.stack = ExitStack()

    def declare(self):
        nc = self.nc
        I = lambda n, s: nc.dram_tensor(n, list(s), F32, kind="ExternalInput").ap()
        O = lambda n, s: nc.dram_tensor(n, list(s), F32, kind="ExternalOutput").ap()
        d = {}
        d["xs"] = I("xs", (2048, D)); d["xp"] = I("xp", (1024, D))
        d["cnak"] = I("cnak", (2, 512, 1024)); d["cnav"] = I("cnav", (2, 512, 1024))
        d["cckv"] = I("cckv", (2, 512, 256)); d["ckpe"] = I("ckpe", (2, 512, 64))
        d["cdk"] = I("cdk", (2, 512, 2048)); d["cdv"] = I("cdv", (2, 512, 2048))
        d["condT"] = I("condT", (128, 32))
        d["w_ada"] = I("w_ada", (4, D, 3 * D)); d["b_ada"] = I("b_ada", (4, 3 * D))
        d["g_pre"] = I("g_pre", (4, D)); d["g_post"] = I("g_post", (4, D))
        d["w_in_even"] = I("w_in_even", (2, D, 5952)); d["w_out_even"] = I("w_out_even", (2, D, D))
        d["mla_g_q"] = I("mla_g_q", (2, 512)); d["mla_w_uq"] = I("mla_w_uq", (2, 512, 1536))
        d["mla_g_kv"] = I("mla_g_kv", (2, 256)); d["mla_w_ukv"] = I("mla_w_ukv", (2, 256, 2048))
        d["w_in_odd"] = I("w_in_odd", (2, D, 8192)); d["w_out_odd"] = I("w_out_odd", (2, D, D))
        d["diff_lambda"] = I("diff_lambda", (2, 512)); d["diff_g"] = I("diff_g", (2, 256))
        d["ident"] = I("ident", (128, 128))
        d["ropeD"] = I("ropeD", (128, 16 * 2 * 2 * 2 * 32)); d["ropeM"] = I("ropeM", (128, 16 * 2 * 2 * 2 * 16))
        d["natab"] = I("natab", (2, 8, 2, 128, 1408))
        d["y_p"] = O("y_p", (1024, D)); d["y_s"] = O("y_s", (2048, D))
        d["o_nak"] = O("o_nak", (4, 2, 256, 1024)); d["o_nav"] = O("o_nav", (4, 2, 256, 1024))
        d["o_ckv"] = O("o_ckv", (4, 2, 256, 256)); d["o_kpe"] = O("o_kpe", (4, 2, 256, 64))
        d["o_dk"] = O("o_dk", (4, 2, 256, 2048)); d["o_dv"] = O("o_dv", (4, 2, 256, 2048))
        d["xst"] = nc.dram_tensor("xst", [3072, D], F32, kind="Internal").ap()
        d["ysc"] = nc.dram_tensor("ysc", [3072, D], BF16, kind="Internal").ap()
        d["modv"] = nc.dram_tensor("modv", [4, 2, 3, D], F32, kind="Internal").ap()
        d["sgsc"] = nc.dram_tensor("sgsc", [3072, 1024], BF16, kind="Internal").ap()
        self.d = d

    def bcast_rows(self, ap_row, nparts=128):
        n = ap_row.shape[-1]
        return bass.AP(tensor=ap_row.tensor, offset=ap_row.offset, ap=[[0, nparts], [1, n]])

    def wview(self, w2d, c0, ncols, kchunks):
        return w2d[:, c0:c0 + ncols].rearrange("(kc p) n -> p kc n", p=128)

    def ps(self, i, n=512, dt=F32):
        t, b = self.PS[i]
        if dt == F32:
            return t[:, 0:n]
        return t[:].bitcast(BF16)[:, 0:n]

    def evac_eng(self):
        self.evi += 1
        return "act" if (self.evi % 2) else "dve"

    def rot(self, name, n):
        v = self.rotc.get(name, 0)
        self.rotc[name] = v + 1
        return v % n

    def psum_gen(self):
        return self.rot("psg", 4)

    def rsqrt(self, sc, out_t, ssq_t, epsn):
        pr = self.pr
        tmp = self.small(sc)
        pr.ts("dve", tmp.t[:, 0:1], ssq_t.t[:, 0:1], float(epsn), None, ALU.add, None, [ssq_t.b], [tmp.b])
        pr.tt("pool", out_t.t[:, 0:1], tmp.t[:, 0:1], self.cst.t[:, 0:1], ALU.pow, [tmp.b, self.cst.b], [out_t.b])

    def small(self, sc):
        return sc.small()

    @staticmethod
    def pipelined(n, first, second):
        first(0)
        for t in range(n):
            if t + 1 < n:
                first(t + 1)
            second(t)

    def w_issue(self, req):
        pr = self.pr
        s = self.wslot_i % len(self.W)
        self.wslot_i += 1
        slot = self.W[s]
        pairs = []
        for (c0, src, kch) in req:
            n = src.shape[-1]
            pairs.append((slot.t[:, 0:kch, c0:c0 + n], src.rearrange("(kc p) n -> p kc n", p=128)))
        pr.dma("pool", pairs, slot.b, reads=(), writes=[slot.b])
        return slot

    def stream(self, reqs, consume):
        nW = len(self.W)
        slots = {}
        for j in range(min(nW - 1, len(reqs))):
            slots[j] = self.w_issue(reqs[j])
        for i in range(len(reqs)):
            j = i + nW - 1
            if j < len(reqs):
                slots[j] = self.w_issue(reqs[j])
            consume(i, slots.pop(i))

    def rope(self, sc, src_ap, src_buf, dst_ap, dst_buf, nb, hw, tab, tile_idx, dup_ap=None):
        pr = self.pr
        n = nb * 4 * hw
        ra = self.ra[self.rot("ra", len(self.ra))]
        rb = self.rb[self.rot("rb", len(self.rb))]
        v5 = lambda ap: ap.rearrange("p (n a h w) -> p n a h w", n=nb, a=2, h=2, w=hw)
        v3 = lambda ap: ap.rearrange("p (n f) -> p n f", n=nb)
        tC = tab.t[:, tile_idx, 0, :, :, :].rearrange("p a h w -> p (a h w)").unsqueeze(1).broadcast_to([128, nb, 4 * hw])
        tS = lambda h: tab.t[:, tile_idx, 1, :, h, :].unsqueeze(1).broadcast_to([128, nb, 2, hw])
        x5 = v5(src_ap)
        pr.tt("dve", v3(ra.t[:, 0:n]), v3(src_ap), tC, ALU.mult, [src_buf, tab.b], [ra.b])
        rb5 = v5(rb.t[:, 0:n])
        pr.tt("dve", rb5[:, :, :, 0, :], x5[:, :, :, 1, :], tS(0), ALU.mult, [src_buf, tab.b], [rb.b])
        pr.tt("dve", rb5[:, :, :, 1, :], x5[:, :, :, 0, :], tS(1), ALU.mult, [src_buf, tab.b], [rb.b])
        pr.tt("pool", dst_ap, ra.t[:, 0:n], rb.t[:, 0:n], ALU.add, [ra.b, rb.b], [dst_buf])
        if dup_ap is not None:
            pr.tt("pool", dup_ap, ra.t[:, 0:n], rb.t[:, 0:n], ALU.add, [ra.b, rb.b], [dst_buf])

    def attention(self, sc, qn, ktiles, st_parts, v_ap, v_reads, dvp1, scale, bias=None, o_base=4):
        return self.attention_multi([dict(qn=qn, ktiles=ktiles, st_parts=st_parts, v_ap=v_ap, v_reads=v_reads, bias=bias)],
                                    dvp1, scale, [o_base])[0]

    def attention_multi(self, jobs, dvp1, scale, o_bases):
        pr = self.pr
        J = []
        for jb, ob0 in zip(jobs, o_bases):
            nsub = jb["qn"] // 128
            J.append(dict(jb, nsub=nsub, nk=len(jb["ktiles"]), obanks=[self.PS[ob0 + s_] for s_ in range(nsub)], bis={}))

        def issue_st(j, idx, extra_reads=()):
            kt = j["ktiles"][idx]
            bi = self.psum_gen()
            pt_, pb = self.PS[bi]
            pairs, rds = j["st_parts"](kt)
            pr.mm(pt_[:, 0:j["qn"]], pairs, list(rds) + list(extra_reads), [pb])
            j["bis"][idx] = bi

        def issue_exp(j, idx):
            kt = j["ktiles"][idx]
            qn = j["qn"]
            pt_, pb = self.PS[j["bis"].pop(idx)]
            ptile = self.PT[self.rot("pt", len(self.PT))]
            bsl = j["bias"](kt) if j.get("bias") is not None else None
            if bsl is None:
                pr.act(ptile.t[:, 0:qn], pt_[:, 0:qn], AF.Exp, [pb], [ptile.b], scale=scale)
            else:
                bap, bbuf = bsl
                tmp = self.TMP[self.rot("tmp", len(self.TMP))]
                pr.stt("dve", tmp.t[:, 0:qn], pt_[:, 0:qn], scale, bap, ALU.mult, ALU.add, [pb, bbuf], [tmp.b])
                pr.act(ptile.t[:, 0:qn], tmp.t[:, 0:qn], AF.Exp, [tmp.b], [ptile.b])
            return ptile

        def issue_pv(j, idx, ptile):
            kt = j["ktiles"][idx]
            va = j["v_ap"](kt)
            obanks, nsub, nk = j["obanks"], j["nsub"], j["nk"]

            def fn(e):
                ins = None
                for s_ in range(nsub):
                    ins = e.matmul(obanks[s_][0][:, 0:dvp1], lhsT=ptile.t[:, s_ * 128:(s_ + 1) * 128], rhs=va,
                                   start=(idx == 0), stop=(idx == nk - 1))
                return ins
            pr.n_ins += nsub - 1
            pr.op("pe", fn, [ptile.b] + list(j["v_reads"]), [ob[1] for ob in obanks])

        nkmax = max(j["nk"] for j in J)
        if len(J) == 1:
            j = J[0]
            nk = j["nk"]
            issue_st(j, 0)
            if nk > 1:
                issue_st(j, 1)
            for idx in range(nk):
                ptile = issue_exp(j, idx)
                if idx + 2 < nk:
                    issue_st(j, idx + 2, extra_reads=[ptile.b])
                issue_pv(j, idx, ptile)
            return [j["obanks"]]
        for j in J:
            issue_st(j, 0)
        for idx in range(nkmax):
            for j in J:
                if idx + 1 < j["nk"]:
                    issue_st(j, idx + 1)
            pts = []
            for j in J:
                pts.append(issue_exp(j, idx) if idx < j["nk"] else None)
            for j, ptile in zip(J, pts):
                if ptile is not None:
                    issue_pv(j, idx, ptile)
        return [j["obanks"] for j in J]

    def mod_prepare(self, g0):
        pr, d = self.pr, self.d
        self.cb = g0.tile([128, 32], BF16, "cTb")
        with pr.scope() as sc:
            cT = sc.tile([128, 32], F32, "cT")
            th = sc.tile([128, 32], F32, "cth")
            pr.dma("sp", [(cT.t[:], d["condT"][:, :])], cT.b, (), [cT.b])
            pr.act(th.t[:], cT.t[:], AF.Tanh, [cT.b], [th.b], scale=0.5)
            pr.stt("dve", th.t[:], th.t[:], 1.0, cT.t[:], ALU.add, ALU.mult, [th.b, cT.b], [th.b])
            pr.ts("dve", self.cb.t[:], th.t[:], 0.5, None, ALU.mult, None, [th.b], [self.cb.b])

    def mod_stream(self, l, sc):
        pr, d = self.pr, self.d
        cb3 = self.cb.t[:].rearrange("p (k r) -> p k r", r=2)
        bt = sc.tile([2, 512], F32, "mbt")
        gt = sc.tile([2, 512], F32, "mgt")
        vt = sc.tile([2, 512], F32, "mvt")
        sq = float(math.sqrt(D))
        reqs = [[(0, d["w_ada"][l][:, c * 512:(c + 1) * 512], KC)] for c in range(12)]
        nW = len(self.W)
        slots = {}
        for j in range(nW - 1):
            slots[j] = self.w_issue(reqs[j])
        state = {"i": 0}

        def step():
            c = state["i"]
            if c >= 12:
                return
            state["i"] += 1
            j = c + nW - 1
            if j < 12:
                slots[j] = self.w_issue(reqs[j])
            slot = slots.pop(c)
            which, jb = c // 4, c % 4
            cs = slice(jb * 512, (jb + 1) * 512)
            pr.dma("sp", [(bt.t[:], self.bcast_rows(d["b_ada"][l, c * 512:(c + 1) * 512], 2))], bt.b, (), [bt.b])
            if which == 1:
                pr.dma("sp", [(gt.t[:], self.bcast_rows(d["g_pre"][l, cs], 2))], gt.b, (), [gt.b])
            elif which == 2:
                pr.dma("sp", [(gt.t[:], self.bcast_rows(d["g_post"][l, cs], 2))], gt.b, (), [gt.b])
            bi = self.psum_gen()
            pt_, pb = self.PS[bi]
            pr.mm(pt_[0:2, 0:512], [(cb3[:, kc, :], slot.t[:, kc, 0:512]) for kc in range(KC)], [self.cb.b, slot.b], [pb])
            pr.tt("dve", vt.t[:], pt_[0:2, 0:512], bt.t[:], ALU.add, [pb, bt.b], [vt.b])
            if which == 0:
                w_ = 1
            elif which == 1:
                pr.stt("dve", vt.t[:], vt.t[:], 1.0, gt.t[:], ALU.add, ALU.mult, [vt.b, gt.b], [vt.b])
                pr.ts("dve", vt.t[:], vt.t[:], sq, None, ALU.mult, None, [vt.b], [vt.b])
                w_ = 0
            else:
                pr.stt("dve", vt.t[:], vt.t[:], sq, gt.t[:], ALU.mult, ALU.mult, [vt.b, gt.b], [vt.b])
                w_ = 2
            pr.dma("sp", [(d["modv"][l, :, w_, cs], vt.t[:])], vt.b, [vt.b], [self.modv_b[l]])
        return step

    def load_mod(self, sc, l, g, which, name):
        pr, d = self.pr, self.d
        t = sc.tile([128, D], F32, name)
        pr.dma("sp", [(t.t[:], self.bcast_rows(d["modv"][l, g.mod_row, which, :]))], t.b, [self.modv_b[l]], [t.b])
        return t

    def phase_A(self, l, g):
        pr, d = self.pr, self.d
        with pr.scope() as sc:
            mA = self.load_mod(sc, l, g, 0, "mA")
            mB = self.load_mod(sc, l, g, 1, "mB")
            xts = [sc.tile([128, D], F32, "xt") for _ in range(3)]
            junk = sc.tile([128, D], BF16, "junk")
            t1s = [sc.tile([128, D], F32, "t1") for _ in range(1)]
            hbs = [sc.tile([128, D], BF16, "hb") for _ in range(2)]
            rst = {}
            ssqs = {}

            def stage1(t):
                T = g.tile0 + t
                xt = xts[t % 3]
                if l == self.layers[0]:
                    src = d["xs"][t * 128:(t + 1) * 128, :] if g.name == "S" else d["xp"][t * 128:(t + 1) * 128, :]
                    pr.dma("sp", [(xt.t[:], src)], xt.b, (), [xt.b])
                else:
                    pr.dma("sp", [(xt.t[:], d["xst"][T * 128:(T + 1) * 128, :])], xt.b, [self.xst_b[T]], [xt.b])
                ssq = self.small(sc)
                pr.act(junk.t[:], xt.t[:], AF.Square, [xt.b], [junk.b, ssq.b], accum_out=ssq.t[:, 0:1])
                ssqs[t] = ssq

            def stage1b(t):
                rstd = self.small(sc)
                self.rsqrt(sc, rstd, ssqs.pop(t), D * EPS)
                rst[t] = rstd

            def stage2(t):
                xt = xts[t % 3]
                rstd = rst.pop(t)
                t1 = t1s[0]
                hb = hbs[t % 2]
                pr.stt("dve", t1.t[:], xt.t[:], rstd.t[:, 0:1], mA.t[:], ALU.mult, ALU.mult,
                       [xt.b, rstd.b, mA.b], [t1.b])
                pr.tt("dve", hb.t[:], t1.t[:], mB.t[:], ALU.add, [t1.b, mB.b], [hb.b])
                for half in range(2):
                    bi = self.psum_gen()
                    pt_, pb = self.PS[bi]
                    pbf = self.ps(bi, 1024, BF16)
                    pr.transposes([(pbf[:, j * 128:(j + 1) * 128], hb.t[:, (half * 8 + j) * 128:(half * 8 + j + 1) * 128])
                                   for j in range(8)], self.ident.t[:], [hb.b, self.ident.b], [pb])
                    pr.copy("act", self.hT.t[:, half * 8:(half + 1) * 8, t * 128:(t + 1) * 128],
                            pbf.rearrange("p (a b) -> p a b", a=8), [pb], [self.hT_b[t]])

            stage1(0)
            stage1b(0)
            stage1(1)
            stage1b(1)
            for t in range(g.nt):
                if t + 2 < g.nt:
                    stage1(t + 2)
                stage2(t)
                if t + 2 < g.nt:
                    stage1b(t + 2)

    def issue_wout(self, l):
        if self.wout_done.get(l):
            return
        self.wout_done[l] = True
        pr, d = self.pr, self.d
        wout = d["w_out_even"][l // 2] if l % 2 == 0 else d["w_out_odd"][l // 2]
        for c in range(4):
            pr.dma("pool", [(self.hT.t[:, :, c * 512:(c + 1) * 512],
                             wout[:, c * 512:(c + 1) * 512].rearrange("(kc p) n -> p kc n", p=128))],
                   self.hT_b[4 * c], (), self.hT_b[4 * c:4 * c + 4])

    def phase_C(self, l):
        pr, d = self.pr, self.d
        last = (l == self.layers[-1])
        wout = d["w_out_even"][l // 2] if l % 2 == 0 else d["w_out_odd"][l // 2]
        with pr.scope() as sc:
            self.issue_wout(l)
            Gt = sc.tile([128, D], F32, "mG")
            ysl = [sc.tile([128, D], BF16, "ysl") for _ in range(1)]
            xts = [sc.tile([128, D], F32, "xt") for _ in range(2)]
            yTs = [sc.tile([128, KC, 128], BF16, "yTt") for _ in range(2)]
            us = [sc.tile([128, D], F32, "u") for _ in range(2)]
            junk = sc.tile([128, 512], BF16, "junk")
            ssq4s = [sc.tile([128, 4], F32, "ssq4") for _ in range(2)]

            def prep(T):
                g = self.gS if T < 16 else self.gP
                t = T - g.tile0
                yl = ysl[0]
                xt = xts[T % 2]
                yT = yTs[T % 2]
                pr.dma("sp", [(yl.t[:], d["ysc"][T * 128:(T + 1) * 128, :])], yl.b, [self.ysc_b[T]], [yl.b])
                if l == self.layers[0]:
                    src = d["xs"][t * 128:(t + 1) * 128, :] if g.name == "S" else d["xp"][t * 128:(t + 1) * 128, :]
                    pr.dma("sp", [(xt.t[:], src)], xt.b, (), [xt.b])
                else:
                    pr.dma("sp", [(xt.t[:], d["xst"][T * 128:(T + 1) * 128, :])], xt.b, [self.xst_b[T]], [xt.b])
                for half in range(2):
                    bi = self.psum_gen()
                    pt_, pb = self.PS[bi]
                    pbf = self.ps(bi, 1024, BF16)
                    pr.transposes([(pbf[:, j * 128:(j + 1) * 128], yl.t[:, (half * 8 + j) * 128:(half * 8 + j + 1) * 128])
                                   for j in range(8)], self.ident.t[:], [yl.b, self.ident.b], [pb])
                    pr.copy("act", yT.t[:, half * 8:(half + 1) * 8, :], pbf.rearrange("p (a b) -> p a b", a=8),
                            [pb], [yT.b])

            mstep = None
            li = self.layers.index(l)
            if li + 1 < len(self.layers):
                mstep = self.mod_stream(self.layers[li + 1], sc)
            prep(0)
            for T in range(24):
                g = self.gS if T < 16 else self.gP
                t = T - g.tile0
                xt = xts[T % 2]
                yT = yTs[T % 2]
                u = us[T % 2]
                if mstep is not None and T % 2 == 1:
                    mstep()
                if t == 0:
                    pr.dma("sp", [(Gt.t[:], self.bcast_rows(d["modv"][l, g.mod_row, 2, :]))], Gt.b, [self.modv_b[l]], [Gt.b])
                ssq4 = ssq4s[T % 2]
                for c in range(4):
                    ot, ob = self.PS[4 + c]
                    pr.mm(ot[:, 0:512], [(yT.t[:, kc, :], self.hT.t[:, kc, c * 512:(c + 1) * 512]) for kc in range(KC)],
                          [yT.b] + self.hT_b[4 * c:4 * c + 4], [ob])
                    pr.act(junk.t[:, 0:512], ot[:, 0:512], AF.Square, [ob], [junk.b, ssq4.b],
                           accum_out=ssq4.t[:, c:c + 1])
                if T + 1 < 24:
                    prep(T + 1)
                ssq = self.small(sc)
                rstd = self.small(sc)
                pr.op("dve", lambda e, ssq=ssq, ssq4=ssq4: e.reduce_sum(out=ssq.t[:, 0:1], in_=ssq4.t[:, 0:4],
                                                                        axis=mybir.AxisListType.X), [ssq4.b], [ssq.b])
                self.rsqrt(sc, rstd, ssq, D * EPS)
                G = Gt
                for c in range(4):
                    ot, ob = self.PS[4 + c]
                    sl = slice(c * 512, (c + 1) * 512)
                    pr.stt("dve", u.t[:, sl], ot[:, 0:512], rstd.t[:, 0:1], G.t[:, sl], ALU.mult, ALU.mult,
                           [ob, rstd.b, G.b], [u.b])
                pr.tt("pool", u.t[:], u.t[:], xt.t[:], ALU.add, [u.b, xt.b], [u.b])
                if last:
                    dst = d["y_s"][t * 128:(t + 1) * 128, :] if g.name == "S" else d["y_p"][t * 128:(t + 1) * 128, :]
                    pr.dma("sp", [(dst, u.t[:])], u.b, [u.b], ())
                else:
                    pr.dma("sp", [(d["xst"][T * 128:(T + 1) * 128, :], u.t[:])], u.b, [u.b], [self.xst_b[T]])

    def phase_B_odd(self, l, g):
        pr, d = self.pr, self.d
        i = l // 2
        lam_init = 0.8 - 0.6 * math.exp(-0.3 * l)
        w = d["w_in_odd"][i]
        if self.cfg.get("dbg", 99) < 1:
            return
        with pr.scope() as sc:
            s2 = sc.tile([128, 2], F32, "s2")
            with pr.scope() as sl:
                lp = sl.tile([128, 4, 128], F32, "lp")
                pr.dma("sp", [(lp.t[:].rearrange("p a b -> p (a b)"), self.bcast_rows(d["diff_lambda"][i, :]))], lp.b, (), [lp.b])
                prod = sl.tile([128, 2, 128], F32, "prod")
                lp4 = lp.t[:].rearrange("p (a two) b -> p a two b", two=2)
                pr.tt("dve", prod.t[:], lp4[:, :, 0, :], lp4[:, :, 1, :], ALU.mult, [lp.b], [prod.b])
                pr.op("dve", lambda e: e.reduce_sum(out=s2.t[:], in_=prod.t[:], axis=mybir.AxisListType.X), [prod.b], [s2.b])
            e2 = sc.tile([128, 2], F32, "e2")
            pr.act(e2.t[:], s2.t[:], AF.Exp, [s2.b], [e2.b])
            nlam = sc.tile([128, 2], F32, "nlam")
            pr.tt("dve", nlam.t[:, 0:1], e2.t[:, 1:2], e2.t[:, 0:1], ALU.subtract, [e2.b], [nlam.b])
            pr.ts("dve", nlam.t[:, 1:2], nlam.t[:, 0:1], -lam_init, None, ALU.add, None, [nlam.b], [nlam.b])
            gsub = sc.tile([128, 256], F32, "gsub")
            pr.dma("sp", [(gsub.t[:], self.bcast_rows(d["diff_g"][i, :]))], gsub.b, (), [gsub.b])
            pr.ts("dve", gsub.t[:], gsub.t[:], (1.0 - lam_init) * 0.5 * 16.0, None, ALU.mult, None, [gsub.b], [gsub.b])
            QKT = sc.tile([128, 4, g.klen], BF16, "QKT")
            QKT_q = sc.buf("QKT_q"); QKT_k = sc.buf("QKT_k")
            VA = sc.tile([128, g.nkt, 264], BF16, "VA")
            SG = sc.tile([128, g.nt, 256], BF16, "SG")
            pr.memset("dve", VA.t[:, :, 256:257], 1.0, [VA.b])
            qkb = [sc.tile([128, 512], BF16, "qkb") for _ in range(2)]
            thb = [sc.tile([128, 256], F32, "thb") for _ in range(1)]
            oraw = [sc.tile([128, 4, 264], F32, "oraw") for _ in range(2)]
            orb = [[sc.buf(f"orb{s_}{j}") for j in range(4)] for s_ in range(2)]
            yst = [sc.tile([128, 256], BF16, "yst") for _ in range(3)]
            junks = [sc.tile([128, 256], BF16, "junk") for _ in range(2)]
            kvst = [sc.tile([128, 512], F32, "kvst") for _ in range(2)] if not g.ctx else []
            kctx = sc.tile([128, 4, 256], BF16, "kctx") if g.ctx else None

            reqs = []
            for h in range(8):
                reqs.append([(0, w[:, h * 256:(h + 1) * 256], KC), (256, w[:, 2048 + h * 256:2048 + (h + 1) * 256], KC)])
                reqs.append([(0, w[:, 4096 + h * 256:4096 + (h + 1) * 256], KC),
                             (256, w[:, 6144 + h * 256:6144 + (h + 1) * 256], KC)])

            dbg = self.cfg.get("dbg", 99)

            def consume(ri, slot):
                h, kind = ri // 2, ri % 2
                if dbg < 2 or (dbg < 3 and kind == 1) or dbg == 5:
                    return
                if kind == 0:
                    if g.ctx and not self.cfg.get("no_ctx"):
                        pr.dma("pool", [(kctx.t[:], d["cdk"][i][:, h * 256:(h + 1) * 256].rearrange("(c p) n -> p c n", p=128))],
                               kctx.b, (), [kctx.b])
                        pr.dma("pool", [(VA.t[:, 16:20, 0:256],
                                         d["cdv"][i][:, h * 256:(h + 1) * 256].rearrange("(c p) n -> p c n", p=128))],
                               VA.b, (), [VA.b])
                    def first(t):
                        bi = self.psum_gen()
                        pt_, pb = self.PS[bi]
                        pr.mm(pt_[:, 0:512], [(self.hT.t[:, kc, t * 128:(t + 1) * 128], slot.t[:, kc, 0:512]) for kc in range(KC)],
                              [self.hT_b[t], slot.b], [pb])
                        qb_ = qkb[t % 2]
                        if g.rope:
                            self.rope(sc, pt_[:, 0:512], pb, qb_.t[:], qb_.b, 4, 32, self.ropeD, t)
                        else:
                            pr.copy("act", qb_.t[:], pt_[:, 0:512], [pb], [qb_.b])
                            ks = kvst[self.rot("kvst", 2)]
                            pr.copy("act", ks.t[:, 0:256], pt_[:, 256:512], [pb], [ks.b])
                            sq_, tt_ = t // 2, (t % 2) * 128
                            pr.dma("sp", [(d["o_dk"][sq_, i, tt_:tt_ + 128, h * 256:(h + 1) * 256], ks.t[:, 0:256])],
                                   ks.b, [ks.b], ())

                    def second(t):
                        qb_ = qkb[t % 2]
                        bj = self.psum_gen()
                        pj, pjb = self.PS[bj]
                        pbf = self.ps(bj, 512, BF16)
                        pr.transposes([(pbf[:, j * 128:(j + 1) * 128], qb_.t[:, j * 128:(j + 1) * 128]) for j in range(4)],
                                      self.ident.t[:], [qb_.b, self.ident.b], [pjb])
                        pr.copy("act", QKT.t[:, :, t * 128:(t + 1) * 128], pbf.rearrange("p (a b) -> p a b", a=4),
                                [pjb], [QKT_q, QKT_k])
                    self.pipelined(g.nt, first, second)
                    if g.ctx and not self.cfg.get("no_ctx"):
                        bj = self.psum_gen()
                        pj, pjb = self.PS[bj]
                        pbf = self.ps(bj, 1024, BF16)
                        pr.transposes([(pbf[:, (s * 4 + c) * 128:(s * 4 + c + 1) * 128], kctx.t[:, c, s * 128:(s + 1) * 128])
                                       for s in range(2) for c in range(4)], self.ident.t[:], [kctx.b, self.ident.b], [pjb])
                        pr.copy(self.evac_eng(), QKT.t[:, 2:4, 2048:2560], pbf.rearrange("p (a b) -> p a b", a=2),
                                [pjb], [QKT_k])
                else:
                    for t in range(g.nt):
                        bi = self.psum_gen()
                        pt_, pb = self.PS[bi]
                        pr.mm(pt_[:, 0:512], [(self.hT.t[:, kc, t * 128:(t + 1) * 128], slot.t[:, kc, 0:512]) for kc in range(KC)],
                              [self.hT_b[t], slot.b], [pb])
                        pr.copy("act", VA.t[:, t, 0:256], pt_[:, 0:256], [pb], [VA.b])
                        th_ = thb[0]
                        pr.act(th_.t[:], pt_[:, 256:512], AF.Tanh, [pb], [th_.b], scale=0.5)
                        pr.stt("dve", SG.t[:, t, :], th_.t[:], 1.0, pt_[:, 256:512], ALU.add, ALU.mult, [th_.b, pb], [SG.b])
                        if not g.ctx:
                            ks = kvst[self.rot("kvst", 2)]
                            pr.copy("act", ks.t[:, 0:256], pt_[:, 0:256], [pb], [ks.b])
                            sq_, tt_ = t // 2, (t % 2) * 128
                            pr.dma("sp", [(d["o_dv"][sq_, i, tt_:tt_ + 128, h * 256:(h + 1) * 256], ks.t[:, 0:256])],
                                   ks.b, [ks.b], ())
                    if g.name == "P" and h == 7:
                        self.issue_wout(l)
                    for (q0, qn, ktiles) in (g.qblocks if dbg >= 4 else []):
                        nsub = qn // 128
                        def mkjob(s, q0=q0, qn=qn, ktiles=ktiles):
                            def st_parts(kt):
                                return ([(QKT.t[:, 2 + s, kt * 128:(kt + 1) * 128], QKT.t[:, s, q0:q0 + qn])], [QKT_q, QKT_k])
                            return dict(qn=qn, ktiles=ktiles, st_parts=st_parts, v_ap=lambda kt: VA.t[:, kt, 0:257],
                                        v_reads=[VA.b], bias=None)
                        if nsub <= 2:
                            obs = self.attention_multi([mkjob(0), mkjob(1)], 257, DIFF_SCALE, [4, 6])
                        else:
                            obs = [self.attention_multi([mkjob(s)], 257, DIFF_SCALE, [4])[0] for s in range(1)]
                        for s in range(2):
                            if nsub > 2:
                                ob = obs[0] if s == 0 else self.attention_multi([mkjob(1)], 257, DIFF_SCALE, [4])[0]
                            else:
                                ob = obs[s]
                            raw = oraw[s]
                            for sub in range(nsub):
                                ot, obuf = ob[sub]
                                pr.copy("dve", raw.t[:, sub, 0:257], ot[:, 0:257], [obuf], [orb[s][sub]])
                        subs = list(range(nsub))
                        B1 = [orb[0][j] for j in subs]; B2 = [orb[1][j] for j in subs]
                        O1 = [oraw[0].t[:, j, :] for j in subs]; O2 = [oraw[1].t[:, j, :] for j in subs]
                        R1 = [self.small(sc) for _ in subs]; R2 = [self.small(sc) for _ in subs]
                        SS = [self.small(sc) for _ in subs]; TM = [self.small(sc) for _ in subs]; RS = [self.small(sc) for _ in subs]
                        for j in subs:
                            pr.op("dve", lambda e, r=R1[j], o1=O1[j]: e.reciprocal(out=r.t[:, 0:1], in_=o1[:, 256:257]), [B1[j]], [R1[j].b])
                        for j in subs:
                            pr.op("dve", lambda e, r=R2[j], o2=O2[j]: e.reciprocal(out=r.t[:, 0:1], in_=o2[:, 256:257]), [B2[j]], [R2[j].b])
                        for j in subs:
                            pr.ts("dve", O1[j][:, 0:256], O1[j][:, 0:256], R1[j].t[:, 0:1], None, ALU.mult, None, [B1[j], R1[j].b], [B1[j]])
                        for j in subs:
                            pr.tt("dve", R2[j].t[:, 1:2], R2[j].t[:, 0:1], nlam.t[:, 1:2], ALU.mult, [R2[j].b, nlam.b], [R2[j].b])
                        for j in subs:
                            pr.stt("dve", O2[j][:, 0:256], O2[j][:, 0:256], R2[j].t[:, 1:2], O1[j][:, 0:256], ALU.mult, ALU.add,
                                   [B2[j], R2[j].b, B1[j]], [B2[j]])
                        for j in subs:
                            jk = junks[j % 2]
                            pr.stt("dve", jk.t[:], O2[j][:, 0:256], 1.0, O2[j][:, 0:256], ALU.mult, ALU.mult, [B2[j]], [jk.b, SS[j].b],
                                   accum_out=SS[j].t[:, 0:1])
                        for j in subs:
                            pr.ts("dve", TM[j].t[:, 0:1], SS[j].t[:, 0:1], float(256 * EPS), None, ALU.add, None, [SS[j].b], [TM[j].b])
                        for j in subs:
                            pr.tt("pool", RS[j].t[:, 0:1], TM[j].t[:, 0:1], self.cst.t[:, 0:1], ALU.pow, [TM[j].b, self.cst.b], [RS[j].b])
                        for j in subs:
                            pr.stt("dve", O2[j][:, 0:256], O2[j][:, 0:256], RS[j].t[:, 0:1], gsub.t[:], ALU.mult, ALU.mult,
                                   [B2[j], RS[j].b, gsub.b], [B2[j]])
                        for j in subs:
                            tq = q0 // 128 + j
                            ys_ = yst[self.rot("yst", 3)]
                            pr.tt("dve", ys_.t[:], O2[j][:, 0:256], SG.t[:, tq, :], ALU.mult, [B2[j], SG.b], [ys_.b])
                            T = g.tile0 + tq
                            pr.dma("sp", [(d["ysc"][T * 128:(T + 1) * 128, h * 256:(h + 1) * 256], ys_.t[:])],
                                   ys_.b, [ys_.b], [self.ysc_b[T]])
            if dbg != 1:
                self.stream(reqs, consume)

    def phase_B_even(self, l, g):
        self.na_stage(l, g)
        self.mla_stage(l, g)

    def na_stage(self, l, g):
        pr, d = self.pr, self.d
        i = l // 2
        w = d["w_in_even"][i]
        with pr.scope() as sc:
            QaT = sc.tile([128, g.ntok], BF16, "QaT")
            KaT = sc.tile([128, g.klen], BF16, "KaT")
            VA = sc.tile([128, g.nkt, 136], BF16, "VAa")
            SG = sc.tile([128, g.nt, 128], BF16, "SGa")
            pr.memset("dve", VA.t[:, :, 128:129], 2.0, [VA.b])
            thb = [sc.tile([128, 128], F32, "thb") for _ in range(2)]
            yst = [sc.tile([128, 128], BF16, "yst") for _ in range(3)]
            oraw = sc.tile([128, 4, 132], F32, "oraw")
            orb = [sc.buf(f"orb{j}") for j in range(4)]
            kvst = [sc.tile([128, 256], F32, "kvst") for _ in range(2)] if not g.ctx else []
            kctx = sc.tile([128, 4, 128], BF16, "kctx") if g.ctx else None
            tabs = [sc.tile([128, 2, 1408], F32, "natab") for _ in range(2)] if g.ctx else []
            self.TMP = [sc.tile([128, 512], F32, "TMP") for _ in range(2)]
            reqs = [[(j * 128, w[:, j * 1024 + h * 128:j * 1024 + (h + 1) * 128], KC) for j in range(4)] for h in range(8)]

            def consume(h, slot):
                tab = None
                if g.ctx:
                    tab = tabs[h % 2]
                    pr.dma("sp", [(tab.t[:], d["natab"][i, h].rearrange("v p n -> p v n"))], tab.b, (), [tab.b])
                    pr.dma("pool", [(kctx.t[:], d["cnak"][i][:, h * 128:(h + 1) * 128].rearrange("(c p) n -> p c n", p=128))],
                           kctx.b, (), [kctx.b])
                    pr.dma("pool", [(VA.t[:, 16:20, 0:128],
                                     d["cnav"][i][:, h * 128:(h + 1) * 128].rearrange("(c p) n -> p c n", p=128))],
                           VA.b, (), [VA.b])
                for which, dst in ((0, QaT), (1, KaT)):
                    for qb in range(g.ntok // 512):
                        bi = self.psum_gen()
                        pt_, pb = self.PS[bi]
                        pr.mm(pt_[:, 0:512], [(slot.t[:, kc, which * 128:(which + 1) * 128], self.hT.t[:, kc, qb * 512:(qb + 1) * 512])
                                               for kc in range(KC)], [slot.b] + self.hT_b[4 * qb:4 * qb + 4], [pb])
                        pr.copy(self.evac_eng(), dst.t[:, qb * 512:(qb + 1) * 512], pt_[:, 0:512], [pb], [dst.b])
                if g.ctx:
                    bj = self.psum_gen()
                    pj, pjb = self.PS[bj]
                    pbf = self.ps(bj, 512, BF16)
                    pr.transposes([(pbf[:, c * 128:(c + 1) * 128], kctx.t[:, c, :]) for c in range(4)],
                                  self.ident.t[:], [kctx.b, self.ident.b], [pjb])
                    pr.copy(self.evac_eng(), KaT.t[:, 2048:2560], pbf, [pjb], [KaT.b])
                for t in range(g.nt):
                    bi = self.psum_gen()
                    pt_, pb = self.PS[bi]
                    c0 = 256 if g.ctx else 128
                    n = 512 - c0
                    pr.mm(pt_[:, 0:n], [(self.hT.t[:, kc, t * 128:(t + 1) * 128], slot.t[:, kc, c0:512]) for kc in range(KC)],
                          [self.hT_b[t], slot.b], [pb])
                    vo = n - 256
                    pr.copy("act", VA.t[:, t, 0:128], pt_[:, vo:vo + 128], [pb], [VA.b])
                    th_ = thb[t % 2]
                    pr.act(th_.t[:], pt_[:, vo + 128:vo + 256], AF.Tanh, [pb], [th_.b], scale=0.5)
                    pr.stt("dve", SG.t[:, t, :], th_.t[:], 1.0, pt_[:, vo + 128:vo + 256], ALU.add, ALU.mult, [th_.b, pb], [SG.b])
                    if not g.ctx:
                        ks = kvst[self.rot("kvst", 2)]
                        pr.copy("act", ks.t[:, 0:256], pt_[:, 0:256], [pb], [ks.b])
                        sq_, tt_ = t // 2, (t % 2) * 128
                        pr.dma("sp", [(d["o_nak"][sq_, i, tt_:tt_ + 128, h * 128:(h + 1) * 128], ks.t[:, 0:128]),
                                      (d["o_nav"][sq_, i, tt_:tt_ + 128, h * 128:(h + 1) * 128], ks.t[:, 128:256])],
                               ks.b, [ks.b], ())
                groups = [[b] for b in range(4)] if g.ctx else [[0, 1], [2, 3]]
                for grp in groups:
                    jobs = []
                    for bq in grp:
                        q0, qn, ktiles = g.qblocks[bq]
                        bias = None
                        if g.ctx:
                            wt = na_window_tiles(bq)
                            ktiles = [16, 17, 18, 19] + [j for (j, _, _) in wt]
                            info = {j: (v, t0) for (j, v, t0) in wt}

                            def bias(kt, info=info, tab=tab):
                                if kt >= 16:
                                    return None
                                v, t0 = info[kt]
                                return (tab.t[:, v, t0 * 64:t0 * 64 + 512], tab.b)

                        def st_parts(kt, q0=q0, qn=qn):
                            return ([(KaT.t[:, kt * 128:(kt + 1) * 128], QaT.t[:, q0:q0 + qn])], [KaT.b, QaT.b])
                        jobs.append(dict(qn=qn, ktiles=ktiles, st_parts=st_parts, v_ap=lambda kt: VA.t[:, kt, 0:129],
                                         v_reads=[VA.b], bias=bias))
                    obs = self.attention_multi(jobs, 129, NA_SCALE, [4, 6][:len(jobs)])
                    idxs = []
                    for ji, bq in enumerate(grp):
                        q0, qn, _ = g.qblocks[bq]
                        for sub in range(qn // 128):
                            ot, obuf = obs[ji][sub]
                            oi = ji * 2 + sub if len(grp) > 1 else sub
                            pr.copy("dve", oraw.t[:, oi, 0:129], ot[:, 0:129], [obuf], [orb[oi]])
                            idxs.append((oi, q0 // 128 + sub))
                    for (oi, tq) in idxs:
                        r = self.small(sc)
                        o_ = oraw.t[:, oi, :]
                        pr.op("dve", lambda e, r=r, o_=o_: e.reciprocal(out=r.t[:, 0:1], in_=o_[:, 128:129]), [orb[oi]], [r.b])
                        ys_ = yst[self.rot("yst", 3)]
                        pr.stt("dve", ys_.t[:], o_[:, 0:128], r.t[:, 0:1], SG.t[:, tq, :], ALU.mult, ALU.mult,
                               [orb[oi], r.b, SG.b], [ys_.b])
                        T = g.tile0 + tq
                        pr.dma("sp", [(d["ysc"][T * 128:(T + 1) * 128, h * 128:(h + 1) * 128], ys_.t[:])],
                               ys_.b, [ys_.b], [self.ysc_b[T]])
            self.stream(reqs, consume)
            for x in yst + kvst + tabs + ([kctx] if kctx else []) + [VA]:
                pr.release_dma(x.b)

    def mla_stage(self, l, g):
        pr, d = self.pr, self.d
        i = l // 2
        w = d["w_in_even"][i]
        with pr.scope() as sc:
            cqT = sc.tile([128, 4, g.ntok], BF16, "cqT")
            CKT = sc.tile([128, 3, g.klen], BF16, "CKT")
            gqb = sc.tile([128, 512], F32, "gqb")
            gkvb = sc.tile([128, 256], F32, "gkvb")
            pr.dma("sp", [(gqb.t[:], self.bcast_rows(d["mla_g_q"][i, :]))], gqb.b, (), [gqb.b])
            pr.ts("dve", gqb.t[:], gqb.t[:], float(math.sqrt(512.0)), None, ALU.mult, None, [gqb.b], [gqb.b])
            pr.dma("sp", [(gkvb.t[:], self.bcast_rows(d["mla_g_kv"][i, :]))], gkvb.b, (), [gkvb.b])
            pr.ts("dve", gkvb.t[:], gkvb.t[:], 16.0, None, ALU.mult, None, [gkvb.b], [gkvb.b])
            with pr.scope() as s1:
                junk = s1.tile([128, 512], BF16, "junk")
                cqn = [s1.tile([128, 512], BF16, "cqn") for _ in range(2)]
                cat = [s1.tile([128, 384], BF16, "cat") for _ in range(2)]
                ckf = [s1.tile([128, 320], F32, "ckf") for _ in range(2)] if not g.ctx else []
                thb = [s1.tile([128, 512], F32, "thb") for _ in range(2)]
                sgst = [s1.tile([128, 512], BF16, "sgst") for _ in range(2)]
                reqs = [[(0, w[:, 4096:4608], KC)], [(0, w[:, 4608:4928], KC)],
                        [(0, w[:, 4928:5440], KC)], [(0, w[:, 5440:5952], KC)]]

                def consume(ri, slot):
                    n = 320 if ri == 1 else 512

                    def proj(t):
                        bi = self.psum_gen()
                        pt_, pb = self.PS[bi]
                        pr.mm(pt_[:, 0:n], [(self.hT.t[:, kc, t * 128:(t + 1) * 128], slot.t[:, kc, 0:n]) for kc in range(KC)],
                              [self.hT_b[t], slot.b], [pb])
                        return pt_, pb

                    if ri == 0:
                        def first(t):
                            pt_, pb = proj(t)
                            ssq = self.small(s1); rs = self.small(s1)
                            pr.act(junk.t[:], pt_[:, 0:512], AF.Square, [pb], [junk.b, ssq.b], accum_out=ssq.t[:, 0:1])
                            self.rsqrt(s1, rs, ssq, 512 * EPS)
                            cq_ = cqn[t % 2]
                            pr.stt("dve", cq_.t[:], pt_[:, 0:512], rs.t[:, 0:1], gqb.t[:], ALU.mult, ALU.mult,
                                   [pb, rs.b, gqb.b], [cq_.b])

                        def second(t):
                            cq_ = cqn[t % 2]
                            bj = self.psum_gen()
                            pj, pjb = self.PS[bj]
                            pbf = self.ps(bj, 512, BF16)
                            pr.transposes([(pbf[:, j * 128:(j + 1) * 128], cq_.t[:, j * 128:(j + 1) * 128]) for j in range(4)],
                                          self.ident.t[:], [cq_.b, self.ident.b], [pjb])
                            pr.copy("act", cqT.t[:, :, t * 128:(t + 1) * 128], pbf.rearrange("p (a b) -> p a b", a=4),
                                    [pjb], [cqT.b])
                        self.pipelined(g.nt, first, second)
                    elif ri == 1:
                        def first(t):
                            pt_, pb = proj(t)
                            ssq = self.small(s1); rs = self.small(s1)
                            pr.act(junk.t[:, 0:256], pt_[:, 0:256], AF.Square, [pb], [junk.b, ssq.b], accum_out=ssq.t[:, 0:1])
                            self.rsqrt(s1, rs, ssq, 256 * EPS)
                            ct_ = cat[t % 2]
                            if g.ctx:
                                pr.stt("dve", ct_.t[:, 0:256], pt_[:, 0:256], rs.t[:, 0:1], gkvb.t[:], ALU.mult, ALU.mult,
                                       [pb, rs.b, gkvb.b], [ct_.b])
                                self.rope(s1, pt_[:, 256:320], pb, ct_.t[:, 256:320], ct_.b, 1, 16, self.ropeM, t,
                                          dup_ap=ct_.t[:, 320:384])
                            else:
                                cf = ckf[t % 2]
                                pr.stt("dve", cf.t[:, 0:256], pt_[:, 0:256], rs.t[:, 0:1], gkvb.t[:], ALU.mult, ALU.mult,
                                       [pb, rs.b, gkvb.b], [cf.b])
                                pr.copy("dve", cf.t[:, 256:320], pt_[:, 256:320], [pb], [cf.b])
                                pr.copy("act", ct_.t[:, 0:320], cf.t[:, 0:320], [cf.b], [ct_.b])
                                pr.copy("act", ct_.t[:, 320:384], cf.t[:, 256:320], [cf.b], [ct_.b])
                                sq_, tt_ = t // 2, (t % 2) * 128
                                pr.dma("sp", [(d["o_ckv"][sq_, i, tt_:tt_ + 128, :], cf.t[:, 0:256]),
                                              (d["o_kpe"][sq_, i, tt_:tt_ + 128, :], cf.t[:, 256:320])],
                                       cf.b, [cf.b], ())

                        def second(t):
                            ct_ = cat[t % 2]
                            bj = self.psum_gen()
                            pj, pjb = self.PS[bj]
                            pbf = self.ps(bj, 384, BF16)
                            pr.transposes([(pbf[:, j * 128:(j + 1) * 128], ct_.t[:, j * 128:(j + 1) * 128]) for j in range(3)],
                                          self.ident.t[:], [ct_.b, self.ident.b], [pjb])
                            pr.copy("act", CKT.t[:, :, t * 128:(t + 1) * 128], pbf.rearrange("p (a b) -> p a b", a=3),
                                    [pjb], [CKT.b])
                        self.pipelined(g.nt, first, second)
                    else:
                        for t in range(g.nt):
                            pt_, pb = proj(t)
                            th_ = thb[t % 2]
                            c0 = (ri - 2) * 512
                            pr.act(th_.t[:], pt_[:, 0:512], AF.Tanh, [pb], [th_.b], scale=0.5)
                            sg_ = sgst[t % 2]
                            pr.stt("dve", sg_.t[:], th_.t[:], 1.0, pt_[:, 0:512], ALU.add, ALU.mult, [th_.b, pb], [sg_.b])
                            T = g.tile0 + t
                            pr.dma("sp", [(d["sgsc"][T * 128:(T + 1) * 128, c0:c0 + 512], sg_.t[:])], sg_.b, [sg_.b],
                                   [self.sgsc_b[T]])
                self.stream(reqs, consume)
                if g.ctx:
                    cc = s1.tile([128, 4, 384], BF16, "ctxcat")
                    pr.dma("pool", [(cc.t[:, :, 0:256], d["cckv"][i].rearrange("(c p) n -> p c n", p=128)),
                                    (cc.t[:, :, 256:320], d["ckpe"][i].rearrange("(c p) n -> p c n", p=128)),
                                    (cc.t[:, :, 320:384], d["ckpe"][i].rearrange("(c p) n -> p c n", p=128))],
                           cc.b, (), [cc.b])
                    for c in range(4):
                        bj = self.psum_gen()
                        pj, pjb = self.PS[bj]
                        pbf = self.ps(bj, 384, BF16)
                        pr.transposes([(pbf[:, j * 128:(j + 1) * 128], cc.t[:, c, j * 128:(j + 1) * 128]) for j in range(3)],
                                      self.ident.t[:], [cc.b, self.ident.b], [pjb])
                        pr.copy(self.evac_eng(), CKT.t[:, :, 2048 + c * 128:2048 + (c + 1) * 128],
                                pbf.rearrange("p (a b) -> p a b", a=3), [pjb], [CKT.b])
                    pr.release_dma(cc.b)
                for x in ckf:
                    pr.release_dma(x.b)
            if g.name == "P":
                self.issue_wout(l)
            with pr.scope() as s2:
                sl0 = self.W[self.wslot_i % 3]; sl1 = self.W[(self.wslot_i + 1) % 3]
                self.wslot_i += 2
                Wuq = Tile(sl0.t[:, 0:12, :].rearrange("p a b -> p (a b)").rearrange("p (k n) -> p k n", k=4), sl0.b)
                Wukv = Tile(sl1.t[:, 0:8, :].rearrange("p a b -> p (a b)").rearrange("p (k n) -> p k n", k=2), sl1.b)
                pr.dma("pool", [(Wuq.t, d["mla_w_uq"][i].rearrange("(kc p) n -> p kc n", p=128))], Wuq.b, (), [Wuq.b])
                pr.dma("pool", [(Wukv.t, d["mla_w_ukv"][i].rearrange("(kc p) n -> p kc n", p=128))], Wukv.b, (), [Wukv.b])
                QrT = s2.tile([128, g.ntok], BF16, "QrT")
                QnT = s2.tile([128, g.ntok], BF16, "QnT")
                KnT = s2.tile([128, g.klen], BF16, "KnT")
                VM = s2.tile([128, g.nkt, 136], BF16, "VM")
                pr.memset("dve", VM.t[:, :, 128:129], 2.0, [VM.b])
                qrb = [s2.tile([128, 128], BF16, "qrb") for _ in range(2)]
                yst = [s2.tile([128, 128], BF16, "yst") for _ in range(3)]
                SGh = s2.tile([128, g.nt, 128], BF16, "SGh")
                oraw = s2.tile([128, 4, 132], F32, "oraw")
                orb = [s2.buf(f"orb{j}") for j in range(4)]
                Wuq4 = Wuq.t.rearrange("p k (h d) -> p k h d", d=192)
                for h in range(8):
                    hp = h % 2
                    pr.dma("sp", [(SGh.t[:], d["sgsc"][g.tile0 * 128:(g.tile0 + g.nt) * 128, h * 128:(h + 1) * 128]
                                   .rearrange("(t p) n -> p t n", p=128))], SGh.b,
                           self.sgsc_b[g.tile0:g.tile0 + g.nt], [SGh.b])
                    if hp == 0:
                        def first(t, h=h):
                            bi = self.psum_gen()
                            pt_, pb = self.PS[bi]
                            pr.mm(pt_[:, 0:128].rearrange("p (h d) -> p h d", d=64),
                                  [(cqT.t[:, kc, t * 128:(t + 1) * 128], Wuq4[:, kc, h:h + 2, 128:192]) for kc in range(4)],
                                  [cqT.b, Wuq.b], [pb])
                            qr_ = qrb[t % 2]
                            if g.rope:
                                self.rope(s2, pt_[:, 0:128], pb, qr_.t[:], qr_.b, 2, 16, self.ropeM, t)
                            else:
                                pr.copy("act", qr_.t[:], pt_[:, 0:128], [pb], [qr_.b])

                        def second(t):
                            qr_ = qrb[t % 2]
                            bj = self.psum_gen()
                            pj, pjb = self.PS[bj]
                            pbf = self.ps(bj, 128, BF16)
                            pr.transposes([(pbf, qr_.t[:])], self.ident.t[:], [qr_.b, self.ident.b], [pjb])
                            pr.copy("act", QrT.t[:, t * 128:(t + 1) * 128], pbf, [pjb], [QrT.b])
                        self.pipelined(g.nt, first, second)
                    for qb in range(g.ntok // 512):
                        bi = self.psum_gen()
                        pt_, pb = self.PS[bi]
                        pr.mm(pt_[:, 0:512], [(Wuq.t[:, kc, h * 192:h * 192 + 128], cqT.t[:, kc, qb * 512:(qb + 1) * 512])
                                               for kc in range(4)], [Wuq.b, cqT.b], [pb])
                        pr.copy(self.evac_eng(), QnT.t[:, qb * 512:(qb + 1) * 512], pt_[:, 0:512], [pb], [QnT.b])
                    for kb in range(g.klen // 512):
                        bi = self.psum_gen()
                        pt_, pb = self.PS[bi]
                        pr.mm(pt_[:, 0:512], [(Wukv.t[:, kc, h * 256:h * 256 + 128], CKT.t[:, kc, kb * 512:(kb + 1) * 512])
                                               for kc in range(2)], [Wukv.b, CKT.b], [pb])
                        pr.copy(self.evac_eng(), KnT.t[:, kb * 512:(kb + 1) * 512], pt_[:, 0:512], [pb], [KnT.b])
                    for k0 in range(0, g.nkt, 4):
                        bi = self.psum_gen()
                        pt_, pb = self.PS[bi]

                        def fn(e, k0=k0, pt_=pt_, h=h):
                            ins = None
                            for j in range(4):
                                kt = k0 + j
                                for kc in range(2):
                                    ins = e.matmul(pt_[:, j * 128:(j + 1) * 128], lhsT=CKT.t[:, kc, kt * 128:(kt + 1) * 128],
                                                   rhs=Wukv.t[:, kc, h * 256 + 128:h * 256 + 256], start=(kc == 0), stop=(kc == 1))
                            return ins
                        pr.n_ins += 7
                        pr.op("pe", fn, [CKT.b, Wukv.b], [pb])
                        pr.copy(self.evac_eng(), VM.t[:, k0:k0 + 4, 0:128], pt_[:, 0:512].rearrange("p (a b) -> p a b", a=4),
                                [pb], [VM.b])
                    groups = [[b] for b in range(4)] if g.ctx else [[0, 1], [2, 3]]
                    for grp in groups:
                        jobs = []
                        for bq in grp:
                            q0, qn, ktiles = g.qblocks[bq]

                            def st_parts(kt, q0=q0, qn=qn, hp=hp):
                                return ([(KnT.t[:, kt * 128:(kt + 1) * 128], QnT.t[:, q0:q0 + qn]),
                                         (CKT.t[hp * 64:(hp + 1) * 64, 2, kt * 128:(kt + 1) * 128],
                                          QrT.t[hp * 64:(hp + 1) * 64, q0:q0 + qn])], [KnT.b, QnT.b, CKT.b, QrT.b])
                            jobs.append(dict(qn=qn, ktiles=ktiles, st_parts=st_parts, v_ap=lambda kt: VM.t[:, kt, 0:129],
                                             v_reads=[VM.b], bias=None))
                        obs = self.attention_multi(jobs, 129, MLA_SCALE, [4, 6][:len(jobs)])
                        idxs = []
                        for ji, bq in enumerate(grp):
                            q0, qn, _ = g.qblocks[bq]
                            for sub in range(qn // 128):
                                ot, obuf = obs[ji][sub]
                                oi = ji * 2 + sub if len(grp) > 1 else sub
                                pr.copy("dve", oraw.t[:, oi, 0:129], ot[:, 0:129], [obuf], [orb[oi]])
                                idxs.append((oi, q0 // 128 + sub))
                        for (oi, tq) in idxs:
                            r = self.small(s2)
                            o_ = oraw.t[:, oi, :]
                            pr.op("dve", lambda e, r=r, o_=o_: e.reciprocal(out=r.t[:, 0:1], in_=o_[:, 128:129]), [orb[oi]], [r.b])
                            ys_ = yst[self.rot("yst", 3)]
                            pr.stt("dve", ys_.t[:], o_[:, 0:128], r.t[:, 0:1], SGh.t[:, tq, :],
                                   ALU.mult, ALU.mult, [orb[oi], r.b, SGh.b], [ys_.b])
                            T = g.tile0 + tq
                            pr.dma("sp", [(d["ysc"][T * 128:(T + 1) * 128, 1024 + h * 128:1024 + (h + 1) * 128], ys_.t[:])],
                                   ys_.b, [ys_.b], [self.ysc_b[T]])
            for x in (gqb, gkvb):
                pr.release_dma(x.b)

    def build(self):
        nc = self.nc
        cfg = self.cfg
        self.declare()
        d = self.d
        with self.stack as st:
            pr = self.pr = Prog(nc, st)
            self.rotc = {}
            self.wout_done = {}
            self.evi = 0
            self.wslot_i = 0
            self.gS, self.gP = Group("S"), Group("P")
            self.xst_b = [pr.gbuf(f"xst{T}") for T in range(24)]
            self.ysc_b = [pr.gbuf(f"ysc{T}") for T in range(24)]
            self.sgsc_b = [pr.gbuf(f"sgsc{T}") for T in range(24)]
            self.modv_b = [pr.gbuf(f"modv{l}") for l in range(4)]
            self.out_b = pr.gbuf("outs")
            self.PS = []
            for i in range(8):
                t = st.enter_context(nc.psum_tensor(f"ps{i}", [128, 512], F32))
                pb_ = pr.gbuf(f"ps{i}")
                pb_.excl = True
                self.PS.append((t, pb_))
            with pr.scope() as g0:
                self.ident = g0.tile([128, 128], BF16, "ident")
                pr.dma("pool", [(self.ident.t[:], d["ident"][:, :])], self.ident.b, (), [self.ident.b])
                pr.release_dma(self.ident.b)
                self.cst = g0.tile([128, 4], F32, "cst")
                pr.memset("pool", self.cst.t[:, 0:1], -0.5, [self.cst.b])
                pr.memset("pool", self.cst.t[:, 1:2], float(D * EPS), [self.cst.b])
                pr.memset("pool", self.cst.t[:, 2:3], float(256 * EPS), [self.cst.b])
                pr.memset("pool", self.cst.t[:, 3:4], float(512 * EPS), [self.cst.b])
                self.W = [g0.tile([128, KC, 512], BF16, "W") for _ in range(3)]
                self.PT = [g0.tile([128, 512], BF16, "PT") for _ in range(4)]
                self.layers = cfg.get("layers", list(range(cfg.get("nlayers", DEPTH))))
                self.mod_prepare(g0)
                with pr.scope() as scm:
                    st0 = self.mod_stream(self.layers[0], scm)
                    for _ in range(12):
                        st0()
                self.ropeD = g0.tile([128, 16, 2, 2, 2, 32], F32, "ropeD")
                self.ropeM = g0.tile([128, 16, 2, 2, 2, 16], F32, "ropeM")
                pr.dma("sp", [(self.ropeD.t[:].rearrange("p a b c d e -> p (a b c d e)"), d["ropeD"][:, :])], self.ropeD.b, (), [self.ropeD.b])
                pr.dma("sp", [(self.ropeM.t[:].rearrange("p a b c d e -> p (a b c d e)"), d["ropeM"][:, :])], self.ropeM.b, (), [self.ropeM.b])
                pr.release_dma(self.ropeD.b); pr.release_dma(self.ropeM.b)
                self.hT = g0.tile([128, KC, 2048], BF16, "hT")
                self.hT_b = [g0.buf(f"hT{t}") for t in range(16)]
                self.ra = [g0.tile([128, 512], F32, "ra") for _ in range(1)]
                self.rb = [g0.tile([128, 512], F32, "rb") for _ in range(1)]
                self.layers = cfg.get("layers", list(range(cfg.get("nlayers", DEPTH))))
                for l in self.layers:
                    for g in ((self.gP,) if cfg.get("only_P") else (self.gS, self.gP) if not cfg.get("only_S") else (self.gS,)):
                        self.phase_A(l, g)
                        if l % 2 == 0:
                            self.phase_B_even(l, g)
                        else:
                            self.phase_B_odd(l, g)
                    self.phase_C(l)
                pr.wait_all_dma("sp")
        return nc


_CONST_CACHE = {}


def _consts():
    if not _CONST_CACHE:
        _CONST_CACHE["ident"] = np.eye(128, dtype=np.float32)
        _CONST_CACHE["ropeD"] = np.ascontiguousarray(rope_tables(32).reshape(128, -1))
        _CONST_CACHE["ropeM"] = np.ascontiguousarray(rope_tables(16).reshape(128, -1))
    return _CONST_CACHE


def make_in_maps(inputs, cores=range(NCORES)):
    f = lambda a: np.ascontiguousarray(np.asarray(a, dtype=np.float32))
    cst = _consts()
    natab = na_bias_tables(np.asarray(inputs["na_rpb"], np.float32))
    shared = {
        "w_ada": f(inputs["w_ada"]), "b_ada": f(inputs["b_ada"]), "g_pre": f(inputs["g_pre"]), "g_post": f(inputs["g_post"]),
        "w_in_even": f(inputs["w_in_even"]), "w_out_even": f(inputs["w_out_even"]),
        "mla_g_q": f(inputs["mla_g_q"]), "mla_w_uq": f(inputs["mla_w_uq"]), "mla_g_kv": f(inputs["mla_g_kv"]),
        "mla_w_ukv": f(inputs["mla_w_ukv"]), "w_in_odd": f(inputs["w_in_odd"]), "w_out_odd": f(inputs["w_out_odd"]),
        "diff_lambda": f(np.asarray(inputs["diff_lambda"]).reshape(2, 512)), "diff_g": f(inputs["diff_g"]),
        "ident": cst["ident"], "ropeD": cst["ropeD"], "ropeM": cst["ropeM"], "natab": natab,
    }
    xs = np.asarray(inputs["x_sample"], np.float32)
    xp = np.asarray(inputs["x_prompt"], np.float32)
    c = np.asarray(inputs["c"], np.float32)
    cctx = np.asarray(inputs["c_ctx"], np.float32)
    maps = []
    for b in cores:
        m = dict(shared)
        m["xs"] = f(xs[b])
        m["xp"] = f(xp[4 * b:4 * b + 4].reshape(1024, D))
        m["cnak"] = f(np.asarray(inputs["cache_na_k"])[b].reshape(2, 512, 1024))
        m["cnav"] = f(np.asarray(inputs["cache_na_v"])[b].reshape(2, 512, 1024))
        m["cckv"] = f(np.asarray(inputs["cache_mla_ckv"])[b])
        m["ckpe"] = f(np.asarray(inputs["cache_mla_kpe"])[b])
        m["cdk"] = f(np.asarray(inputs["cache_diff_k"])[b].reshape(2, 512, 2048))
        m["cdv"] = f(np.asarray(inputs["cache_diff_v"])[b].reshape(2, 512, 2048))
        cond = np.stack([c[b], cctx], axis=0)
        m["condT"] = f(cond.reshape(2, KC, 128).transpose(2, 1, 0).reshape(128, 32))
        maps.append(m)
    return maps


_NC_CACHE = {}


def get_program(cfg=None):
    key = repr(sorted((cfg or {}).items()))
    if key not in _NC_CACHE:
        _NC_CACHE[key] = Builder(cfg or {}).build()
    return _NC_CACHE[key]


def kernel(**inputs):
    nc = get_program()
    in_maps = make_in_maps(inputs)
    res = run_bass_kernel_spmd(nc, in_maps, core_ids=list(range(NCORES)))
    R = res.results
    cat = lambda k: np.stack([np.asarray(r[k]) for r in R], axis=0)
    y_p = cat("y_p").reshape(32, 256, D)
    y_s = cat("y_s").reshape(8, 2048, D)
    nak = cat("o_nak").reshape(32, 2, 256, 8, 128)
    nav = cat("o_nav").reshape(32, 2, 256, 8, 128)
    ckv = cat("o_ckv").reshape(32, 2, 256, 256)
    kpe = cat("o_kpe").reshape(32, 2, 256, 64)
    dk = cat("o_dk").reshape(32, 2, 256, 8, 256)
    dv = cat("o_dv").reshape(32, 2, 256, 8, 256)
    return tuple(np.ascontiguousarray(a, dtype=np.float32) for a in (y_p, y_s, nak, nav, ckv, kpe, dk, dv))
```

```python
import math
from contextlib import ExitStack

import numpy as np
import concourse.bass as bass
import concourse.mybir as mybir
from concourse.bass_utils import run_bass_kernel_spmd

F32 = mybir.dt.float32
BF16 = mybir.dt.bfloat16
AF = mybir.ActivationFunctionType
ALU = mybir.AluOpType

D = 2048
KC = 16
DEPTH = 4
EPS = 1e-6
NA_SCALE = 128 ** -0.5
MLA_SCALE = 192 ** -0.5
DIFF_SCALE = 128 ** -0.5
MASKV = -30000.0
NCORES = 8


class Buf:
    __slots__ = ("name", "w", "r", "init", "excl")

    def __init__(self, name, init):
        self.name = name
        self.excl = False
        self.w = None
        self.r = {}
        self.init = dict(init)


class Tile:
    __slots__ = ("t", "b")

    def __init__(self, t, b):
        self.t = t
        self.b = b


class Scope:
    def __init__(self, pr):
        self.pr = pr
        self.stack = ExitStack()
        self.bufs = []
        self.sems = []

    def __enter__(self):
        self.stack.__enter__()
        return self

    def tile(self, shape, dt, name=None):
        pr = self.pr
        pr.uid += 1
        nm = f"{name or 't'}_{pr.uid}"
        t = self.stack.enter_context(pr.nc.sbuf_tensor(nm, list(shape), dt))
        b = Buf(nm, pr.frontier)
        self.bufs.append(b)
        return Tile(t, b)

    def buf(self, name="b"):
        b = Buf(name, self.pr.frontier)
        self.bufs.append(b)
        return b

    def small(self):
        if not hasattr(self, "_sm"):
            self._sm = [self.tile([128, 2], F32, "sm") for _ in range(32)]
            self._smi = 0
        self._smi += 1
        return self._sm[self._smi % 32]

    def __exit__(self, *a):
        pr = self.pr
        for b in self.bufs:
            if b.w is not None:
                pr.front_merge(b.w[0], b.w[1])
            for k, v in b.r.items():
                pr.front_merge(k, v)
            pr.release_dma(b)
        return self.stack.__exit__(*a)


class Prog:
    def __init__(self, nc, stack, n_dma_sems=56):
        self.nc = nc
        self.uid = 0
        self.eng = {"pe": nc.tensor, "act": nc.scalar, "dve": nc.vector, "pool": nc.gpsimd, "sp": nc.sync}
        self.sems = {}
        for e in ("pe", "act", "dve", "pool"):
            self.sems[e] = stack.enter_context(nc.semaphore(f"sem_{e}"))
        self.cnt = {e: 0 for e in ("pe", "act", "dve", "pool")}
        self.known = {e: {} for e in self.eng}
        self.frontier = {}
        self.dma_cnt = {}
        self.free_dma = {"sp": [], "pool": []}
        for i in range(n_dma_sems):
            k = f"dma{i}"
            self.sems[k] = stack.enter_context(nc.semaphore(f"sem_{k}"))
            self.dma_cnt[k] = 0
            self.free_dma["pool" if i < 16 else "sp"].append(k)
        self.buf_dma = {}
        self.n_ins = 0

    def front_merge(self, k, v):
        if self.frontier.get(k, 0) < v:
            self.frontier[k] = v

    def scope(self):
        return Scope(self)

    def gbuf(self, name):
        return Buf(name, {})

    def dma_key(self, b, q):
        k = self.buf_dma.get((id(b), q))
        if k is None:
            k = self.free_dma[q].pop()
            self.buf_dma[(id(b), q)] = k
        return k

    def release_dma(self, b):
        for q in ("sp", "pool"):
            k = self.buf_dma.pop((id(b), q), None)
            if k is not None:
                self.free_dma[q].insert(0, k)

    def _waits(self, eng, own, reads, writes):
        w = {}

        def add(k, v, raw):
            if k == own and eng == "pe":
                return
            if w.get(k, 0) < v:
                w[k] = v

        for b in reads:
            for k, v in b.init.items():
                add(k, v, True)
            if b.w is not None:
                add(b.w[0], b.w[1], True)
            if b.excl:
                for k, v in b.r.items():
                    if k != own:
                        add(k, v, False)
        for b in writes:
            for k, v in b.init.items():
                add(k, v, True)
            if b.w is not None:
                add(b.w[0], b.w[1], False)
            for k, v in b.r.items():
                add(k, v, False)
        return w

    def _emit_waits(self, eng, w):
        e = self.eng[eng]
        kn = self.known[eng]
        for k, v in w.items():
            if kn.get(k, 0) < v:
                e.wait_ge(self.sems[k], v)
                kn[k] = v
                self.n_ins += 1

    def _record(self, ev, reads, writes):
        k, v = ev
        for b in reads:
            if b.r.get(k, 0) < v:
                b.r[k] = v
        for b in writes:
            b.w = ev
            b.r = {}

    def op(self, eng, fn, reads=(), writes=()):
        w = self._waits(eng, eng, reads, writes)
        self._emit_waits(eng, w)
        ins = fn(self.eng[eng])
        self.cnt[eng] += 1
        ins.then_inc(self.sems[eng], 1)
        self.n_ins += 1
        self._record((eng, self.cnt[eng]), reads, writes)

    def dma(self, q, pairs, slot, reads=(), writes=()):
        k = self.dma_key(slot, q)
        w = self._waits(q, None, reads, writes)
        if self.dma_cnt[k] > 0:
            w[k] = max(w.get(k, 0), self.dma_cnt[k])
        self._emit_waits(q, w)
        e = self.eng[q]
        for (o, i) in pairs:
            e.dma_start(out=o, in_=i).then_inc(self.sems[k], 16)
            self.dma_cnt[k] += 16
            self.n_ins += 1
        self._record((k, self.dma_cnt[k]), reads, writes)

    def wait_all_dma(self, eng):
        w = {k: v for k, v in self.dma_cnt.items() if v > 0}
        self._emit_waits(eng, w)

    def mm(self, out, pairs, reads, writes):
        n = len(pairs)

        def fn(e):
            ins = None
            for j, (l, r) in enumerate(pairs):
                ins = e.matmul(out, lhsT=l, rhs=r, start=(j == 0), stop=(j == n - 1))
            return ins
        self.n_ins += n - 1
        self.op("pe", fn, reads, writes)

    def transposes(self, items, ident, reads, writes):
        def fn(e):
            ins = None
            for (o, i) in items:
                ins = e.transpose(o, i, ident)
            return ins
        self.n_ins += len(items) - 1
        self.op("pe", fn, reads, writes)

    def act(self, out, in_, func, reads, writes, **kw):
        self.op("act", lambda e: e.activation(out=out, in_=in_, func=func, **kw), reads, writes)

    def copy(self, eng, out, in_, reads, writes):
        if eng == "act":
            self.op("act", lambda e: e.activation(out=out, in_=in_, func=AF.Copy), reads, writes)
        else:
            self.op(eng, lambda e: e.tensor_copy(out=out, in_=in_), reads, writes)

    def tt(self, eng, out, in0, in1, op, reads, writes):
        self.op(eng, lambda e: e.tensor_tensor(out=out, in0=in0, in1=in1, op=op), reads, writes)

    def ts(self, eng, out, in0, s1, s2, op0, op1, reads, writes):
        if s2 is None:
            self.op(eng, lambda e: e.tensor_single_scalar(out=out, in_=in0, scalar=s1, op=op0), reads, writes)
        else:
            self.op(eng, lambda e: e.tensor_scalar(out=out, in0=in0, scalar1=s1, scalar2=s2, op0=op0, op1=op1),
                    reads, writes)

    def stt(self, eng, out, in0, scalar, in1, op0, op1, reads, writes, accum_out=None):
        if accum_out is None:
            self.op(eng, lambda e: e.scalar_tensor_tensor(out=out, in0=in0, scalar=scalar, in1=in1, op0=op0, op1=op1),
                    reads, writes)
        else:
            self.op(eng, lambda e: e.scalar_tensor_tensor(out=out, in0=in0, scalar=scalar, in1=in1, op0=op0, op1=op1,
                                                          accum_out=accum_out), reads, writes)

    def memset(self, eng, ap, val, writes):
        self.op(eng, lambda e: e.memset(ap, val), (), writes)


def rope_tables(hw):
    inv = (10000.0 ** (-np.arange(hw, dtype=np.float32) / np.float32(hw))).astype(np.float32)
    t = np.arange(2048)
    rows = (t // 64).astype(np.float32)
    cols = (t % 64).astype(np.float32)
    out = np.zeros((128, 16, 2, 2, 2, hw), np.float32)
    for ax, pos in enumerate((rows, cols)):
        ang = (pos[:, None] * inv[None, :]).astype(np.float32)
        c = np.cos(ang).astype(np.float32).reshape(16, 128, hw).transpose(1, 0, 2)
        s = np.sin(ang).astype(np.float32).reshape(16, 128, hw).transpose(1, 0, 2)
        out[:, :, 0, ax, 0, :] = c
        out[:, :, 0, ax, 1, :] = c
        out[:, :, 1, ax, 0, :] = -s
        out[:, :, 1, ax, 1, :] = s
    return out


def na_bias_tables(rpb):
    L, H = rpb.shape[0], rpb.shape[1]
    kc = np.arange(64)[:, None]
    c = np.arange(64)[None, :]
    cs = np.clip(c - 8, 0, 48)
    colok = (kc >= cs) & (kc < cs + 16)
    dcidx = np.clip(kc - c + 15, 0, 30)
    out = np.full((L, H, 2, 128, 22, 64), MASKV, np.float32)
    for a in range(2):
        for t in range(22):
            dr = 10 + a - t
            if abs(dr) > 7:
                continue
            vals = rpb[:, :, dr + 7, :][:, :, dcidx]
            blk = np.where(colok[None, None], vals, np.float32(MASKV)).astype(np.float32)
            out[:, :, 1, a * 64:(a + 1) * 64, t, :] = blk
            if -4 <= dr <= 3:
                out[:, :, 0, a * 64:(a + 1) * 64, t, :] = blk
    return out.reshape(L, H, 2, 128, 22 * 64)


def na_window_tiles(b):
    rng = {0: range(0, 6), 1: range(2, 10), 2: range(6, 14), 3: range(10, 16)}[b]
    res = []
    for j in rng:
        full = (b == 0 and j in (2, 3)) or (b == 3 and j in (12, 13))
        res.append((j, 1 if full else 0, 10 - (2 * j - 8 * b)))
    return res


class Group:
    def __init__(self, name):
        self.name = name
        if name == "S":
            self.nt = 16
            self.tile0 = 0
            self.klen = 2560
            self.rope = True
            self.ctx = True
            self.qblocks = [(qb * 512, 512, list(range(16)) + [16, 17, 18, 19]) for qb in range(4)]
            self.mod_row = 0
        else:
            self.nt = 8
            self.tile0 = 16
            self.klen = 1024
            self.rope = False
            self.ctx = False
            self.qblocks = [(j * 256, 256, [2 * j, 2 * j + 1]) for j in range(4)]
            self.mod_row = 1
        self.ntok = self.nt * 128
        self.nkt = self.klen // 128


class Builder:
    def __init__(self, cfg):
        self.cfg = cfg
        self.nc = bass.Bass("TRN2", target_bir_lowering=False)
        self.stack = ExitStack()

    def declare(self):
        nc = self.nc
        I = lambda n, s: nc.dram_tensor(n, list(s), F32, kind="ExternalInput").ap()
        O = lambda n, s: nc.dram_tensor(n, list(s), F32, kind="ExternalOutput").ap()
        d = {}
        d["xs"] = I("xs", (2048, D)); d["xp"] = I("xp", (1024, D))
        d["cnak"] = I("cnak", (2, 512, 1024)); d["cnav"] = I("cnav", (2, 512, 1024))
        d["cckv"] = I("cckv", (2, 512, 256)); d["ckpe"] = I("ckpe", (2, 512, 64))
        d["cdk"] = I("cdk", (2, 512, 2048)); d["cdv"] = I("cdv", (2, 512, 2048))
        d["condT"] = I("condT", (128, 32))
        d["w_ada"] = I("w_ada", (4, D, 3 * D)); d["b_ada"] = I("b_ada", (4, 3 * D))
        d["g_pre"] = I("g_pre", (4, D)); d["g_post"] = I("g_post", (4, D))
        d["w_in_even"] = I("w_in_even", (2, D, 5952)); d["w_out_even"] = I("w_out_even", (2, D, D))
        d["mla_g_q"] = I("mla_g_q", (2, 512)); d["mla_w_uq"] = I("mla_w_uq", (2, 512, 1536))
        d["mla_g_kv"] = I("mla_g_kv", (2, 256)); d["mla_w_ukv"] = I("mla_w_ukv", (2, 256, 2048))
        d["w_in_odd"] = I("w_in_odd", (2, D, 8192)); d["w_out_odd"] = I("w_out_odd", (2, D, D))
        d["diff_lambda"] = I("diff_lambda", (2, 512)); d["diff_g"] = I("diff_g", (2, 256))
        d["ident"] = I("ident", (128, 128))
        d["ropeD"] = I("ropeD", (128, 16 * 2 * 2 * 2 * 32)); d["ropeM"] = I("ropeM", (128, 16 * 2 * 2 * 2 * 16))
        d["natab"] = I("natab", (2, 8, 2, 128, 1408))
        d["y_p"] = O("y_p", (1024, D)); d["y_s"] = O("y_s", (2048, D))
        d["o_nak"] = O("o_nak", (4, 2, 256, 1024)); d["o_nav"] = O("o_nav", (4, 2, 256, 1024))
        d["o_ckv"] = O("o_ckv", (4, 2, 256, 256)); d["o_kpe"] = O("o_kpe", (4, 2, 256, 64))
        d["o_dk"] = O("o_dk", (4, 2, 256, 2048)); d["o_dv"] = O("o_dv", (4, 2, 256, 2048))
        d["xst"] = nc.dram_tensor("xst", [3072, D], F32, kind="Internal").ap()
        d["ysc"] = nc.dram_tensor("ysc", [3072, D], BF16, kind="Internal").ap()
        d["modv"] = nc.dram_tensor("modv", [4, 2, 3, D], F32, kind="Internal").ap()
        d["sgsc"] = nc.dram_tensor("sgsc", [3072, 1024], BF16, kind="Internal").ap()
        self.d = d

    def bcast_rows(self, ap_row, nparts=128):
        n = ap_row.shape[-1]
        return bass.AP(tensor=ap_row.tensor, offset=ap_row.offset, ap=[[0, nparts], [1, n]])

    def wview(self, w2d, c0, ncols, kchunks):
        return w2d[:, c0:c0 + ncols].rearrange("(kc p) n -> p kc n", p=128)

    def ps(self, i, n=512, dt=F32):
        t, b = self.PS[i]
        if dt == F32:
            return t[:, 0:n]
        return t[:].bitcast(BF16)[:, 0:n]

    def evac_eng(self):
        self.evi += 1
        return "act" if (self.evi % 2) else "dve"

    def rot(self, name, n):
        v = self.rotc.get(name, 0)
        self.rotc[name] = v + 1
        return v % n

    def psum_gen(self):
        return self.rot("psg", 4)

    def rsqrt(self, sc, out_t, ssq_t, epsn):
        pr = self.pr
        tmp = self.small(sc)
        pr.ts("dve", tmp.t[:, 0:1], ssq_t.t[:, 0:1], float(epsn), None, ALU.add, None, [ssq_t.b], [tmp.b])
        pr.tt("pool", out_t.t[:, 0:1], tmp.t[:, 0:1], self.cst.t[:, 0:1], ALU.pow, [tmp.b, self.cst.b], [out_t.b])

    def small(self, sc):
        return sc.small()

    @staticmethod
    def pipelined(n, first, second, depth=1):
        for t in range(min(depth, n)):
            first(t)
        for t in range(n):
            if t + depth < n:
                first(t + depth)
            second(t)

    def w_issue(self, req):
        pr = self.pr
        s = self.wslot_i % len(self.W)
        self.wslot_i += 1
        slot = self.W[s]
        pairs = []
        for (c0, src, kch) in req:
            n = src.shape[-1]
            pairs.append((slot.t[:, 0:kch, c0:c0 + n], src.rearrange("(kc p) n -> p kc n", p=128)))
        pr.dma("pool", pairs, slot.b, reads=(), writes=[slot.b])
        return slot

    def stream(self, reqs, consume):
        nW = len(self.W)
        slots = {}
        for j in range(min(nW - 1, len(reqs))):
            slots[j] = self.w_issue(reqs[j])
        for i in range(len(reqs)):
            j = i + nW - 1
            if j < len(reqs):
                slots[j] = self.w_issue(reqs[j])
            consume(i, slots.pop(i))

    def rope(self, sc, src_ap, src_buf, dst_ap, dst_buf, nb, hw, tab, tile_idx, dup_ap=None):
        pr = self.pr
        n = nb * 4 * hw
        ra = self.ra[self.rot("ra", len(self.ra))]
        rb = self.rb[self.rot("rb", len(self.rb))]
        v5 = lambda ap: ap.rearrange("p (n a h w) -> p n a h w", n=nb, a=2, h=2, w=hw)
        v3 = lambda ap: ap.rearrange("p (n f) -> p n f", n=nb)
        tC = tab.t[:, tile_idx, 0, :, :, :].rearrange("p a h w -> p (a h w)").unsqueeze(1).broadcast_to([128, nb, 4 * hw])
        tS = lambda h: tab.t[:, tile_idx, 1, :, h, :].unsqueeze(1).broadcast_to([128, nb, 2, hw])
        x5 = v5(src_ap)
        pr.tt("dve", v3(ra.t[:, 0:n]), v3(src_ap), tC, ALU.mult, [src_buf, tab.b], [ra.b])
        rb5 = v5(rb.t[:, 0:n])
        pr.tt("dve", rb5[:, :, :, 0, :], x5[:, :, :, 1, :], tS(0), ALU.mult, [src_buf, tab.b], [rb.b])
        pr.tt("dve", rb5[:, :, :, 1, :], x5[:, :, :, 0, :], tS(1), ALU.mult, [src_buf, tab.b], [rb.b])
        pr.tt("pool", dst_ap, ra.t[:, 0:n], rb.t[:, 0:n], ALU.add, [ra.b, rb.b], [dst_buf])
        if dup_ap is not None:
            pr.tt("pool", dup_ap, ra.t[:, 0:n], rb.t[:, 0:n], ALU.add, [ra.b, rb.b], [dst_buf])

    def attention(self, sc, qn, ktiles, st_parts, v_ap, v_reads, dvp1, scale, bias=None, o_base=4):
        return self.attention_multi([dict(qn=qn, ktiles=ktiles, st_parts=st_parts, v_ap=v_ap, v_reads=v_reads, bias=bias)],
                                    dvp1, scale, [o_base])[0]

    def attention_multi(self, jobs, dvp1, scale, o_bases):
        pr = self.pr
        J = []
        for jb, ob0 in zip(jobs, o_bases):
            nsub = jb["qn"] // 128
            J.append(dict(jb, nsub=nsub, nk=len(jb["ktiles"]), obanks=[self.PS[ob0 + s_] for s_ in range(nsub)], bis={}))

        def issue_st(j, idx, extra_reads=()):
            kt = j["ktiles"][idx]
            bi = self.psum_gen()
            pt_, pb = self.PS[bi]
            pairs, rds = j["st_parts"](kt)
            pr.mm(pt_[:, 0:j["qn"]], pairs, list(rds) + list(extra_reads), [pb])
            j["bis"][idx] = bi

        def issue_exp(j, idx):
            kt = j["ktiles"][idx]
            qn = j["qn"]
            pt_, pb = self.PS[j["bis"].pop(idx)]
            ptile = self.PT[self.rot("pt", len(self.PT))]
            bsl = j["bias"](kt) if j.get("bias") is not None else None
            if bsl is None:
                pr.act(ptile.t[:, 0:qn], pt_[:, 0:qn], AF.Exp, [pb], [ptile.b], scale=scale)
            else:
                bap, bbuf = bsl
                tmp = self.TMP[self.rot("tmp", len(self.TMP))]
                pr.stt("dve", tmp.t[:, 0:qn], pt_[:, 0:qn], scale, bap, ALU.mult, ALU.add, [pb, bbuf], [tmp.b])
                pr.act(ptile.t[:, 0:qn], tmp.t[:, 0:qn], AF.Exp, [tmp.b], [ptile.b])
            return ptile

        def issue_pv(j, idx, ptile):
            kt = j["ktiles"][idx]
            va = j["v_ap"](kt)
            obanks, nsub, nk = j["obanks"], j["nsub"], j["nk"]

            def fn(e):
                ins = None
                for s_ in range(nsub):
                    ins = e.matmul(obanks[s_][0][:, 0:dvp1], lhsT=ptile.t[:, s_ * 128:(s_ + 1) * 128], rhs=va,
                                   start=(idx == 0), stop=(idx == nk - 1))
                return ins
            pr.n_ins += nsub - 1
            pr.op("pe", fn, [ptile.b] + list(j["v_reads"]), [ob[1] for ob in obanks])

        nkmax = max(j["nk"] for j in J)
        if len(J) == 1:
            j = J[0]
            nk = j["nk"]
            issue_st(j, 0)
            if nk > 1:
                issue_st(j, 1)
            for idx in range(nk):
                ptile = issue_exp(j, idx)
                if idx + 2 < nk:
                    issue_st(j, idx + 2, extra_reads=[ptile.b])
                issue_pv(j, idx, ptile)
            return [j["obanks"]]
        for j in J:
            issue_st(j, 0)
        for idx in range(nkmax):
            for j in J:
                if idx + 1 < j["nk"]:
                    issue_st(j, idx + 1)
            pts = []
            for j in J:
                pts.append(issue_exp(j, idx) if idx < j["nk"] else None)
            for j, ptile in zip(J, pts):
                if ptile is not None:
                    issue_pv(j, idx, ptile)
        return [j["obanks"] for j in J]

    def mod_prepare(self, g0):
        pr, d = self.pr, self.d
        self.cb = g0.tile([128, 32], BF16, "cTb")
        with pr.scope() as sc:
            cT = sc.tile([128, 32], F32, "cT")
            th = sc.tile([128, 32], F32, "cth")
            pr.dma("sp", [(cT.t[:], d["condT"][:, :])], cT.b, (), [cT.b])
            pr.act(th.t[:], cT.t[:], AF.Tanh, [cT.b], [th.b], scale=0.5)
            pr.stt("dve", th.t[:], th.t[:], 1.0, cT.t[:], ALU.add, ALU.mult, [th.b, cT.b], [th.b])
            pr.ts("dve", self.cb.t[:], th.t[:], 0.5, None, ALU.mult, None, [th.b], [self.cb.b])

    def mod_stream(self, l, sc):
        pr, d = self.pr, self.d
        cb3 = self.cb.t[:].rearrange("p (k r) -> p k r", r=2)
        bt = sc.tile([2, 512], F32, "mbt")
        gt = sc.tile([2, 512], F32, "mgt")
        vt = sc.tile([2, 512], F32, "mvt")
        sq = float(math.sqrt(D))
        reqs = [[(0, d["w_ada"][l][:, c * 512:(c + 1) * 512], KC)] for c in range(12)]
        nW = len(self.W)
        slots = {}
        for j in range(nW - 1):
            slots[j] = self.w_issue(reqs[j])
        state = {"i": 0}

        def step():
            c = state["i"]
            if c >= 12:
                return
            state["i"] += 1
            j = c + nW - 1
            if j < 12:
                slots[j] = self.w_issue(reqs[j])
            slot = slots.pop(c)
            which, jb = c // 4, c % 4
            cs = slice(jb * 512, (jb + 1) * 512)
            pr.dma("sp", [(bt.t[:], self.bcast_rows(d["b_ada"][l, c * 512:(c + 1) * 512], 2))], bt.b, (), [bt.b])
            if which == 1:
                pr.dma("sp", [(gt.t[:], self.bcast_rows(d["g_pre"][l, cs], 2))], gt.b, (), [gt.b])
            elif which == 2:
                pr.dma("sp", [(gt.t[:], self.bcast_rows(d["g_post"][l, cs], 2))], gt.b, (), [gt.b])
            bi = self.psum_gen()
            pt_, pb = self.PS[bi]
            pr.mm(pt_[0:2, 0:512], [(cb3[:, kc, :], slot.t[:, kc, 0:512]) for kc in range(KC)], [self.cb.b, slot.b], [pb])
            pr.tt("dve", vt.t[:], pt_[0:2, 0:512], bt.t[:], ALU.add, [pb, bt.b], [vt.b])
            if which == 0:
                w_ = 1
            elif which == 1:
                pr.stt("dve", vt.t[:], vt.t[:], 1.0, gt.t[:], ALU.add, ALU.mult, [vt.b, gt.b], [vt.b])
                pr.ts("dve", vt.t[:], vt.t[:], sq, None, ALU.mult, None, [vt.b], [vt.b])
                w_ = 0
            else:
                pr.stt("dve", vt.t[:], vt.t[:], sq, gt.t[:], ALU.mult, ALU.mult, [vt.b, gt.b], [vt.b])
                w_ = 2
            pr.dma("sp", [(d["modv"][l, :, w_, cs], vt.t[:])], vt.b, [vt.b], [self.modv_b[l]])
        return step

    def load_mod(self, sc, l, g, which, name):
        pr, d = self.pr, self.d
        t = sc.tile([128, D], F32, name)
        pr.dma("sp", [(t.t[:], self.bcast_rows(d["modv"][l, g.mod_row, which, :]))], t.b, [self.modv_b[l]], [t.b])
        return t

    def phase_A(self, l, g):
        pr, d = self.pr, self.d
        with pr.scope() as sc:
            xts = [sc.tile([128, D], F32, "xt") for _ in range(3)]
            junk = sc.tile([128, D], BF16, "junk")
            t1s = [sc.tile([128, D], F32, "t1") for _ in range(1)]
            hbs = [sc.tile([128, D], BF16, "hb") for _ in range(2)]
            rst = {}
            ssqs = {}

            def stage1(t):
                T = g.tile0 + t
                xt = xts[t % 3]
                if l == self.layers[0]:
                    src = d["xs"][t * 128:(t + 1) * 128, :] if g.name == "S" else d["xp"][t * 128:(t + 1) * 128, :]
                    pr.dma("sp", [(xt.t[:], src)], xt.b, (), [xt.b])
                else:
                    pr.dma("sp", [(xt.t[:], d["xst"][T * 128:(T + 1) * 128, :])], xt.b, [self.xst_b[T]], [xt.b])
                ssq = self.small(sc)
                pr.act(junk.t[:], xt.t[:], AF.Square, [xt.b], [junk.b, ssq.b], accum_out=ssq.t[:, 0:1])
                ssqs[t] = ssq

            def stage1b(t):
                rstd = self.small(sc)
                self.rsqrt(sc, rstd, ssqs.pop(t), D * EPS)
                rst[t] = rstd

            def stage2(t):
                xt = xts[t % 3]
                rstd = rst.pop(t)
                t1 = t1s[0]
                hb = hbs[t % 2]
                pr.stt("dve", t1.t[:], xt.t[:], rstd.t[:, 0:1], mA.t[:], ALU.mult, ALU.mult,
                       [xt.b, rstd.b, mA.b], [t1.b])
                pr.tt("dve", hb.t[:], t1.t[:], mB.t[:], ALU.add, [t1.b, mB.b], [hb.b])
                for half in range(2):
                    bi = self.psum_gen()
                    pt_, pb = self.PS[bi]
                    pbf = self.ps(bi, 1024, BF16)
                    pr.transposes([(pbf[:, j * 128:(j + 1) * 128], hb.t[:, (half * 8 + j) * 128:(half * 8 + j + 1) * 128])
                                   for j in range(8)], self.ident.t[:], [hb.b, self.ident.b], [pb])
                    pr.copy("act", self.hT.t[:, half * 8:(half + 1) * 8, t * 128:(t + 1) * 128],
                            pbf.rearrange("p (a b) -> p a b", a=8), [pb], [self.hT_b[t]])

            stage1(0)
            stage1b(0)
            stage1(1)
            stage1b(1)
            mA = self.load_mod(sc, l, g, 0, "mA")
            mB = self.load_mod(sc, l, g, 1, "mB")
            for t in range(g.nt):
                if t + 2 < g.nt:
                    stage1(t + 2)
                stage2(t)
                if t + 2 < g.nt:
                    stage1b(t + 2)

    def issue_wout(self, l):
        if self.wout_done.get(l):
            return
        self.wout_done[l] = True
        pr, d = self.pr, self.d
        wout = d["w_out_even"][l // 2] if l % 2 == 0 else d["w_out_odd"][l // 2]
        for c in range(4):
            pr.dma("pool", [(self.hT.t[:, :, c * 512:(c + 1) * 512],
                             wout[:, c * 512:(c + 1) * 512].rearrange("(kc p) n -> p kc n", p=128))],
                   self.hT_b[4 * c], (), self.hT_b[4 * c:4 * c + 4])

    def phase_C(self, l):
        pr, d = self.pr, self.d
        last = (l == self.layers[-1])
        wout = d["w_out_even"][l // 2] if l % 2 == 0 else d["w_out_odd"][l // 2]
        with pr.scope() as sc:
            self.issue_wout(l)
            Gt = sc.tile([128, D], F32, "mG")
            ysl = [sc.tile([128, D], BF16, "ysl") for _ in range(1)]
            xts = [sc.tile([128, D], F32, "xt") for _ in range(2)]
            yTs = [sc.tile([128, KC, 128], BF16, "yTt") for _ in range(2)]
            us = [sc.tile([128, D], F32, "u") for _ in range(2)]
            junk = sc.tile([128, 512], BF16, "junk")
            ssq4s = [sc.tile([128, 4], F32, "ssq4") for _ in range(2)]

            def prep(T):
                g = self.gS if T < 16 else self.gP
                t = T - g.tile0
                yl = ysl[0]
                xt = xts[T % 2]
                yT = yTs[T % 2]
                pr.dma("sp", [(yl.t[:], d["ysc"][T * 128:(T + 1) * 128, :])], yl.b, [self.ysc_b[T]], [yl.b])
                if l == self.layers[0]:
                    src = d["xs"][t * 128:(t + 1) * 128, :] if g.name == "S" else d["xp"][t * 128:(t + 1) * 128, :]
                    pr.dma("sp", [(xt.t[:], src)], xt.b, (), [xt.b])
                else:
                    pr.dma("sp", [(xt.t[:], d["xst"][T * 128:(T + 1) * 128, :])], xt.b, [self.xst_b[T]], [xt.b])
                for half in range(2):
                    bi = self.psum_gen()
                    pt_, pb = self.PS[bi]
                    pbf = self.ps(bi, 1024, BF16)
                    pr.transposes([(pbf[:, j * 128:(j + 1) * 128], yl.t[:, (half * 8 + j) * 128:(half * 8 + j + 1) * 128])
                                   for j in range(8)], self.ident.t[:], [yl.b, self.ident.b], [pb])
                    pr.copy("act", yT.t[:, half * 8:(half + 1) * 8, :], pbf.rearrange("p (a b) -> p a b", a=8),
                            [pb], [yT.b])

            mstep = None
            li = self.layers.index(l)
            if li + 1 < len(self.layers):
                mstep = self.mod_stream(self.layers[li + 1], sc)
            prep(0)
            for T in range(24):
                g = self.gS if T < 16 else self.gP
                t = T - g.tile0
                xt = xts[T % 2]
                yT = yTs[T % 2]
                u = us[T % 2]
                if mstep is not None and T % 2 == 1:
                    mstep()
                if t == 0:
                    pr.dma("sp", [(Gt.t[:], self.bcast_rows(d["modv"][l, g.mod_row, 2, :]))], Gt.b, [self.modv_b[l]], [Gt.b])
                ssq4 = ssq4s[T % 2]
                for c in range(4):
                    ot, ob = self.PS[4 + c]
                    pr.mm(ot[:, 0:512], [(yT.t[:, kc, :], self.hT.t[:, kc, c * 512:(c + 1) * 512]) for kc in range(KC)],
                          [yT.b] + self.hT_b[4 * c:4 * c + 4], [ob])
                    pr.act(junk.t[:, 0:512], ot[:, 0:512], AF.Square, [ob], [junk.b, ssq4.b],
                           accum_out=ssq4.t[:, c:c + 1])
                if T + 1 < 24:
                    prep(T + 1)
                ssq = self.small(sc)
                rstd = self.small(sc)
                pr.op("dve", lambda e, ssq=ssq, ssq4=ssq4: e.reduce_sum(out=ssq.t[:, 0:1], in_=ssq4.t[:, 0:4],
                                                                        axis=mybir.AxisListType.X), [ssq4.b], [ssq.b])
                self.rsqrt(sc, rstd, ssq, D * EPS)
                G = Gt
                for c in range(4):
                    ot, ob = self.PS[4 + c]
                    sl = slice(c * 512, (c + 1) * 512)
                    pr.stt("dve", u.t[:, sl], ot[:, 0:512], rstd.t[:, 0:1], G.t[:, sl], ALU.mult, ALU.mult,
                           [ob, rstd.b, G.b], [u.b])
                pr.tt("pool", u.t[:], u.t[:], xt.t[:], ALU.add, [u.b, xt.b], [u.b])
                if last:
                    dst = d["y_s"][t * 128:(t + 1) * 128, :] if g.name == "S" else d["y_p"][t * 128:(t + 1) * 128, :]
                    pr.dma("sp", [(dst, u.t[:])], u.b, [u.b], ())
                else:
                    pr.dma("sp", [(d["xst"][T * 128:(T + 1) * 128, :], u.t[:])], u.b, [u.b], [self.xst_b[T]])

    def phase_B_odd(self, l, g):
        pr, d = self.pr, self.d
        i = l // 2
        lam_init = 0.8 - 0.6 * math.exp(-0.3 * l)
        w = d["w_in_odd"][i]
        if self.cfg.get("dbg", 99) < 1:
            return
        with pr.scope() as sc:
            s2 = sc.tile([128, 2], F32, "s2")
            with pr.scope() as sl:
                lp = sl.tile([128, 4, 128], F32, "lp")
                pr.dma("sp", [(lp.t[:].rearrange("p a b -> p (a b)"), self.bcast_rows(d["diff_lambda"][i, :]))], lp.b, (), [lp.b])
                prod = sl.tile([128, 2, 128], F32, "prod")
                lp4 = lp.t[:].rearrange("p (a two) b -> p a two b", two=2)
                pr.tt("dve", prod.t[:], lp4[:, :, 0, :], lp4[:, :, 1, :], ALU.mult, [lp.b], [prod.b])
                pr.op("dve", lambda e: e.reduce_sum(out=s2.t[:], in_=prod.t[:], axis=mybir.AxisListType.X), [prod.b], [s2.b])
            e2 = sc.tile([128, 2], F32, "e2")
            pr.act(e2.t[:], s2.t[:], AF.Exp, [s2.b], [e2.b])
            nlam = sc.tile([128, 2], F32, "nlam")
            pr.tt("dve", nlam.t[:, 0:1], e2.t[:, 1:2], e2.t[:, 0:1], ALU.subtract, [e2.b], [nlam.b])
            pr.ts("dve", nlam.t[:, 1:2], nlam.t[:, 0:1], -lam_init, None, ALU.add, None, [nlam.b], [nlam.b])
            gsub = sc.tile([128, 256], F32, "gsub")
            pr.dma("sp", [(gsub.t[:], self.bcast_rows(d["diff_g"][i, :]))], gsub.b, (), [gsub.b])
            pr.ts("dve", gsub.t[:], gsub.t[:], (1.0 - lam_init) * 0.5 * 16.0, None, ALU.mult, None, [gsub.b], [gsub.b])
            QKT = sc.tile([128, 4, g.klen], BF16, "QKT")
            QKT_q = sc.buf("QKT_q"); QKT_k = sc.buf("QKT_k")
            VA = sc.tile([128, g.nkt, 264], BF16, "VA")
            SG = sc.tile([128, g.nt, 256], BF16, "SG")
            pr.memset("dve", VA.t[:, :, 256:257], 1.0, [VA.b])
            qkb = [sc.tile([128, 512], BF16, "qkb") for _ in range(3)]
            thb = [sc.tile([128, 256], F32, "thb") for _ in range(1)]
            oraw = [sc.tile([128, 4, 264], F32, "oraw") for _ in range(2)]
            orb = [[sc.buf(f"orb{s_}{j}") for j in range(4)] for s_ in range(2)]
            yst = [sc.tile([128, 256], BF16, "yst") for _ in range(3)]
            junks = [sc.tile([128, 256], BF16, "junk") for _ in range(2)]
            kvst = [sc.tile([128, 512], F32, "kvst") for _ in range(2)] if not g.ctx else []
            kctx = sc.tile([128, 4, 256], BF16, "kctx") if g.ctx else None

            reqs = []
            for h in range(8):
                reqs.append([(0, w[:, h * 256:(h + 1) * 256], KC), (256, w[:, 2048 + h * 256:2048 + (h + 1) * 256], KC)])
                reqs.append([(0, w[:, 4096 + h * 256:4096 + (h + 1) * 256], KC),
                             (256, w[:, 6144 + h * 256:6144 + (h + 1) * 256], KC)])

            dbg = self.cfg.get("dbg", 99)

            def consume(ri, slot):
                h, kind = ri // 2, ri % 2
                if dbg < 2 or (dbg < 3 and kind == 1) or dbg == 5:
                    return
                if kind == 0:
                    if g.ctx and not self.cfg.get("no_ctx"):
                        pr.dma("pool", [(kctx.t[:], d["cdk"][i][:, h * 256:(h + 1) * 256].rearrange("(c p) n -> p c n", p=128))],
                               kctx.b, (), [kctx.b])
                        pr.dma("pool", [(VA.t[:, 16:20, 0:256],
                                         d["cdv"][i][:, h * 256:(h + 1) * 256].rearrange("(c p) n -> p c n", p=128))],
                               VA.b, (), [VA.b])
                    def first(t):
                        bi = self.psum_gen()
                        pt_, pb = self.PS[bi]
                        pr.mm(pt_[:, 0:512], [(self.hT.t[:, kc, t * 128:(t + 1) * 128], slot.t[:, kc, 0:512]) for kc in range(KC)],
                              [self.hT_b[t], slot.b], [pb])
                        qb_ = qkb[t % 3]
                        if g.rope:
                            self.rope(sc, pt_[:, 0:512], pb, qb_.t[:], qb_.b, 4, 32, self.ropeD, t)
                        else:
                            pr.copy("act", qb_.t[:], pt_[:, 0:512], [pb], [qb_.b])
                            ks = kvst[self.rot("kvst", 2)]
                            pr.copy("act", ks.t[:, 0:256], pt_[:, 256:512], [pb], [ks.b])
                            sq_, tt_ = t // 2, (t % 2) * 128
                            pr.dma("sp", [(d["o_dk"][sq_, i, tt_:tt_ + 128, h * 256:(h + 1) * 256], ks.t[:, 0:256])],
                                   ks.b, [ks.b], ())

                    def second(t):
                        qb_ = qkb[t % 3]
                        bj = self.psum_gen()
                        pj, pjb = self.PS[bj]
                        pbf = self.ps(bj, 512, BF16)
                        pr.transposes([(pbf[:, j * 128:(j + 1) * 128], qb_.t[:, j * 128:(j + 1) * 128]) for j in range(4)],
                                      self.ident.t[:], [qb_.b, self.ident.b], [pjb])
                        pr.copy("act", QKT.t[:, :, t * 128:(t + 1) * 128], pbf.rearrange("p (a b) -> p a b", a=4),
                                [pjb], [QKT_q, QKT_k])
                    self.pipelined(g.nt, first, second, depth=2)
                    if g.ctx and not self.cfg.get("no_ctx"):
                        bj = self.psum_gen()
                        pj, pjb = self.PS[bj]
                        pbf = self.ps(bj, 1024, BF16)
                        pr.transposes([(pbf[:, (s * 4 + c) * 128:(s * 4 + c + 1) * 128], kctx.t[:, c, s * 128:(s + 1) * 128])
                                       for s in range(2) for c in range(4)], self.ident.t[:], [kctx.b, self.ident.b], [pjb])
                        pr.copy(self.evac_eng(), QKT.t[:, 2:4, 2048:2560], pbf.rearrange("p (a b) -> p a b", a=2),
                                [pjb], [QKT_k])
                else:
                    for t in range(g.nt):
                        bi = self.psum_gen()
                        pt_, pb = self.PS[bi]
                        pr.mm(pt_[:, 0:512], [(self.hT.t[:, kc, t * 128:(t + 1) * 128], slot.t[:, kc, 0:512]) for kc in range(KC)],
                              [self.hT_b[t], slot.b], [pb])
                        pr.copy("act", VA.t[:, t, 0:256], pt_[:, 0:256], [pb], [VA.b])
                        th_ = thb[0]
                        pr.act(th_.t[:], pt_[:, 256:512], AF.Tanh, [pb], [th_.b], scale=0.5)
                        pr.stt("dve", SG.t[:, t, :], th_.t[:], 1.0, pt_[:, 256:512], ALU.add, ALU.mult, [th_.b, pb], [SG.b])
                        if not g.ctx:
                            ks = kvst[self.rot("kvst", 2)]
                            pr.copy("act", ks.t[:, 0:256], pt_[:, 0:256], [pb], [ks.b])
                            sq_, tt_ = t // 2, (t % 2) * 128
                            pr.dma("sp", [(d["o_dv"][sq_, i, tt_:tt_ + 128, h * 256:(h + 1) * 256], ks.t[:, 0:256])],
                                   ks.b, [ks.b], ())
                    if g.name == "P" and h == 7:
                        self.issue_wout(l)
                    for (q0, qn, ktiles) in (g.qblocks if dbg >= 4 else []):
                        nsub = qn // 128
                        def mkjob(s, q0=q0, qn=qn, ktiles=ktiles):
                            def st_parts(kt):
                                return ([(QKT.t[:, 2 + s, kt * 128:(kt + 1) * 128], QKT.t[:, s, q0:q0 + qn])], [QKT_q, QKT_k])
                            return dict(qn=qn, ktiles=ktiles, st_parts=st_parts, v_ap=lambda kt: VA.t[:, kt, 0:257],
                                        v_reads=[VA.b], bias=None)
                        if nsub <= 2:
                            obs = self.attention_multi([mkjob(0), mkjob(1)], 257, DIFF_SCALE, [4, 6])
                        else:
                            obs = [self.attention_multi([mkjob(s)], 257, DIFF_SCALE, [4])[0] for s in range(1)]
                        for s in range(2):
                            if nsub > 2:
                                ob = obs[0] if s == 0 else self.attention_multi([mkjob(1)], 257, DIFF_SCALE, [4])[0]
                            else:
                                ob = obs[s]
                            raw = oraw[s]
                            for sub in range(nsub):
                                ot, obuf = ob[sub]
                                pr.copy("dve", raw.t[:, sub, 0:257], ot[:, 0:257], [obuf], [orb[s][sub]])
                        subs = list(range(nsub))
                        B1 = [orb[0][j] for j in subs]; B2 = [orb[1][j] for j in subs]
                        O1 = [oraw[0].t[:, j, :] for j in subs]; O2 = [oraw[1].t[:, j, :] for j in subs]
                        R1 = [self.small(sc) for _ in subs]; R2 = [self.small(sc) for _ in subs]
                        SS = [self.small(sc) for _ in subs]; TM = [self.small(sc) for _ in subs]; RS = [self.small(sc) for _ in subs]
                        for j in subs:
                            pr.op("dve", lambda e, r=R1[j], o1=O1[j]: e.reciprocal(out=r.t[:, 0:1], in_=o1[:, 256:257]), [B1[j]], [R1[j].b])
                        for j in subs:
                            pr.op("dve", lambda e, r=R2[j], o2=O2[j]: e.reciprocal(out=r.t[:, 0:1], in_=o2[:, 256:257]), [B2[j]], [R2[j].b])
                        for j in subs:
                            pr.ts("dve", O1[j][:, 0:256], O1[j][:, 0:256], R1[j].t[:, 0:1], None, ALU.mult, None, [B1[j], R1[j].b], [B1[j]])
                        for j in subs:
                            pr.tt("dve", R2[j].t[:, 1:2], R2[j].t[:, 0:1], nlam.t[:, 1:2], ALU.mult, [R2[j].b, nlam.b], [R2[j].b])
                        for j in subs:
                            pr.stt("dve", O2[j][:, 0:256], O2[j][:, 0:256], R2[j].t[:, 1:2], O1[j][:, 0:256], ALU.mult, ALU.add,
                                   [B2[j], R2[j].b, B1[j]], [B2[j]])
                        for j in subs:
                            jk = junks[j % 2]
                            pr.stt("dve", jk.t[:], O2[j][:, 0:256], 1.0, O2[j][:, 0:256], ALU.mult, ALU.mult, [B2[j]], [jk.b, SS[j].b],
                                   accum_out=SS[j].t[:, 0:1])
                        for j in subs:
                            pr.ts("dve", TM[j].t[:, 0:1], SS[j].t[:, 0:1], float(256 * EPS), None, ALU.add, None, [SS[j].b], [TM[j].b])
                        for j in subs:
                            pr.tt("pool", RS[j].t[:, 0:1], TM[j].t[:, 0:1], self.cst.t[:, 0:1], ALU.pow, [TM[j].b, self.cst.b], [RS[j].b])
                        for j in subs:
                            pr.stt("dve", O2[j][:, 0:256], O2[j][:, 0:256], RS[j].t[:, 0:1], gsub.t[:], ALU.mult, ALU.mult,
                                   [B2[j], RS[j].b, gsub.b], [B2[j]])
                        for j in subs:
                            tq = q0 // 128 + j
                            ys_ = yst[self.rot("yst", 3)]
                            pr.tt("dve", ys_.t[:], O2[j][:, 0:256], SG.t[:, tq, :], ALU.mult, [B2[j], SG.b], [ys_.b])
                            T = g.tile0 + tq
                            pr.dma("sp", [(d["ysc"][T * 128:(T + 1) * 128, h * 256:(h + 1) * 256], ys_.t[:])],
                                   ys_.b, [ys_.b], [self.ysc_b[T]])
            if dbg != 1:
                self.stream(reqs, consume)

    def phase_B_even(self, l, g):
        self.na_stage(l, g)
        self.mla_stage(l, g)

    def na_stage(self, l, g):
        pr, d = self.pr, self.d
        i = l // 2
        w = d["w_in_even"][i]
        with pr.scope() as sc:
            QaT = sc.tile([128, g.ntok], BF16, "QaT")
            KaT = sc.tile([128, g.klen], BF16, "KaT")
            VA = sc.tile([128, g.nkt, 136], BF16, "VAa")
            SG = sc.tile([128, g.nt, 128], BF16, "SGa")
            pr.memset("dve", VA.t[:, :, 128:129], 2.0, [VA.b])
            thb = [sc.tile([128, 128], F32, "thb") for _ in range(2)]
            yst = [sc.tile([128, 128], BF16, "yst") for _ in range(3)]
            oraw = sc.tile([128, 4, 132], F32, "oraw")
            orb = [sc.buf(f"orb{j}") for j in range(4)]
            kvst = [sc.tile([128, 256], F32, "kvst") for _ in range(2)] if not g.ctx else []
            kctx = sc.tile([128, 4, 128], BF16, "kctx") if g.ctx else None
            tabs = [sc.tile([128, 2, 1408], F32, "natab") for _ in range(2)] if g.ctx else []
            self.TMP = [sc.tile([128, 512], F32, "TMP") for _ in range(2)]
            reqs = [[(j * 128, w[:, j * 1024 + h * 128:j * 1024 + (h + 1) * 128], KC) for j in range(4)] for h in range(8)]

            def consume(h, slot):
                tab = None
                if g.ctx:
                    tab = tabs[h % 2]
                    pr.dma("sp", [(tab.t[:], d["natab"][i, h].rearrange("v p n -> p v n"))], tab.b, (), [tab.b])
                    pr.dma("pool", [(kctx.t[:], d["cnak"][i][:, h * 128:(h + 1) * 128].rearrange("(c p) n -> p c n", p=128))],
                           kctx.b, (), [kctx.b])
                    pr.dma("pool", [(VA.t[:, 16:20, 0:128],
                                     d["cnav"][i][:, h * 128:(h + 1) * 128].rearrange("(c p) n -> p c n", p=128))],
                           VA.b, (), [VA.b])
                for which, dst in ((0, QaT), (1, KaT)):
                    for qb in range(g.ntok // 512):
                        bi = self.psum_gen()
                        pt_, pb = self.PS[bi]
                        pr.mm(pt_[:, 0:512], [(slot.t[:, kc, which * 128:(which + 1) * 128], self.hT.t[:, kc, qb * 512:(qb + 1) * 512])
                                               for kc in range(KC)], [slot.b] + self.hT_b[4 * qb:4 * qb + 4], [pb])
                        pr.copy(self.evac_eng(), dst.t[:, qb * 512:(qb + 1) * 512], pt_[:, 0:512], [pb], [dst.b])
                if g.ctx:
                    bj = self.psum_gen()
                    pj, pjb = self.PS[bj]
                    pbf = self.ps(bj, 512, BF16)
                    pr.transposes([(pbf[:, c * 128:(c + 1) * 128], kctx.t[:, c, :]) for c in range(4)],
                                  self.ident.t[:], [kctx.b, self.ident.b], [pjb])
                    pr.copy(self.evac_eng(), KaT.t[:, 2048:2560], pbf, [pjb], [KaT.b])
                for t in range(g.nt):
                    bi = self.psum_gen()
                    pt_, pb = self.PS[bi]
                    c0 = 256 if g.ctx else 128
                    n = 512 - c0
                    pr.mm(pt_[:, 0:n], [(self.hT.t[:, kc, t * 128:(t + 1) * 128], slot.t[:, kc, c0:512]) for kc in range(KC)],
                          [self.hT_b[t], slot.b], [pb])
                    vo = n - 256
                    pr.copy("act", VA.t[:, t, 0:128], pt_[:, vo:vo + 128], [pb], [VA.b])
                    th_ = thb[t % 2]
                    pr.act(th_.t[:], pt_[:, vo + 128:vo + 256], AF.Tanh, [pb], [th_.b], scale=0.5)
                    pr.stt("dve", SG.t[:, t, :], th_.t[:], 1.0, pt_[:, vo + 128:vo + 256], ALU.add, ALU.mult, [th_.b, pb], [SG.b])
                    if not g.ctx:
                        ks = kvst[self.rot("kvst", 2)]
                        pr.copy("act", ks.t[:, 0:256], pt_[:, 0:256], [pb], [ks.b])
                        sq_, tt_ = t // 2, (t % 2) * 128
                        pr.dma("sp", [(d["o_nak"][sq_, i, tt_:tt_ + 128, h * 128:(h + 1) * 128], ks.t[:, 0:128]),
                                      (d["o_nav"][sq_, i, tt_:tt_ + 128, h * 128:(h + 1) * 128], ks.t[:, 128:256])],
                               ks.b, [ks.b], ())
                groups = [[b] for b in range(4)] if g.ctx else [[0, 1], [2, 3]]
                for grp in groups:
                    jobs = []
                    for bq in grp:
                        q0, qn, ktiles = g.qblocks[bq]
                        bias = None
                        if g.ctx:
                            wt = na_window_tiles(bq)
                            ktiles = [16, 17, 18, 19] + [j for (j, _, _) in wt]
                            info = {j: (v, t0) for (j, v, t0) in wt}

                            def bias(kt, info=info, tab=tab):
                                if kt >= 16:
                                    return None
                                v, t0 = info[kt]
                                return (tab.t[:, v, t0 * 64:t0 * 64 + 512], tab.b)

                        def st_parts(kt, q0=q0, qn=qn):
                            return ([(KaT.t[:, kt * 128:(kt + 1) * 128], QaT.t[:, q0:q0 + qn])], [KaT.b, QaT.b])
                        jobs.append(dict(qn=qn, ktiles=ktiles, st_parts=st_parts, v_ap=lambda kt: VA.t[:, kt, 0:129],
                                         v_reads=[VA.b], bias=bias))
                    obs = self.attention_multi(jobs, 129, NA_SCALE, [4, 6][:len(jobs)])
                    idxs = []
                    for ji, bq in enumerate(grp):
                        q0, qn, _ = g.qblocks[bq]
                        for sub in range(qn // 128):
                            ot, obuf = obs[ji][sub]
                            oi = ji * 2 + sub if len(grp) > 1 else sub
                            pr.copy("dve", oraw.t[:, oi, 0:129], ot[:, 0:129], [obuf], [orb[oi]])
                            idxs.append((oi, q0 // 128 + sub))
                    for (oi, tq) in idxs:
                        r = self.small(sc)
                        o_ = oraw.t[:, oi, :]
                        pr.op("dve", lambda e, r=r, o_=o_: e.reciprocal(out=r.t[:, 0:1], in_=o_[:, 128:129]), [orb[oi]], [r.b])
                        ys_ = yst[self.rot("yst", 3)]
                        pr.stt("dve", ys_.t[:], o_[:, 0:128], r.t[:, 0:1], SG.t[:, tq, :], ALU.mult, ALU.mult,
                               [orb[oi], r.b, SG.b], [ys_.b])
                        T = g.tile0 + tq
                        pr.dma("sp", [(d["ysc"][T * 128:(T + 1) * 128, h * 128:(h + 1) * 128], ys_.t[:])],
                               ys_.b, [ys_.b], [self.ysc_b[T]])
            self.stream(reqs, consume)
            for x in yst + kvst + tabs + ([kctx] if kctx else []) + [VA]:
                pr.release_dma(x.b)

    def mla_stage(self, l, g):
        pr, d = self.pr, self.d
        i = l // 2
        w = d["w_in_even"][i]
        with pr.scope() as sc:
            cqT = sc.tile([128, 4, g.ntok], BF16, "cqT")
            CKT = sc.tile([128, 3, g.klen], BF16, "CKT")
            gqb = sc.tile([128, 512], F32, "gqb")
            gkvb = sc.tile([128, 256], F32, "gkvb")
            pr.dma("sp", [(gqb.t[:], self.bcast_rows(d["mla_g_q"][i, :]))], gqb.b, (), [gqb.b])
            pr.ts("dve", gqb.t[:], gqb.t[:], float(math.sqrt(512.0)), None, ALU.mult, None, [gqb.b], [gqb.b])
            pr.dma("sp", [(gkvb.t[:], self.bcast_rows(d["mla_g_kv"][i, :]))], gkvb.b, (), [gkvb.b])
            pr.ts("dve", gkvb.t[:], gkvb.t[:], 16.0, None, ALU.mult, None, [gkvb.b], [gkvb.b])
            with pr.scope() as s1:
                junk = s1.tile([128, 512], BF16, "junk")
                cqn = [s1.tile([128, 512], BF16, "cqn") for _ in range(3)]
                cat = [s1.tile([128, 384], BF16, "cat") for _ in range(3)]
                ckf = [s1.tile([128, 320], F32, "ckf") for _ in range(2)] if not g.ctx else []
                thb = [s1.tile([128, 512], F32, "thb") for _ in range(2)]
                sgst = [s1.tile([128, 512], BF16, "sgst") for _ in range(2)]
                reqs = [[(0, w[:, 4096:4608], KC)], [(0, w[:, 4608:4928], KC)],
                        [(0, w[:, 4928:5440], KC)], [(0, w[:, 5440:5952], KC)]]

                def consume(ri, slot):
                    n = 320 if ri == 1 else 512

                    def proj(t):
                        bi = self.psum_gen()
                        pt_, pb = self.PS[bi]
                        pr.mm(pt_[:, 0:n], [(self.hT.t[:, kc, t * 128:(t + 1) * 128], slot.t[:, kc, 0:n]) for kc in range(KC)],
                              [self.hT_b[t], slot.b], [pb])
                        return pt_, pb

                    if ri == 0:
                        def first(t):
                            pt_, pb = proj(t)
                            ssq = self.small(s1); rs = self.small(s1)
                            pr.act(junk.t[:], pt_[:, 0:512], AF.Square, [pb], [junk.b, ssq.b], accum_out=ssq.t[:, 0:1])
                            self.rsqrt(s1, rs, ssq, 512 * EPS)
                            cq_ = cqn[t % 3]
                            pr.stt("dve", cq_.t[:], pt_[:, 0:512], rs.t[:, 0:1], gqb.t[:], ALU.mult, ALU.mult,
                                   [pb, rs.b, gqb.b], [cq_.b])

                        def second(t):
                            cq_ = cqn[t % 3]
                            bj = self.psum_gen()
                            pj, pjb = self.PS[bj]
                            pbf = self.ps(bj, 512, BF16)
                            pr.transposes([(pbf[:, j * 128:(j + 1) * 128], cq_.t[:, j * 128:(j + 1) * 128]) for j in range(4)],
                                          self.ident.t[:], [cq_.b, self.ident.b], [pjb])
                            pr.copy("act", cqT.t[:, :, t * 128:(t + 1) * 128], pbf.rearrange("p (a b) -> p a b", a=4),
                                    [pjb], [cqT.b])
                        self.pipelined(g.nt, first, second, depth=2)
                    elif ri == 1:
                        def first(t):
                            pt_, pb = proj(t)
                            ssq = self.small(s1); rs = self.small(s1)
                            pr.act(junk.t[:, 0:256], pt_[:, 0:256], AF.Square, [pb], [junk.b, ssq.b], accum_out=ssq.t[:, 0:1])
                            self.rsqrt(s1, rs, ssq, 256 * EPS)
                            ct_ = cat[t % 3]
                            if g.ctx:
                                pr.stt("dve", ct_.t[:, 0:256], pt_[:, 0:256], rs.t[:, 0:1], gkvb.t[:], ALU.mult, ALU.mult,
                                       [pb, rs.b, gkvb.b], [ct_.b])
                                self.rope(s1, pt_[:, 256:320], pb, ct_.t[:, 256:320], ct_.b, 1, 16, self.ropeM, t,
                                          dup_ap=ct_.t[:, 320:384])
                            else:
                                cf = ckf[t % 2]
                                pr.stt("dve", cf.t[:, 0:256], pt_[:, 0:256], rs.t[:, 0:1], gkvb.t[:], ALU.mult, ALU.mult,
                                       [pb, rs.b, gkvb.b], [cf.b])
                                pr.copy("dve", cf.t[:, 256:320], pt_[:, 256:320], [pb], [cf.b])
                                pr.copy("act", ct_.t[:, 0:320], cf.t[:, 0:320], [cf.b], [ct_.b])
                                pr.copy("act", ct_.t[:, 320:384], cf.t[:, 256:320], [cf.b], [ct_.b])
                                sq_, tt_ = t // 2, (t % 2) * 128
                                pr.dma("sp", [(d["o_ckv"][sq_, i, tt_:tt_ + 128, :], cf.t[:, 0:256]),
                                              (d["o_kpe"][sq_, i, tt_:tt_ + 128, :], cf.t[:, 256:320])],
                                       cf.b, [cf.b], ())

                        def second(t):
                            ct_ = cat[t % 3]
                            bj = self.psum_gen()
                            pj, pjb = self.PS[bj]
                            pbf = self.ps(bj, 384, BF16)
                            pr.transposes([(pbf[:, j * 128:(j + 1) * 128], ct_.t[:, j * 128:(j + 1) * 128]) for j in range(3)],
                                          self.ident.t[:], [ct_.b, self.ident.b], [pjb])
                            pr.copy("act", CKT.t[:, :, t * 128:(t + 1) * 128], pbf.rearrange("p (a b) -> p a b", a=3),
                                    [pjb], [CKT.b])
                        self.pipelined(g.nt, first, second, depth=2)
                    else:
                        for t in range(g.nt):
                            pt_, pb = proj(t)
                            th_ = thb[t % 2]
                            c0 = (ri - 2) * 512
                            pr.act(th_.t[:], pt_[:, 0:512], AF.Tanh, [pb], [th_.b], scale=0.5)
                            sg_ = sgst[t % 2]
                            pr.stt("dve", sg_.t[:], th_.t[:], 1.0, pt_[:, 0:512], ALU.add, ALU.mult, [th_.b, pb], [sg_.b])
                            T = g.tile0 + t
                            pr.dma("sp", [(d["sgsc"][T * 128:(T + 1) * 128, c0:c0 + 512], sg_.t[:])], sg_.b, [sg_.b],
                                   [self.sgsc_b[T]])
                self.stream(reqs, consume)
                if g.ctx:
                    cc = s1.tile([128, 4, 384], BF16, "ctxcat")
                    pr.dma("pool", [(cc.t[:, :, 0:256], d["cckv"][i].rearrange("(c p) n -> p c n", p=128)),
                                    (cc.t[:, :, 256:320], d["ckpe"][i].rearrange("(c p) n -> p c n", p=128)),
                                    (cc.t[:, :, 320:384], d["ckpe"][i].rearrange("(c p) n -> p c n", p=128))],
                           cc.b, (), [cc.b])
                    for c in range(4):
                        bj = self.psum_gen()
                        pj, pjb = self.PS[bj]
                        pbf = self.ps(bj, 384, BF16)
                        pr.transposes([(pbf[:, j * 128:(j + 1) * 128], cc.t[:, c, j * 128:(j + 1) * 128]) for j in range(3)],
                                      self.ident.t[:], [cc.b, self.ident.b], [pjb])
                        pr.copy(self.evac_eng(), CKT.t[:, :, 2048 + c * 128:2048 + (c + 1) * 128],
                                pbf.rearrange("p (a b) -> p a b", a=3), [pjb], [CKT.b])
                    pr.release_dma(cc.b)
                for x in ckf:
                    pr.release_dma(x.b)
            if g.name == "P":
                self.issue_wout(l)
            with pr.scope() as s2:
                sl0 = self.W[self.wslot_i % 3]; sl1 = self.W[(self.wslot_i + 1) % 3]
                self.wslot_i += 2
                Wuq = Tile(sl0.t[:, 0:12, :].rearrange("p a b -> p (a b)").rearrange("p (k n) -> p k n", k=4), sl0.b)
                Wukv = Tile(sl1.t[:, 0:8, :].rearrange("p a b -> p (a b)").rearrange("p (k n) -> p k n", k=2), sl1.b)
                pr.dma("pool", [(Wuq.t, d["mla_w_uq"][i].rearrange("(kc p) n -> p kc n", p=128))], Wuq.b, (), [Wuq.b])
                pr.dma("pool", [(Wukv.t, d["mla_w_ukv"][i].rearrange("(kc p) n -> p kc n", p=128))], Wukv.b, (), [Wukv.b])
                QrT = s2.tile([128, g.ntok], BF16, "QrT")
                QnT = s2.tile([128, g.ntok], BF16, "QnT")
                KnT = s2.tile([128, g.klen], BF16, "KnT")
                VM = s2.tile([128, g.nkt, 136], BF16, "VM")
                pr.memset("dve", VM.t[:, :, 128:129], 2.0, [VM.b])
                qrb = [s2.tile([128, 128], BF16, "qrb") for _ in range(2)]
                yst = [s2.tile([128, 128], BF16, "yst") for _ in range(3)]
                SGh = s2.tile([128, g.nt, 128], BF16, "SGh")
                oraw = s2.tile([128, 4, 132], F32, "oraw")
                orb = [s2.buf(f"orb{j}") for j in range(4)]
                Wuq4 = Wuq.t.rearrange("p k (h d) -> p k h d", d=192)
                for h in range(8):
                    hp = h % 2
                    pr.dma("sp", [(SGh.t[:], d["sgsc"][g.tile0 * 128:(g.tile0 + g.nt) * 128, h * 128:(h + 1) * 128]
                                   .rearrange("(t p) n -> p t n", p=128))], SGh.b,
                           self.sgsc_b[g.tile0:g.tile0 + g.nt], [SGh.b])
                    if hp == 0:
                        def first(t, h=h):
                            bi = self.psum_gen()
                            pt_, pb = self.PS[bi]
                            pr.mm(pt_[:, 0:128].rearrange("p (h d) -> p h d", d=64),
                                  [(cqT.t[:, kc, t * 128:(t + 1) * 128], Wuq4[:, kc, h:h + 2, 128:192]) for kc in range(4)],
                                  [cqT.b, Wuq.b], [pb])
                            qr_ = qrb[t % 2]
                            if g.rope:
                                self.rope(s2, pt_[:, 0:128], pb, qr_.t[:], qr_.b, 2, 16, self.ropeM, t)
                            else:
                                pr.copy("act", qr_.t[:], pt_[:, 0:128], [pb], [qr_.b])

                        def second(t):
                            qr_ = qrb[t % 2]
                            bj = self.psum_gen()
                            pj, pjb = self.PS[bj]
                            pbf = self.ps(bj, 128, BF16)
                            pr.transposes([(pbf, qr_.t[:])], self.ident.t[:], [qr_.b, self.ident.b], [pjb])
                            pr.copy("act", QrT.t[:, t * 128:(t + 1) * 128], pbf, [pjb], [QrT.b])
                        self.pipelined(g.nt, first, second)
                    for qb in range(g.ntok // 512):
                        bi = self.psum_gen()
                        pt_, pb = self.PS[bi]
                        pr.mm(pt_[:, 0:512], [(Wuq.t[:, kc, h * 192:h * 192 + 128], cqT.t[:, kc, qb * 512:(qb + 1) * 512])
                                               for kc in range(4)], [Wuq.b, cqT.b], [pb])
                        pr.copy(self.evac_eng(), QnT.t[:, qb * 512:(qb + 1) * 512], pt_[:, 0:512], [pb], [QnT.b])
                    for kb in range(g.klen // 512):
                        bi = self.psum_gen()
                        pt_, pb = self.PS[bi]
                        pr.mm(pt_[:, 0:512], [(Wukv.t[:, kc, h * 256:h * 256 + 128], CKT.t[:, kc, kb * 512:(kb + 1) * 512])
                                               for kc in range(2)], [Wukv.b, CKT.b], [pb])
                        pr.copy(self.evac_eng(), KnT.t[:, kb * 512:(kb + 1) * 512], pt_[:, 0:512], [pb], [KnT.b])
                    for k0 in range(0, g.nkt, 4):
                        bi = self.psum_gen()
                        pt_, pb = self.PS[bi]

                        def fn(e, k0=k0, pt_=pt_, h=h):
                            ins = None
                            for j in range(4):
                                kt = k0 + j
                                for kc in range(2):
                                    ins = e.matmul(pt_[:, j * 128:(j + 1) * 128], lhsT=CKT.t[:, kc, kt * 128:(kt + 1) * 128],
                                                   rhs=Wukv.t[:, kc, h * 256 + 128:h * 256 + 256], start=(kc == 0), stop=(kc == 1))
                            return ins
                        pr.n_ins += 7
                        pr.op("pe", fn, [CKT.b, Wukv.b], [pb])
                        pr.copy(self.evac_eng(), VM.t[:, k0:k0 + 4, 0:128], pt_[:, 0:512].rearrange("p (a b) -> p a b", a=4),
                                [pb], [VM.b])
                    groups = [[b] for b in range(4)] if g.ctx else [[0, 1], [2, 3]]
                    for grp in groups:
                        jobs = []
                        for bq in grp:
                            q0, qn, ktiles = g.qblocks[bq]

                            def st_parts(kt, q0=q0, qn=qn, hp=hp):
                                return ([(KnT.t[:, kt * 128:(kt + 1) * 128], QnT.t[:, q0:q0 + qn]),
                                         (CKT.t[hp * 64:(hp + 1) * 64, 2, kt * 128:(kt + 1) * 128],
                                          QrT.t[hp * 64:(hp + 1) * 64, q0:q0 + qn])], [KnT.b, QnT.b, CKT.b, QrT.b])
                            jobs.append(dict(qn=qn, ktiles=ktiles, st_parts=st_parts, v_ap=lambda kt: VM.t[:, kt, 0:129],
                                             v_reads=[VM.b], bias=None))
                        obs = self.attention_multi(jobs, 129, MLA_SCALE, [4, 6][:len(jobs)])
                        idxs = []
                        for ji, bq in enumerate(grp):
                            q0, qn, _ = g.qblocks[bq]
                            for sub in range(qn // 128):
                                ot, obuf = obs[ji][sub]
                                oi = ji * 2 + sub if len(grp) > 1 else sub
                                pr.copy("dve", oraw.t[:, oi, 0:129], ot[:, 0:129], [obuf], [orb[oi]])
                                idxs.append((oi, q0 // 128 + sub))
                        for (oi, tq) in idxs:
                            r = self.small(s2)
                            o_ = oraw.t[:, oi, :]
                            pr.op("dve", lambda e, r=r, o_=o_: e.reciprocal(out=r.t[:, 0:1], in_=o_[:, 128:129]), [orb[oi]], [r.b])
                            ys_ = yst[self.rot("yst", 3)]
                            pr.stt("dve", ys_.t[:], o_[:, 0:128], r.t[:, 0:1], SGh.t[:, tq, :],
                                   ALU.mult, ALU.mult, [orb[oi], r.b, SGh.b], [ys_.b])
                            T = g.tile0 + tq
                            pr.dma("sp", [(d["ysc"][T * 128:(T + 1) * 128, 1024 + h * 128:1024 + (h + 1) * 128], ys_.t[:])],
                                   ys_.b, [ys_.b], [self.ysc_b[T]])
            for x in (gqb, gkvb):
                pr.release_dma(x.b)

    def build(self):
        nc = self.nc
        cfg = self.cfg
        self.declare()
        d = self.d
        with self.stack as st:
            pr = self.pr = Prog(nc, st)
            self.rotc = {}
            self.wout_done = {}
            self.evi = 0
            self.wslot_i = 0
            self.gS, self.gP = Group("S"), Group("P")
            self.xst_b = [pr.gbuf(f"xst{T}") for T in range(24)]
            self.ysc_b = [pr.gbuf(f"ysc{T}") for T in range(24)]
            self.sgsc_b = [pr.gbuf(f"sgsc{T}") for T in range(24)]
            self.modv_b = [pr.gbuf(f"modv{l}") for l in range(4)]
            self.out_b = pr.gbuf("outs")
            self.PS = []
            for i in range(8):
                t = st.enter_context(nc.psum_tensor(f"ps{i}", [128, 512], F32))
                pb_ = pr.gbuf(f"ps{i}")
                pb_.excl = True
                self.PS.append((t, pb_))
            with pr.scope() as g0:
                self.ident = g0.tile([128, 128], BF16, "ident")
                pr.dma("pool", [(self.ident.t[:], d["ident"][:, :])], self.ident.b, (), [self.ident.b])
                pr.release_dma(self.ident.b)
                self.cst = g0.tile([128, 4], F32, "cst")
                pr.memset("pool", self.cst.t[:, 0:1], -0.5, [self.cst.b])
                pr.memset("pool", self.cst.t[:, 1:2], float(D * EPS), [self.cst.b])
                pr.memset("pool", self.cst.t[:, 2:3], float(256 * EPS), [self.cst.b])
                pr.memset("pool", self.cst.t[:, 3:4], float(512 * EPS), [self.cst.b])
                self.W = [g0.tile([128, KC, 512], BF16, "W") for _ in range(3)]
                self.PT = [g0.tile([128, 512], BF16, "PT") for _ in range(4)]
                self.layers = cfg.get("layers", list(range(cfg.get("nlayers", DEPTH))))
                self.mod_prepare(g0)
                with pr.scope() as scm:
                    st0 = self.mod_stream(self.layers[0], scm)
                    for _ in range(12):
                        st0()
                self.ropeD = g0.tile([128, 16, 2, 2, 2, 32], F32, "ropeD")
                self.ropeM = g0.tile([128, 16, 2, 2, 2, 16], F32, "ropeM")
                pr.dma("sp", [(self.ropeD.t[:].rearrange("p a b c d e -> p (a b c d e)"), d["ropeD"][:, :])], self.ropeD.b, (), [self.ropeD.b])
                pr.dma("sp", [(self.ropeM.t[:].rearrange("p a b c d e -> p (a b c d e)"), d["ropeM"][:, :])], self.ropeM.b, (), [self.ropeM.b])
                pr.release_dma(self.ropeD.b); pr.release_dma(self.ropeM.b)
                self.hT = g0.tile([128, KC, 2048], BF16, "hT")
                self.hT_b = [g0.buf(f"hT{t}") for t in range(16)]
                self.ra = [g0.tile([128, 512], F32, "ra") for _ in range(1)]
                self.rb = [g0.tile([128, 512], F32, "rb") for _ in range(1)]
                self.layers = cfg.get("layers", list(range(cfg.get("nlayers", DEPTH))))
                for l in self.layers:
                    for g in ((self.gP,) if cfg.get("only_P") else (self.gS, self.gP) if not cfg.get("only_S") else (self.gS,)):
                        self.phase_A(l, g)
                        if l % 2 == 0:
                            self.phase_B_even(l, g)
                        else:
                            self.phase_B_odd(l, g)
                    self.phase_C(l)
                pr.wait_all_dma("sp")
        return nc


_CONST_CACHE = {}


def _consts():
    if not _CONST_CACHE:
        _CONST_CACHE["ident"] = np.eye(128, dtype=np.float32)
        _CONST_CACHE["ropeD"] = np.ascontiguousarray(rope_tables(32).reshape(128, -1))
        _CONST_CACHE["ropeM"] = np.ascontiguousarray(rope_tables(16).reshape(128, -1))
    return _CONST_CACHE


def make_in_maps(inputs, cores=range(NCORES)):
    f = lambda a: np.ascontiguousarray(np.asarray(a, dtype=np.float32))
    cst = _consts()
    natab = na_bias_tables(np.asarray(inputs["na_rpb"], np.float32))
    shared = {
        "w_ada": f(inputs["w_ada"]), "b_ada": f(inputs["b_ada"]), "g_pre": f(inputs["g_pre"]), "g_post": f(inputs["g_post"]),
        "w_in_even": f(inputs["w_in_even"]), "w_out_even": f(inputs["w_out_even"]),
        "mla_g_q": f(inputs["mla_g_q"]), "mla_w_uq": f(inputs["mla_w_uq"]), "mla_g_kv": f(inputs["mla_g_kv"]),
        "mla_w_ukv": f(inputs["mla_w_ukv"]), "w_in_odd": f(inputs["w_in_odd"]), "w_out_odd": f(inputs["w_out_odd"]),
        "diff_lambda": f(np.asarray(inputs["diff_lambda"]).reshape(2, 512)), "diff_g": f(inputs["diff_g"]),
        "ident": cst["ident"], "ropeD": cst["ropeD"], "ropeM": cst["ropeM"], "natab": natab,
    }
    xs = np.asarray(inputs["x_sample"], np.float32)
    xp = np.asarray(inputs["x_prompt"], np.float32)
    c = np.asarray(inputs["c"], np.float32)
    cctx = np.asarray(inputs["c_ctx"], np.float32)
    maps = []
    for b in cores:
        m = dict(shared)
        m["xs"] = f(xs[b])
        m["xp"] = f(xp[4 * b:4 * b + 4].reshape(1024, D))
        m["cnak"] = f(np.asarray(inputs["cache_na_k"])[b].reshape(2, 512, 1024))
        m["cnav"] = f(np.asarray(inputs["cache_na_v"])[b].reshape(2, 512, 1024))
        m["cckv"] = f(np.asarray(inputs["cache_mla_ckv"])[b])
        m["ckpe"] = f(np.asarray(inputs["cache_mla_kpe"])[b])
        m["cdk"] = f(np.asarray(inputs["cache_diff_k"])[b].reshape(2, 512, 2048))
        m["cdv"] = f(np.asarray(inputs["cache_diff_v"])[b].reshape(2, 512, 2048))
        cond = np.stack([c[b], cctx], axis=0)
        m["condT"] = f(cond.reshape(2, KC, 128).transpose(2, 1, 0).reshape(128, 32))
        maps.append(m)
    return maps


_NC_CACHE = {}


def get_program(cfg=None):
    key = repr(sorted((cfg or {}).items()))
    if key not in _NC_CACHE:
        _NC_CACHE[key] = Builder(cfg or {}).build()
    return _NC_CACHE[key]


def kernel(**inputs):
    nc = get_program()
    in_maps = make_in_maps(inputs)
    res = run_bass_kernel_spmd(nc, in_maps, core_ids=list(range(NCORES)))
    R = res.results
    cat = lambda k: np.stack([np.asarray(r[k]) for r in R], axis=0)
    y_p = cat("y_p").reshape(32, 256, D)
    y_s = cat("y_s").reshape(8, 2048, D)
    nak = cat("o_nak").reshape(32, 2, 256, 8, 128)
    nav = cat("o_nav").reshape(32, 2, 256, 8, 128)
    ckv = cat("o_ckv").reshape(32, 2, 256, 256)
    kpe = cat("o_kpe").reshape(32, 2, 256, 64)
    dk = cat("o_dk").reshape(32, 2, 256, 8, 256)
    dv = cat("o_dv").reshape(32, 2, 256, 8, 256)
    return tuple(np.ascontiguousarray(a, dtype=np.float32) for a in (y_p, y_s, nak, nav, ckv, kpe, dk, dv))
```
